# Optimizing a Trainium2 kernel written in Bass

```python
import math
import jax, jax.numpy as jnp
from jax import lax
import numpy as np

D_MODEL = 4096
BATCH = 1
SEQ = 16384
DEPTH = 2

GDN_DIM = 128
GDN_WIDTH = 3 * D_MODEL // 8
GDN_HEADS = GDN_WIDTH // GDN_DIM
GDN_CONV = 4
GDN_CHUNK = 64
SWA_DIM = 128
SWA_WIDTH = D_MODEL // 4
SWA_HEADS = SWA_WIDTH // SWA_DIM
SWA_PATTERNS = ((128, 1), (512, 4), (2048, 16))
NUM_BUCKETS = 32
MAX_DISTANCE = 2048
RWKV_DIM = 64
RWKV_WIDTH = D_MODEL - GDN_WIDTH - SWA_WIDTH
RWKV_HEADS = RWKV_WIDTH // RWKV_DIM
DECAY_LORA = 128
ICL_LORA = 128
GATE_LORA = 480
GDN_IN = 4 * GDN_WIDTH + 2 * GDN_HEADS
SWA_IN = 3 * SWA_WIDTH
RWKV_IN = 3 * RWKV_WIDTH + DECAY_LORA + ICL_LORA + GATE_LORA
IN_WIDTH = GDN_IN + SWA_IN + RWKV_IN
D_FF = 11008
FFN_CONV = 3
RMS_EPS = 1e-6
GN_EPS = 64e-5
NEG_INF = -1e30

kernel_name = "hybrid_gdn_dilated_rwkv7_convffn"


def rmsnorm(t, w):
    tf = t.astype(jnp.float32)
    y = tf * lax.rsqrt(jnp.mean(tf * tf, axis=-1, keepdims=True) + RMS_EPS) * w.astype(jnp.float32)
    return y.astype(t.dtype)


def l2norm(t, eps=1e-6):
    return t * lax.rsqrt(jnp.sum(t * t, axis=-1, keepdims=True) + eps)


def causal_dwconv(t, w):
    width = w.shape[0]
    seq = t.shape[1]
    tp = jnp.pad(t, ((0, 0), (width - 1, 0), (0, 0)))
    out = tp[:, 0:seq] * w[0]
    for i in range(1, width):
        out = out + tp[:, i:i + seq] * w[i]
    return out


def token_shift(t):
    return jnp.pad(t, ((0, 0), (1, 0), (0, 0)))[:, :-1]


def chunk_gated_delta_rule(q, k, v, g, beta):
    B, S, H, D = q.shape
    C = GDN_CHUNK
    N = S // C

    def chunked(t):
        return jnp.moveaxis(t.reshape(B, N, C, H, *t.shape[3:]), 3, 2)

    q, k, v = chunked(q), chunked(k), chunked(v)
    g, beta = chunked(g), chunked(beta)
    G = jnp.cumsum(g, axis=-1)
    causal = jnp.tril(jnp.ones((C, C), bool))
    strict = jnp.tril(jnp.ones((C, C), bool), -1)
    gamma = jnp.exp(jnp.where(causal, G[..., :, None] - G[..., None, :], -jnp.inf))
    kb = k * beta[..., None]
    a_mat = jnp.where(strict, jnp.einsum('bnhid,bnhjd->bnhij', kb, k) * gamma, 0.0)
    eye = jnp.eye(C, dtype=jnp.float32)
    rhs = jnp.concatenate([v * beta[..., None], kb * jnp.exp(G)[..., None]], axis=-1)
    sol = lax.linalg.triangular_solve(eye + a_mat, rhs, left_side=True, lower=True, unit_diagonal=True)
    u, w = jnp.split(sol, 2, axis=-1)
    attn = jnp.einsum('bnhid,bnhjd->bnhij', q, k) * gamma
    qg = q * jnp.exp(G)[..., None]
    kd = k * jnp.exp(G[..., -1:] - G)[..., None]
    g_tot = jnp.exp(G[..., -1])

    def step(state, xs):
        u_c, w_c, qg_c, kd_c, attn_c, gt_c = xs
        v_new = u_c - jnp.einsum('bhck,bhkv->bhcv', w_c, state)
        o = jnp.einsum('bhck,bhkv->bhcv', qg_c, state) + jnp.einsum('bhij,bhjv->bhiv', attn_c, v_new)
        state = state * gt_c[..., None, None] + jnp.einsum('bhck,bhcv->bhkv', kd_c, v_new)
        return state, o

    xs = tuple(jnp.moveaxis(t, 1, 0) for t in (u, w, qg, kd, attn, g_tot))
    state0 = jnp.zeros((B, H, D, D), jnp.float32)
    _, o = lax.scan(step, state0, xs)
    return jnp.moveaxis(o, 0, 1).swapaxes(2, 3).reshape(B, S, H, D)


def gated_deltanet(p, conv_w, a_log, dt_bias, norm_w):
    B, S, _ = p.shape
    H, Dh, W = GDN_HEADS, GDN_DIM, GDN_WIDTH
    qkv, z, b, a = jnp.split(p, [3 * W, 4 * W, 4 * W + H], axis=-1)
    qkv = jax.nn.silu(causal_dwconv(qkv, conv_w)).astype(jnp.float32)
    q, k, v = [t.reshape(B, S, H, Dh) for t in jnp.split(qkv, 3, axis=-1)]
    q = l2norm(q) * (Dh ** -0.5)
    k = l2norm(k)
    beta = jax.nn.sigmoid(b.astype(jnp.float32))
    g = -jnp.exp(a_log.astype(jnp.float32)) * jax.nn.softplus(a.astype(jnp.float32) + dt_bias.astype(jnp.float32))
    o = chunk_gated_delta_rule(q, k, v, g, beta)
    o = o * lax.rsqrt(jnp.mean(o * o, axis=-1, keepdims=True) + RMS_EPS) * norm_w.astype(jnp.float32)
    o = o * jax.nn.silu(z.astype(jnp.float32).reshape(B, S, H, Dh))
    return o.reshape(B, S, W).astype(p.dtype)


def t5_bucket(dist):
    exact = NUM_BUCKETS // 2
    d = jnp.maximum(dist, 1).astype(jnp.float32)
    log_b = exact + (jnp.log(d / exact) / math.log(MAX_DISTANCE / exact) * (NUM_BUCKETS - exact)).astype(jnp.int32)
    return jnp.where(dist < exact, dist, jnp.minimum(log_b, NUM_BUCKETS - 1))


def dilated_window_branch(q, k, v, rel_bias, window, dilation):
    B, S, H, D = q.shape
    blk = window // dilation
    span = blk * dilation
    s_pad = -(-S // span) * span
    n_sub = s_pad // dilation
    nb = n_sub // blk

    def to_blocks(t):
        t = jnp.pad(t, ((0, 0), (0, s_pad - S), (0, 0), (0, 0)))
        t = t.reshape(B, n_sub, dilation, H, D).transpose(0, 2, 1, 3, 4)
        return t.reshape(B, dilation, nb, blk, H, D)

    def with_prev(t):
        prev = jnp.pad(t, ((0, 0), (0, 0), (1, 0), (0, 0), (0, 0), (0, 0)))[:, :, :-1]
        return jnp.concatenate([prev, t], axis=3)

    qb = to_blocks(q)
    kw = with_prev(to_blocks(k))
    vw = with_prev(to_blocks(v))
    s = jnp.einsum('brnqhd,brnkhd->brnhqk', qb, kw)
    i = jnp.arange(blk)[:, None]
    j = jnp.arange(2 * blk)[None, :]
    steps = i + blk - j
    in_band = (steps >= 0) & (steps <= blk)
    first = (jnp.arange(nb) == 0)[:, None, None]
    valid = in_band[None] & ~(first & (j < blk)[None])
    bias = rel_bias[t5_bucket(jnp.maximum(steps, 0) * dilation)].transpose(2, 0, 1)
    s = jnp.where(valid[None, None, :, None], s + bias, NEG_INF)
    m = jnp.max(s, axis=-1)
    p = jnp.exp(s - m[..., None])
    l = jnp.sum(p, axis=-1)
    o = jnp.einsum('brnhqk,brnkhd->brnqhd', p, vw)

    def from_blocks(t):
        t = t.reshape(B, dilation, n_sub, *t.shape[4:]).swapaxes(1, 2)
        return t.reshape(B, s_pad, *t.shape[3:])[:, :S]

    return from_blocks(o), from_blocks(m.swapaxes(3, 4)), from_blocks(l.swapaxes(3, 4))


def dilated_attention_mixture(p, rel_bias):
    B, S, _ = p.shape
    q, k, v = [t.reshape(B, S, SWA_HEADS, SWA_DIM).astype(jnp.float32) for t in jnp.split(p, 3, axis=-1)]
    q = q * (SWA_DIM ** -0.5)
    rb = rel_bias.astype(jnp.float32)
    branches = [dilated_window_branch(q, k, v, rb, w, d) for (w, d) in SWA_PATTERNS]
    m_max = jnp.max(jnp.stack([br[1] for br in branches]), axis=0)
    num = jnp.zeros_like(q)
    den = jnp.zeros_like(m_max)
    for o, m, l in branches:
        sc = jnp.exp(m - m_max)
        num = num + o * sc[..., None]
        den = den + l * sc
    return (num / den[..., None]).reshape(B, S, SWA_WIDTH).astype(p.dtype)


def rwkv7_recurrence(r, w, k, v, a, b):
    B, S, H, N = r.shape

    def step(state, xs):
        r_t, w_t, k_t, v_t, a_t, b_t = xs
        sa = jnp.einsum('bhvk,bhk->bhv', state, a_t)
        state = state * w_t[:, :, None, :] + sa[..., None] * b_t[:, :, None, :] + v_t[..., None] * k_t[:, :, None, :]
        return state, jnp.einsum('bhvk,bhk->bhv', state, r_t)

    xs = tuple(jnp.moveaxis(t, 1, 0) for t in (r, w, k, v, a, b))
    _, y = lax.scan(step, jnp.zeros((B, H, N, N), jnp.float32), xs)
    return jnp.moveaxis(y, 0, 1)


def rwkv7_time_mix(p, mu, w0, w_up, a0, a_up, g_up, k_k, k_a, r_k, ln_w, ln_b):
    B, S, _ = p.shape
    H, N, W = RWKV_HEADS, RWKV_DIM, RWKV_WIDTH
    p = (p + mu * (token_shift(p) - p)).astype(jnp.float32)
    r, k, v, wd, ad, gd = jnp.split(p, [W, 2 * W, 3 * W, 3 * W + DECAY_LORA, 3 * W + DECAY_LORA + ICL_LORA], axis=-1)
    f = lambda t: t.astype(jnp.float32)
    heads = lambda t: t.reshape(B, S, H, N)
    w_log = -jax.nn.softplus(-(f(w0) + jnp.tanh(wd) @ f(w_up))) - 0.5
    decay = jnp.exp(-jnp.exp(w_log))
    a = jax.nn.sigmoid(f(a0) + ad @ f(a_up))
    g = jax.nn.sigmoid(gd) @ f(g_up)
    kk = l2norm(heads(k * f(k_k)), eps=1e-12)
    k = k * (1.0 + (a - 1.0) * f(k_a))
    y = rwkv7_recurrence(heads(r), heads(decay), heads(k), heads(v), -kk, kk * heads(a))
    mean = jnp.mean(y, axis=-1, keepdims=True)
    var = jnp.mean(jnp.square(y - mean), axis=-1, keepdims=True)
    y = ((y - mean) * lax.rsqrt(var + GN_EPS)).reshape(B, S, W) * f(ln_w) + f(ln_b)
    bonus = jnp.sum(heads(r) * heads(k) * f(r_k), axis=-1, keepdims=True) * heads(v)
    return ((y + bonus.reshape(B, S, W)) * g).astype(mu.dtype)


def conv_gated_ffn(h, w_gate, w_up, conv_w, conv_b, w_down):
    gate = causal_dwconv(h @ w_gate, conv_w) + conv_b
    return (jax.nn.silu(gate) * (h @ w_up)) @ w_down


def setup_inputs(seed: int = 0) -> dict:
    key = jax.random.key(seed)
    ks = jax.random.split(key, 32)
    L = DEPTH
    nrm = lambda kk, shape, scale: scale * jax.random.normal(kk, shape, jnp.float32)
    uni = lambda kk, shape, lo, hi: jax.random.uniform(kk, shape, jnp.float32, lo, hi)
    dt = jnp.exp(uni(ks[5], (L, GDN_HEADS), math.log(1e-3), math.log(1e-1)))
    return {
        "x": nrm(ks[0], (BATCH, SEQ, D_MODEL), 1.0),
        "attn_norm": 1.0 + nrm(ks[1], (L, D_MODEL), 0.02),
        "w_in": nrm(ks[2], (L, D_MODEL, IN_WIDTH), D_MODEL ** -0.5),
        "gdn_conv": nrm(ks[3], (L, GDN_CONV, 3 * GDN_WIDTH), GDN_CONV ** -0.5),
        "gdn_a_log": jnp.log(uni(ks[4], (L, GDN_HEADS), 1.0, 16.0)),
        "gdn_dt_bias": dt + jnp.log(-jnp.expm1(-dt)),
        "gdn_norm": 1.0 + nrm(ks[6], (L, GDN_DIM), 0.02),
        "rwkv_mu": uni(ks[7], (L, RWKV_IN), 0.0, 1.0),
        "rwkv_w0": uni(ks[8], (L, RWKV_WIDTH), -6.0, 1.0),
        "rwkv_w_up": nrm(ks[9], (L, DECAY_LORA, RWKV_WIDTH), 0.5 * DECAY_LORA ** -0.5),
        "rwkv_a0": nrm(ks[10], (L, RWKV_WIDTH), 0.1),
        "rwkv_a_up": nrm(ks[11], (L, ICL_LORA, RWKV_WIDTH), ICL_LORA ** -0.5),
        "rwkv_g_up": nrm(ks[12], (L, GATE_LORA, RWKV_WIDTH), GATE_LORA ** -0.5),
        "rwkv_k_k": 0.85 + nrm(ks[13], (L, RWKV_WIDTH), 0.02),
        "rwkv_k_a": 1.0 + nrm(ks[14], (L, RWKV_WIDTH), 0.02),
        "rwkv_r_k": nrm(ks[15], (L, RWKV_HEADS, RWKV_DIM), 0.1),
        "rwkv_ln_w": 1.0 + nrm(ks[16], (L, RWKV_WIDTH), 0.02),
        "rwkv_ln_b": nrm(ks[17], (L, RWKV_WIDTH), 0.01),
        "w_out": nrm(ks[18], (L, D_MODEL, D_MODEL), D_MODEL ** -0.5),
        "ffn_norm": 1.0 + nrm(ks[19], (L, D_MODEL), 0.02),
        "w_ffn_gate": nrm(ks[20], (L, D_MODEL, D_FF), D_MODEL ** -0.5),
        "w_ffn_up": nrm(ks[21], (L, D_MODEL, D_FF), D_MODEL ** -0.5),
        "ffn_conv": nrm(ks[22], (L, FFN_CONV, D_FF), FFN_CONV ** -0.5),
        "ffn_conv_b": nrm(ks[23], (L, D_FF), 0.01),
        "w_ffn_down": nrm(ks[24], (L, D_FF, D_MODEL), D_FF ** -0.5),
        "rel_bias": nrm(ks[25], (NUM_BUCKETS, SWA_HEADS), 0.5),
        "final_norm": 1.0 + nrm(ks[26], (D_MODEL,), 0.02),
    }


def reference(x, attn_norm, w_in, gdn_conv, gdn_a_log, gdn_dt_bias, gdn_norm, rwkv_mu, rwkv_w0, rwkv_w_up,
              rwkv_a0, rwkv_a_up, rwkv_g_up, rwkv_k_k, rwkv_k_a, rwkv_r_k, rwkv_ln_w, rwkv_ln_b, w_out,
              ffn_norm, w_ffn_gate, w_ffn_up, ffn_conv, ffn_conv_b, w_ffn_down, rel_bias, final_norm):
    for l in range(DEPTH):
        h = rmsnorm(x, attn_norm[l])
        proj = h @ w_in[l]
        p_a, p_b, p_c = jnp.split(proj, [GDN_IN, GDN_IN + SWA_IN], axis=-1)
        o_a = gated_deltanet(p_a, gdn_conv[l], gdn_a_log[l], gdn_dt_bias[l], gdn_norm[l])
        o_b = dilated_attention_mixture(p_b, rel_bias)
        o_c = rwkv7_time_mix(p_c, rwkv_mu[l], rwkv_w0[l], rwkv_w_up[l], rwkv_a0[l], rwkv_a_up[l], rwkv_g_up[l],
                             rwkv_k_k[l], rwkv_k_a[l], rwkv_r_k[l], rwkv_ln_w[l], rwkv_ln_b[l])
        mix = jnp.concatenate([o_a, o_b.astype(o_a.dtype), o_c.astype(o_a.dtype)], axis=-1)
        x = x + (mix @ w_out[l]).astype(x.dtype)
        h = rmsnorm(x, ffn_norm[l])
        x = x + conv_gated_ffn(h, w_ffn_gate[l], w_ffn_up[l], ffn_conv[l], ffn_conv_b[l], w_ffn_down[l]).astype(x.dtype)
    return rmsnorm(x, final_norm)
```

```python
import math
from contextlib import ExitStack

import numpy as np
import ml_dtypes

import concourse.bass as bass
import concourse.mybir as mybir
from concourse.bass_utils import run_bass_kernel_spmd

F32 = mybir.dt.float32
BF16 = mybir.dt.bfloat16
AF = mybir.ActivationFunctionType
ALU = mybir.AluOpType
AX = mybir.AxisListType
NPBF = ml_dtypes.bfloat16


class Cfg:
    def __init__(self, D=4096, S=16384, NR=8, DFF=11008):
        self.D, self.S, self.NR, self.DFF = D, S, NR, DFF
        self.GD, self.GW = 128, 3 * D // 8
        self.GH = self.GW // 128
        self.AD, self.AW = 128, D // 4
        self.AH = self.AW // 128
        self.RD = 64
        self.RW = D - self.GW - self.AW
        self.RH = self.RW // 64
        self.LW, self.LA, self.LG = 128, 128, 480
        self.GDN_IN = 4 * self.GW + 2 * self.GH
        self.SWA_IN = 3 * self.AW
        self.RWKV_IN = 3 * self.RW + self.LW + self.LA + self.LG
        self.IN_W = self.GDN_IN + self.SWA_IN + self.RWKV_IN
        self.GS = -(-self.GH // NR)
        assert self.AH == NR and self.RH == 3 * NR and self.GS == 2
        self.DC = D // NR
        self.FC = DFF // NR
        self.RC = 3 * 64
        self.MIXC = self.GS * 128 + 128 + self.RC
        self.o_g = 0
        self.o_a = 512 * self.GS
        self.o_r = self.o_a + 384
        self.o_l = self.o_r + 3 * self.RC
        self.o_s = self.o_l + self.LW + self.LA + self.LG
        self.PC = self.o_s + 2 * self.GS


class Sem:
    def __init__(self, h, name):
        self.h, self.name, self.cnt = h, name, 0


class Buf:
    def __init__(self, name, t, space):
        self.name, self.t, self.space = name, t, space
        self.last_w = None
        self.readers = {}
        self.dsem = None
        self.last_w_dma = False

    def __getitem__(self, idx):
        return self.t[idx]

    def ap(self):
        return self.t.ap() if hasattr(self.t, "ap") else self.t[:]


class Prog:
    def __init__(self, nc):
        self.nc = nc
        self.root = ExitStack()
        self.stacks = [self.root]
        self.eng = {"pe": nc.tensor, "act": nc.scalar, "dve": nc.vector, "pool": nc.gpsimd, "sp": nc.sync}
        self.esem = {}
        for e in ("pe", "act", "dve", "pool"):
            self.esem[e] = Sem(self.root.enter_context(nc.semaphore("es_" + e)), e)
        self.seen = {e: {} for e in self.eng}
        self.free_dsems = []
        self.all_dsems = []
        self.scope_dsems = [[]]
        self.uid = 0
        self.pe_pending = False

    def _nm(self, name):
        self.uid += 1
        return f"{name}_{self.uid}"

    def sb(self, name, shape, dtype=F32):
        t = self.stacks[-1].enter_context(self.nc.sbuf_tensor(self._nm(name), list(shape), dtype))
        return Buf(name, t, "sb")

    def ps(self, name, shape, dtype=F32):
        t = self.stacks[-1].enter_context(self.nc.psum_tensor(self._nm(name), list(shape), dtype))
        return Buf(name, t, "ps")

    def dram(self, name, shape, dtype=F32, kind="Internal"):
        t = self.nc.dram_tensor(name, list(shape), dtype, kind=kind)
        return Buf(name, t, "dram")

    def _get_dsem(self, buf):
        if buf.dsem is None:
            if self.free_dsems:
                s = self.free_dsems.pop()
            else:
                s = Sem(self.root.enter_context(self.nc.semaphore(self._nm("ds"))), "ds")
                self.all_dsems.append(s)
            buf.dsem = s
            if buf.space != "dram":
                self.scope_dsems[-1].append(s)
        return buf.dsem

    def push(self):
        st = ExitStack()
        self.stacks.append(st)
        self.scope_dsems.append([])

    def pop(self):
        self.barrier()
        self.stacks.pop().close()
        self.free_dsems.extend(self.scope_dsems.pop())

    def barrier(self):
        toks = [(s, s.cnt) for s in self.esem.values() if s.cnt > 0]
        toks += [(s, 16 * s.cnt) for s in self.all_dsems if s.cnt > 0]
        for e in self.eng:
            self._wait(e, toks)

    def _wait(self, e, toks):
        seen = self.seen[e]
        own = self.esem.get(e)
        for s, v in toks:
            if e == "pe" and s is own:
                continue
            if seen.get(s, 0) < v:
                self.eng[e].wait_ge(s.h, v)
                seen[s] = v

    def _deps(self, reads, writes, dma_dst=None):
        toks = []
        for b in reads:
            if b.last_w is not None:
                toks.append(b.last_w)
        for b in writes:
            if b.last_w is not None and not (b is dma_dst and b.last_w_dma):
                toks.append(b.last_w)
            toks.extend(b.readers.items())
        return toks

    def _commit(self, tok, reads, writes, is_dma=False):
        for b in writes:
            b.last_w = tok
            b.last_w_dma = is_dma
            b.readers = {}
        for b in reads:
            s, v = tok
            if b.readers.get(s, 0) < v:
                b.readers[s] = v

    def op(self, e, fn, reads=(), writes=(), inc=True):
        self._wait(e, self._deps(reads, writes))
        ins = fn(self.eng[e])
        s = self.esem[e]
        if inc:
            s.cnt += 1
            ins.then_inc(s.h, 1)
            tok = (s, s.cnt)
        else:
            tok = (s, s.cnt + 1)
        self._commit(tok, reads, writes)
        return ins

    def dma(self, q, dst, dst_ap, src, src_ap):
        self._wait(q, self._deps([src], [dst], dma_dst=dst))
        s = self._get_dsem(dst)
        ins = self.eng[q].dma_start(out=dst_ap, in_=src_ap)
        s.cnt += 1
        ins.then_inc(s.h, 16)
        self._commit((s, 16 * s.cnt), [src], [dst], is_dma=True)
        return ins

    def finish(self, out_bufs):
        toks = [b.last_w for b in out_bufs if b.last_w is not None]
        self._wait("sp", toks)
        self.barrier()
        while len(self.stacks) > 1:
            self.stacks.pop().close()
        self.root.close()

    def mm(self, ps, ps_ap, a, a_ap, b, b_ap, start=True, stop=True, inc=None):
        if inc is None:
            inc = stop
        return self.op("pe", lambda e: e.matmul(ps_ap, a_ap, b_ap, start=start, stop=stop),
                       reads=[a, b], writes=[ps], inc=inc)

    def tr(self, ps, ps_ap, a, a_ap, ident, ident_ap):
        return self.op("pe", lambda e: e.transpose(ps_ap, a_ap, ident_ap), reads=[a, ident], writes=[ps])

    def act(self, out, out_ap, in_, in_ap, func, bias=None, scale=1.0, extra_reads=(), e="act", accum=None):
        kw = {}
        if bias is not None:
            kw["bias"] = bias
        if accum is not None:
            kw["accum_out"] = accum[1]
        w = [out] + ([accum[0]] if accum is not None else [])
        return self.op(e, lambda en: en.activation(out=out_ap, in_=in_ap, func=func, scale=scale, **kw),
                       reads=[in_] + list(extra_reads), writes=w)

    def ts(self, e, out, out_ap, in_, in_ap, s1, s2, op0, op1=None, extra_reads=()):
        if op1 is None:
            f = lambda en: en.tensor_scalar(out=out_ap, in0=in_ap, scalar1=s1, scalar2=None, op0=op0)
        else:
            f = lambda en: en.tensor_scalar(out=out_ap, in0=in_ap, scalar1=s1, scalar2=s2, op0=op0, op1=op1)
        return self.op(e, f, reads=[in_] + list(extra_reads), writes=[out])

    def tt(self, e, out, out_ap, a, a_ap, b, b_ap, op):
        return self.op(e, lambda en: en.tensor_tensor(out=out_ap, in0=a_ap, in1=b_ap, op=op),
                       reads=[a, b], writes=[out])

    def stt(self, e, out, out_ap, a, a_ap, scalar, b, b_ap, op0, op1, extra_reads=()):
        return self.op(e, lambda en: en.scalar_tensor_tensor(out=out_ap, in0=a_ap, scalar=scalar, in1=b_ap,
                                                              op0=op0, op1=op1),
                       reads=[a, b] + list(extra_reads), writes=[out])

    def copy(self, e, out, out_ap, in_, in_ap):
        if e == "act":
            return self.op(e, lambda en: en.copy(out=out_ap, in_=in_ap), reads=[in_], writes=[out])
        return self.op(e, lambda en: en.tensor_copy(out=out_ap, in_=in_ap), reads=[in_], writes=[out])

    def memset(self, e, out, out_ap, val):
        return self.op(e, lambda en: en.memset(out_ap, val), reads=[], writes=[out])


def ktiles(K):
    return [(k0, min(128, K - k0)) for k0 in range(0, K, 128)]


def linear(P, xT, K, S, w, Mc, m_tiles, TB, GC, epilogue, pre_group=None, n_ps=2, ps_bufs=None):
    kts = ktiles(K)
    KT = len(kts)
    groups, cur, cw = [], [], 0
    for mi, (m0, msz) in enumerate(m_tiles):
        if cw + msz > GC and cur:
            groups.append(cur)
            cur, cw = [], 0
        cur.append((mi, m0, msz))
        cw += msz
    if cur:
        groups.append(cur)
    P.push()
    wbf = P.sb("wbf", [128, KT, GC], BF16)
    KC = 4
    wst = [P.sb("wst", [128, KC, GC], F32) for _ in range(2)]
    xb = [P.sb("xblk", [128, KT, TB], BF16) for _ in range(2)]
    if ps_bufs is None:
        ps_bufs = [P.ps("linps", [128, 512], F32) for _ in range(n_ps)]
    xap = xT.ap()
    wap = w.ap()
    nblk = S // TB
    it = 0
    pi = 0
    for grp in groups:
        g0 = grp[0][1]
        gw = sum(g[2] for g in grp)
        ci = 0
        for kc0 in range(0, KT, KC):
            kc1 = min(KT, kc0 + KC)
            st = wst[ci % 2]
            for kt in range(kc0, kc1):
                k0, ksz = kts[kt]
                P.dma("sp", st, st[0:ksz, kt - kc0, 0:gw], w, wap[k0:k0 + ksz, g0:g0 + gw])
            full = [kt for kt in range(kc0, kc1) if kts[kt][1] == 128]
            part = [kt for kt in range(kc0, kc1) if kts[kt][1] != 128]
            ce = "dve" if ci % 2 == 0 else "pool"
            if full:
                a, b = full[0], full[-1] + 1
                P.copy(ce, wbf, wbf[:, a:b, 0:gw], st, st[:, a - kc0:b - kc0, 0:gw])
            for kt in part:
                ksz = kts[kt][1]
                P.copy(ce, wbf, wbf[0:ksz, kt, 0:gw], st, st[0:ksz, kt - kc0, 0:gw])
            ci += 1
        if pre_group is not None:
            pre_group(grp)
        for bi in range(nblk):
            t0 = bi * TB
            x = xb[it % 2]
            it += 1
            for kt, (k0, ksz) in enumerate(kts):
                pass
            nfull = sum(1 for k in kts if k[1] == 128)
            for ka in range(0, nfull, 8):
                kb = min(nfull, ka + 8)
                P.dma("sp", x, x[:, ka:kb, :], xT,
                      xap[ka * 128:kb * 128, t0:t0 + TB].rearrange("(kt p) s -> p kt s", p=128))
            if nfull < KT:
                k0, ksz = kts[-1]
                P.dma("sp", x, x[0:ksz, KT - 1, :], xT, xap[k0:k0 + ksz, t0:t0 + TB])
            for (mi, m0, msz) in grp:
                pb = ps_bufs[pi % len(ps_bufs)]
                pi += 1
                for kt, (k0, ksz) in enumerate(kts):
                    P.mm(pb, pb[0:msz, 0:TB], wbf, wbf[0:ksz, kt, m0 - g0:m0 - g0 + msz], x, x[0:ksz, kt, :],
                         start=(kt == 0), stop=(kt == KT - 1))
                epilogue(mi, (m0, msz), t0, pb)
    P.pop()


def load_rstd(P, part, NR, S, D, eps):
    rstd = P.sb("rstd", [128, S], F32)
    P.push()
    ones = P.sb("ones", [NR, 128], F32)
    P.memset("dve", ones, ones[:, :], 1.0)
    pt = P.sb("part", [NR, S], F32)
    P.dma("sp", pt, pt[:, :], part, part.ap())
    ps = [P.ps("rsps", [128, 512], F32) for _ in range(2)]
    for i, t0 in enumerate(range(0, S, 512)):
        pb = ps[i % 2]
        P.mm(pb, pb[:, :], ones, ones[:, :], pt, pt[:, t0:t0 + 512])
        P.act(rstd, rstd[:, t0:t0 + 512], pb, pb[:, :], AF.Sqrt, bias=eps_ap(P, eps), scale=1.0 / D,
              extra_reads=[P.consts["eps"]])
        P.op("dve", lambda en, t0=t0: en.reciprocal(out=rstd[:, t0:t0 + 512], in_=rstd[:, t0:t0 + 512]),
             reads=[rstd], writes=[rstd])
    P.pop()
    return rstd


def eps_ap(P, eps):
    return P.consts["eps"][:, 0:1]


def setup_consts(P, eps):
    P.consts = {}
    e = P.sb("epsc", [128, 1], F32)
    P.memset("dve", e, e[:, :], eps)
    P.consts["eps"] = e


def norm_prep_epilogue(P, cfg, xnew, xn_ap, msz, m0, t0, TB, normw, x_out, xw_out, sq_ps, first, last, part_sb):
    pass


def build_prep(cfg):
    nc = bass.Bass("TRN2", target_bir_lowering=False)
    P = Prog(nc)
    DC, S = cfg.DC, cfg.S
    x = P.dram("x", [DC, S], F32, "ExternalInput")
    nw = P.dram("nw", [128, DC // 128], F32, "ExternalInput")
    xw = P.dram("xw", [DC, S], BF16, "ExternalOutput")
    part = P.dram("part", [1, S], F32, "ExternalOutput")
    emit_norm_prep(P, cfg, x, nw, xw, part)
    P.finish([xw, part])
    return nc


def emit_norm_prep(P, cfg, x, nw, xw, part):
    DC, S = cfg.DC, cfg.S
    MT = DC // 128
    TB = 512
    P.push()
    nws = P.sb("nws", [128, MT], F32)
    P.dma("sp", nws, nws[:, :], nw, nw.ap())
    ones = P.sb("ones1", [128, 1], F32)
    P.memset("dve", ones, ones[:, :], 1.0)
    xs = [P.sb("xs", [128, MT, TB], F32) for _ in range(2)]
    sq = [P.sb("sq", [128, MT, TB], F32) for _ in range(2)]
    xo = [P.sb("xo", [128, MT, TB], BF16) for _ in range(2)]
    po = [P.sb("po", [1, TB], F32) for _ in range(2)]
    pss = [P.ps("pss", [1, TB], F32) for _ in range(2)]
    xa = x.ap().rearrange("(m p) s -> p m s", p=128)
    xwa = xw.ap().rearrange("(m p) s -> p m s", p=128)
    for i, t0 in enumerate(range(0, S, TB)):
        a, q, o, pb, pp = xs[i % 2], sq[i % 2], xo[i % 2], pss[i % 2], po[i % 2]
        P.dma("sp", a, a[:, :, :], x, xa[:, :, t0:t0 + TB])
        P.act(q, q[:, :, :], a, a[:, :, :], AF.Square)
        for m in range(MT):
            P.mm(pb, pb[:, :], ones, ones[:, :], q, q[:, m, :], start=(m == 0), stop=(m == MT - 1))
            P.ts("dve" if m % 2 == 0 else "pool", o, o[:, m, :], a, a[:, m, :], nws[:, m:m + 1], None, ALU.mult,
                 extra_reads=[nws])
        P.copy("dve", pp, pp[:, :], pb, pb[:, :])
        P.dma("pool", xw, xwa[:, :, t0:t0 + TB], o, o[:, :, :])
        P.dma("pool", part, part[0:1, t0:t0 + TB], pp, pp[:, :])
    P.pop()


def build_res(cfg, K, final=False):
    nc = bass.Bass("TRN2", target_bir_lowering=False)
    P = Prog(nc)
    DC, S = cfg.DC, cfg.S
    MT = DC // 128
    xin = P.dram("xin", [K, S], BF16, "ExternalInput")
    w = P.dram("w", [K, DC], F32, "ExternalInput")
    x = P.dram("x", [DC, S], F32, "ExternalInput")
    nw = P.dram("nw", [128, DC // 128], F32, "ExternalInput")
    xo = P.dram("xo", [DC, S], F32, "ExternalOutput")
    xw = P.dram("xw", [DC, S], BF16, "ExternalOutput")
    part = P.dram("part", [1, S], F32, "ExternalOutput")
    big = K > 6000
    TB = 256 if big else 512
    GC = 256 if big else 512
    P.push()
    nws = P.sb("nws", [128, MT], F32)
    P.dma("sp", nws, nws[:, :], nw, nw.ap())
    xr = [P.sb("xr", [128, TB], F32) for _ in range(3)]
    xn = [P.sb("xn", [128, TB], F32) for _ in range(3)]
    xb = [P.sb("xb", [128, TB], BF16) for _ in range(3)]
    cnt = [0]

    def epi(mi, mt, t0, pb):
        m0, msz = mt
        i = cnt[0] % 3
        cnt[0] += 1
        a, n, b = xr[i], xn[i], xb[i]
        P.dma("act", a, a[:, :], x, x[m0:m0 + msz, t0:t0 + TB])
        P.tt("dve", n, n[:, :], pb, pb[:, 0:TB], a, a[:, :], ALU.add)
        P.dma("pool", xo, xo[m0:m0 + msz, t0:t0 + TB], n, n[:, :])
        P.ts("pool", b, b[:, :], n, n[:, :], nws[:, mi:mi + 1], None, ALU.mult, extra_reads=[nws])
        P.dma("pool", xw, xw[m0:m0 + msz, t0:t0 + TB], b, b[:, :])

    linear(P, xin, K, S, w, DC, [(m * 128, 128) for m in range(MT)], TB, GC, epi)
    P.pop()
    emit_sumsq(P, cfg, xo, part)
    P.finish([xo, xw, part])
    return nc


def emit_sumsq(P, cfg, x, part):
    DC, S = cfg.DC, cfg.S
    MT = DC // 128
    TB = 512
    P.push()
    ones = P.sb("ones1", [128, 1], F32)
    P.memset("dve", ones, ones[:, :], 1.0)
    xs = [P.sb("xs", [128, MT, TB], F32) for _ in range(2)]
    sq = [P.sb("sq", [128, MT, TB], F32) for _ in range(2)]
    po = [P.sb("po", [1, TB], F32) for _ in range(2)]
    pss = [P.ps("pss", [1, TB], F32) for _ in range(2)]
    xa = x.ap().rearrange("(m p) s -> p m s", p=128)
    for i, t0 in enumerate(range(0, S, TB)):
        a, q, pb, pp = xs[i % 2], sq[i % 2], pss[i % 2], po[i % 2]
        P.dma("sp", a, a[:, :, :], x, xa[:, :, t0:t0 + TB])
        P.act(q, q[:, :, :], a, a[:, :, :], AF.Square)
        for m in range(MT):
            P.mm(pb, pb[:, :], ones, ones[:, :], q, q[:, m, :], start=(m == 0), stop=(m == MT - 1))
        P.copy("dve", pp, pp[:, :], pb, pb[:, :])
        P.dma("pool", part, part[0:1, t0:t0 + TB], pp, pp[:, :])
    P.pop()


def build_ffn(cfg):
    nc = bass.Bass("TRN2", target_bir_lowering=False)
    P = Prog(nc)
    D, S, FC, NR = cfg.D, cfg.S, cfg.FC, cfg.NR
    xin = P.dram("xin", [D, S], BF16, "ExternalInput")
    parts = P.dram("parts", [NR, S], F32, "ExternalInput")
    wg = P.dram("wg", [D, FC], F32, "ExternalInput")
    wu = P.dram("wu", [D, FC], F32, "ExternalInput")
    cw = P.dram("cw", [FC, 4], F32, "ExternalInput")
    out = P.dram("out", [FC, S], BF16, "ExternalOutput")
    setup_consts(P, 1e-6)
    rstd = load_rstd(P, parts, NR, S, D, 1e-6)
    mts = ktiles(FC)
    gsc = P.dram("gsc", [FC, S], F32)
    TB = 512
    P.push()
    cws = P.sb("cws", [128, len(mts), 4], F32)
    for mi, (m0, msz) in enumerate(mts):
        P.dma("sp", cws, cws[0:msz, mi, :], cw, cw[m0:m0 + msz, :])
    gbuf = [P.sb("gbuf", [128, TB + 2], F32) for _ in range(2)]
    acc = [P.sb("acc", [128, TB], F32) for _ in range(2)]
    cnt = [0]
    halo = {}

    def epi_gate(mi, mt, t0, pb):
        m0, msz = mt
        i = cnt[0] % 2
        cnt[0] += 1
        g, a = gbuf[i], acc[i]
        prev = halo.get(mi)
        if t0 == 0:
            P.memset("pool", g, g[:, 0:2], 0.0)
        else:
            pg = prev
            P.copy("pool", g, g[0:msz, 0:2], pg, pg[0:msz, TB:TB + 2])
        P.tt("dve", g, g[0:msz, 2:TB + 2], pb, pb[0:msz, 0:TB], rstd, rstd[0:msz, t0:t0 + TB], ALU.mult)
        halo[mi] = g
        P.ts("dve", a, a[0:msz, :], g, g[0:msz, 0:TB], cws[0:msz, mi, 0:1], cws[0:msz, mi, 3:4], ALU.mult, ALU.add,
             extra_reads=[cws])
        P.stt("dve", a, a[0:msz, :], g, g[0:msz, 1:TB + 1], cws[0:msz, mi, 1:2], a, a[0:msz, :], ALU.mult, ALU.add,
              extra_reads=[cws])
        P.stt("dve", a, a[0:msz, :], g, g[0:msz, 2:TB + 2], cws[0:msz, mi, 2:3], a, a[0:msz, :], ALU.mult, ALU.add,
              extra_reads=[cws])
        P.act(a, a[0:msz, :], a, a[0:msz, :], AF.Silu)
        P.dma("pool", gsc, gsc[m0:m0 + msz, t0:t0 + TB], a, a[0:msz, :])

    halo_sb = P.sb("halo", [128, len(mts), 2], F32)

    def epi_gate2(mi, mt, t0, pb):
        m0, msz = mt
        i = cnt[0] % 2
        cnt[0] += 1
        g, a = gbuf[i], acc[i]
        if t0 == 0:
            P.memset("pool", g, g[:, 0:2], 0.0)
        else:
            P.copy("pool", g, g[0:msz, 0:2], halo_sb, halo_sb[0:msz, mi, :])
        P.tt("dve", g, g[0:msz, 2:TB + 2], pb, pb[0:msz, 0:TB], rstd, rstd[0:msz, t0:t0 + TB], ALU.mult)
        P.copy("pool", halo_sb, halo_sb[0:msz, mi, :], g, g[0:msz, TB:TB + 2])
        P.ts("dve", a, a[0:msz, :], g, g[0:msz, 0:TB], cws[0:msz, mi, 0:1], cws[0:msz, mi, 3:4], ALU.mult, ALU.add,
             extra_reads=[cws])
        P.stt("dve", a, a[0:msz, :], g, g[0:msz, 1:TB + 1], cws[0:msz, mi, 1:2], a, a[0:msz, :], ALU.mult, ALU.add,
              extra_reads=[cws])
        P.stt("dve", a, a[0:msz, :], g, g[0:msz, 2:TB + 2], cws[0:msz, mi, 2:3], a, a[0:msz, :], ALU.mult, ALU.add,
              extra_reads=[cws])
        P.act(a, a[0:msz, :], a, a[0:msz, :], AF.Silu)
        P.dma("pool", gsc, gsc[m0:m0 + msz, t0:t0 + TB], a, a[0:msz, :])

    linear(P, xin, D, S, wg, FC, mts, TB, 512, epi_gate2)

    gl = [P.sb("gl", [128, TB], F32) for _ in range(2)]
    ub = [P.sb("ub", [128, TB], F32) for _ in range(2)]
    ob = [P.sb("ob", [128, TB], BF16) for _ in range(2)]

    def epi_up(mi, mt, t0, pb):
        m0, msz = mt
        i = cnt[0] % 2
        cnt[0] += 1
        g, u, o = gl[i], ub[i], ob[i]
        P.dma("act", g, g[0:msz, :], gsc, gsc[m0:m0 + msz, t0:t0 + TB])
        P.tt("dve", u, u[0:msz, :], pb, pb[0:msz, 0:TB], rstd, rstd[0:msz, t0:t0 + TB], ALU.mult)
        P.tt("pool", o, o[0:msz, :], u, u[0:msz, :], g, g[0:msz, :], ALU.mult)
        P.dma("pool", out, out[m0:m0 + msz, t0:t0 + TB], o, o[0:msz, :])

    linear(P, xin, D, S, wu, FC, mts, TB, 512, epi_up)
    P.pop()
    P.finish([out])
    return nc


def build_final(cfg):
    nc = bass.Bass("TRN2", target_bir_lowering=False)
    P = Prog(nc)
    DC, S, NR, D = cfg.DC, cfg.S, cfg.NR, cfg.D
    MT = DC // 128
    x = P.dram("x", [DC, S], F32, "ExternalInput")
    nw = P.dram("nw", [128, DC // 128], F32, "ExternalInput")
    parts = P.dram("parts", [NR, S], F32, "ExternalInput")
    out = P.dram("out", [DC, S], F32, "ExternalOutput")
    setup_consts(P, 1e-6)
    rstd = load_rstd(P, parts, NR, S, D, 1e-6)
    TB = 512
    P.push()
    nws = P.sb("nws", [128, MT], F32)
    P.dma("sp", nws, nws[:, :], nw, nw.ap())
    xs = [P.sb("xs", [128, TB], F32) for _ in range(3)]
    i = 0
    for m in range(MT):
        for t0 in range(0, S, TB):
            a = xs[i % 3]
            i += 1
            P.dma("sp", a, a[:, :], x, x[m * 128:(m + 1) * 128, t0:t0 + TB])
            P.stt("dve", a, a[:, :], a, a[:, :], nws[:, m:m + 1], rstd, rstd[:, t0:t0 + TB], ALU.mult, ALU.mult,
                  extra_reads=[nws])
            P.dma("pool", out, out[m * 128:(m + 1) * 128, t0:t0 + TB], a, a[:, :])
    P.pop()
    P.finish([out])
    return nc


def launch(nc, in_maps, n):
    res = run_bass_kernel_spmd(nc, in_maps, core_ids=list(range(n)))
    return res.results


def colslice(v, c, n):
    return np.ascontiguousarray(v[c * n:(c + 1) * n])


def run_model(cfg, inp, skip_mixers=False, debug=None):
    NR, S, D, DC, FC, DFF = cfg.NR, cfg.S, cfg.D, cfg.DC, cfg.FC, cfg.DFF
    f32 = np.float32
    L = inp["w_in"].shape[0]
    xT = np.ascontiguousarray(np.asarray(inp["x"], f32)[0].T)
    xs = [colslice(xT, c, DC) for c in range(NR)]
    nws = lambda v: [np.ascontiguousarray(colslice(np.asarray(v, f32), c, DC).reshape(DC // 128, 128).T) for c in range(NR)]

    nc_prep = build_prep(cfg)
    nw = nws(inp["attn_norm"][0])
    r = launch(nc_prep, [{"x": xs[c], "nw": nw[c]} for c in range(NR)], NR)
    xw = np.concatenate([r[c]["xw"] for c in range(NR)], 0)
    parts = np.concatenate([r[c]["part"] for c in range(NR)], 0)

    nc_attn = None if skip_mixers else build_attn(cfg)
    nc_res_a = build_res(cfg, NR * cfg.MIXC)
    nc_ffn = build_ffn(cfg)
    nc_res_f = build_res(cfg, DFF)
    for l in range(L):
        if skip_mixers:
            mixT = np.zeros((NR * cfg.MIXC, S), NPBF)
        else:
            maps = [attn_inputs(cfg, inp, l, c, xw, parts) for c in range(NR)]
            r = launch(nc_attn, maps, NR)
            if debug is not None:
                debug.append(r)
            mixT = np.concatenate([r[c]["mix"] for c in range(NR)], 0)
        wo = wout_shards(cfg, np.asarray(inp["w_out"][l], f32))
        nw = nws(inp["ffn_norm"][l])
        r = launch(nc_res_a, [{"xin": mixT, "w": wo[c], "x": xs[c], "nw": nw[c]} for c in range(NR)], NR)
        xs = [r[c]["xo"] for c in range(NR)]
        xw = np.concatenate([r[c]["xw"] for c in range(NR)], 0)
        parts = np.concatenate([r[c]["part"] for c in range(NR)], 0)

        wg = np.asarray(inp["w_ffn_gate"][l], f32)
        wu = np.asarray(inp["w_ffn_up"][l], f32)
        cwl = np.asarray(inp["ffn_conv"][l], f32)
        cb = np.asarray(inp["ffn_conv_b"][l], f32)
        maps = []
        for c in range(NR):
            sl = slice(c * FC, (c + 1) * FC)
            cw = np.ascontiguousarray(np.concatenate([cwl[:, sl].T, cb[sl][:, None]], 1))
            maps.append({"xin": xw, "parts": parts, "wg": np.ascontiguousarray(wg[:, sl]),
                         "wu": np.ascontiguousarray(wu[:, sl]), "cw": cw})
        r = launch(nc_ffn, maps, NR)
        actT = np.concatenate([r[c]["out"] for c in range(NR)], 0)

        wd = np.asarray(inp["w_ffn_down"][l], f32)
        nxt = inp["attn_norm"][l + 1] if l + 1 < L else inp["final_norm"]
        nw = nws(nxt)
        r = launch(nc_res_f, [{"xin": actT, "w": np.ascontiguousarray(wd[:, c * DC:(c + 1) * DC]), "x": xs[c],
                               "nw": nw[c]} for c in range(NR)], NR)
        xs = [r[c]["xo"] for c in range(NR)]
        xw = np.concatenate([r[c]["xw"] for c in range(NR)], 0)
        parts = np.concatenate([r[c]["part"] for c in range(NR)], 0)

    nc_fin = build_final(cfg)
    nw = nws(inp["final_norm"])
    r = launch(nc_fin, [{"x": xs[c], "nw": nw[c], "parts": parts} for c in range(NR)], NR)
    outT = np.concatenate([r[c]["out"] for c in range(NR)], 0)
    return np.ascontiguousarray(outT.T)[None].astype(np.float32)


def gdn_head(cfg, c, j):
    h = c + cfg.NR * j
    return h if h < cfg.GH else None


def wout_shards(cfg, wo):
    NR = cfg.NR
    rows = []
    for r in range(NR):
        for j in range(cfg.GS):
            h = gdn_head(cfg, r, j)
            rows.append(wo[h * 128:(h + 1) * 128] if h is not None else np.zeros((128, cfg.D), np.float32))
        rows.append(wo[cfg.GW + r * 128: cfg.GW + (r + 1) * 128])
        b = cfg.GW + cfg.AW + r * cfg.RC
        rows.append(wo[b:b + cfg.RC])
    full = np.concatenate(rows, 0)
    return [np.ascontiguousarray(full[:, c * cfg.DC:(c + 1) * cfg.DC]) for c in range(NR)]


def kernel(**inputs):
    return run_model(Cfg(), inputs)


def t5_bucket_np(dist):
    exact = 16
    d = np.maximum(dist, 1).astype(np.float32)
    log_b = exact + (np.log(d / np.float32(exact)) / np.float32(math.log(2048 / exact)) * np.float32(32 - exact)).astype(np.int32)
    return np.where(dist < exact, dist, np.minimum(log_b, 31))


SWA_DIL = (1, 4, 16)


def swa_bias_tiles(rel_bias_col):
    out = np.zeros((3, 2, 128, 128), np.float32)
    j = np.arange(128)[:, None]
    i = np.arange(128)[None, :]
    for p, d in enumerate(SWA_DIL):
        for t in range(2):
            steps = (i + 128 - j) if t == 0 else (i - j)
            valid = (steps >= 0) & (steps <= 128)
            idx = t5_bucket_np(np.maximum(steps, 0) * d)
            out[p, t] = np.where(valid, rel_bias_col[idx], np.float32(-30000.0))
    return out


def attn_inputs(cfg, inp, l, c, xw, parts):
    f32 = np.float32
    NR = cfg.NR
    W = np.asarray(inp["w_in"][l], f32)
    D = cfg.D
    cols = []
    GW, GH = cfg.GW, cfg.GH
    zc = np.zeros((D, 128), f32)
    for j in range(cfg.GS):
        h = gdn_head(cfg, c, j)
        for blk in range(4):
            cols.append(W[:, blk * GW + h * 128: blk * GW + (h + 1) * 128] if h is not None else zc)
    b = cfg.GDN_IN
    for blk in range(3):
        cols.append(W[:, b + blk * cfg.AW + c * 128: b + blk * cfg.AW + (c + 1) * 128])
    b = cfg.GDN_IN + cfg.SWA_IN
    for blk in range(3):
        cols.append(W[:, b + blk * cfg.RW + c * cfg.RC: b + blk * cfg.RW + (c + 1) * cfg.RC])
    cols.append(W[:, b + 3 * cfg.RW: b + 3 * cfg.RW + cfg.LW + cfg.LA + cfg.LG])
    z1 = np.zeros((D, 1), f32)
    for j in range(cfg.GS):
        h = gdn_head(cfg, c, j)
        cols.append(W[:, 4 * GW + h: 4 * GW + h + 1] if h is not None else z1)
        cols.append(W[:, 4 * GW + GH + h: 4 * GW + GH + h + 1] if h is not None else z1)
    w = np.ascontiguousarray(np.concatenate(cols, 1))
    assert w.shape[1] == cfg.PC
    m = {"xin": xw, "parts": parts, "w": w}
    m["cst"] = host_consts()
    m["bm"] = swa_bias_tiles(np.asarray(inp["rel_bias"], f32)[:, c])
    m.update(gdn_inputs(cfg, inp, l, c))
    m.update(rwkv_inputs(cfg, inp, l, c))
    return m


def gdn_inputs(cfg, inp, l, c):
    f32 = np.float32
    conv = np.asarray(inp["gdn_conv"][l], f32)
    gconv = np.zeros((128, cfg.GS * 3 * 4), f32)
    gsc = np.zeros((1, cfg.GS * 2), f32)
    for j in range(cfg.GS):
        h = gdn_head(cfg, c, j)
        if h is None:
            continue
        for b in range(3):
            gconv[:, (j * 3 + b) * 4:(j * 3 + b + 1) * 4] = conv[:, b * cfg.GW + h * 128: b * cfg.GW + (h + 1) * 128].T
        gsc[0, 2 * j] = np.asarray(inp["gdn_a_log"], f32)[l, h]
        gsc[0, 2 * j + 1] = np.asarray(inp["gdn_dt_bias"], f32)[l, h]
    gnw = np.ascontiguousarray(np.asarray(inp["gdn_norm"], f32)[l][:, None])
    return {"gconv": gconv, "gsc": gsc, "gnw": gnw}


def rwkv_inputs(cfg, inp, l, c):
    f32 = np.float32
    RW, RC = cfg.RW, cfg.RC
    ch = slice(c * RC, (c + 1) * RC)
    mu = np.asarray(inp["rwkv_mu"], f32)[l]
    vec = np.zeros((64, 3, 10), f32)
    srcs = [mu[0:RW][ch], mu[RW:2 * RW][ch], mu[2 * RW:3 * RW][ch],
            np.asarray(inp["rwkv_w0"], f32)[l][ch], np.asarray(inp["rwkv_a0"], f32)[l][ch],
            np.asarray(inp["rwkv_k_k"], f32)[l][ch], np.asarray(inp["rwkv_k_a"], f32)[l][ch],
            np.asarray(inp["rwkv_r_k"], f32)[l].reshape(-1)[ch],
            np.asarray(inp["rwkv_ln_w"], f32)[l][ch], np.asarray(inp["rwkv_ln_b"], f32)[l][ch]]
    for i, v in enumerate(srcs):
        vec[:, :, i] = v.reshape(3, 64).T
    mul = np.zeros((128, 6), f32)
    ml = mu[3 * RW:]
    for t in range(6):
        seg = ml[t * 128:(t + 1) * 128]
        mul[:len(seg), t] = seg
    gup = np.zeros((512, RC), f32)
    gup[:cfg.LG] = np.asarray(inp["rwkv_g_up"], f32)[l][:, ch]
    return {"rvec": np.ascontiguousarray(vec.reshape(64, 30)), "rmul": mul,
            "rwup": np.ascontiguousarray(np.asarray(inp["rwkv_w_up"], f32)[l][:, ch]),
            "raup": np.ascontiguousarray(np.asarray(inp["rwkv_a_up"], f32)[l][:, ch]),
            "rgup": gup}


def build_attn(cfg, parts_enabled=("swa", "gdn", "rwkv")):
    nc = bass.Bass("TRN2", target_bir_lowering=False)
    P = Prog(nc)
    D, S, NR, PC = cfg.D, cfg.S, cfg.NR, cfg.PC
    xin = P.dram("xin", [D, S], BF16, "ExternalInput")
    parts = P.dram("parts", [NR, S], F32, "ExternalInput")
    w = P.dram("w", [D, PC], F32, "ExternalInput")
    bm = P.dram("bm", [3, 2, 128, 128], F32, "ExternalInput")
    P.cst = P.dram("cst", [4, 128, 128], F32, "ExternalInput")
    P.rin = {"rvec": P.dram("rvec", [64, 30], F32, "ExternalInput"),
             "rmul": P.dram("rmul", [128, 6], F32, "ExternalInput"),
             "rwup": P.dram("rwup", [128, cfg.RC], F32, "ExternalInput"),
             "raup": P.dram("raup", [128, cfg.RC], F32, "ExternalInput"),
             "rgup": P.dram("rgup", [512, cfg.RC], F32, "ExternalInput")}
    P.gin = {"gconv": P.dram("gconv", [128, cfg.GS * 12], F32, "ExternalInput"),
             "gsc": P.dram("gsc", [1, cfg.GS * 2], F32, "ExternalInput"),
             "gnw": P.dram("gnw", [128, 1], F32, "ExternalInput")}
    mix = P.dram("mix", [cfg.MIXC, S], BF16, "ExternalOutput")
    proj = P.dram("proj", [PC, S], F32)
    setup_consts(P, 1e-6)
    P.push()
    rstd = load_rstd(P, parts, NR, S, D, 1e-6)
    TB = 512
    pj = [P.sb("pj", [128, TB], F32) for _ in range(3)]
    cnt = [0]

    def epi(mi, mt, t0, pb):
        m0, msz = mt
        a = pj[cnt[0] % 3]
        cnt[0] += 1
        P.tt("dve", a, a[0:msz, :], pb, pb[0:msz, 0:TB], rstd, rstd[0:msz, t0:t0 + TB], ALU.mult)
        P.dma("pool", proj, proj[m0:m0 + msz, t0:t0 + TB], a, a[0:msz, :])

    linear(P, xin, D, S, w, PC, ktiles(PC), TB, 512, epi)
    P.pop()
    emit_gdn(P, cfg, proj, mix) if "gdn" in parts_enabled else emit_zero_rows(P, cfg, mix, 0, 256)
    emit_swa(P, cfg, proj, mix, bm) if "swa" in parts_enabled else emit_zero_rows(P, cfg, mix, 256, 128)
    emit_rwkv(P, cfg, proj, mix) if "rwkv" in parts_enabled else emit_zero_rows(P, cfg, mix, 384, 192)
    P.finish([mix])
    return nc


def emit_zero_rows(P, cfg, mix, r0, n):
    P.push()
    z = P.sb("z", [128, 2048], BF16)
    P.memset("dve", z, z[:, :], 0.0)
    for a in range(r0, r0 + n, 128):
        m = min(128, r0 + n - a)
        for t0 in range(0, cfg.S, 2048):
            P.dma("pool", mix, mix[a:a + m, t0:t0 + 2048], z, z[0:m, :])
    P.pop()


def emit_gdn(P, cfg, proj, mix):
    S = cfg.S
    SEG = 1024 if S >= 1024 else S
    NSEG = S // SEG
    C = 128
    NSL = cfg.GS
    P.push()
    ident = P.sb("ident", [128, 128], F32)
    make_identity(P, ident)
    Ui = P.sb("Ui", [128, 128], F32)
    P.dma("sp", Ui, Ui[:, :], P.cst, P.cst[1, :, :])
    Us = P.sb("Us", [128, 128], F32)
    P.dma("sp", Us, Us[:, :], P.cst, P.cst[2, :, :])
    ones = P.sb("ones", [128, 128], F32)
    P.memset("dve", ones, ones[:, :], 1.0)
    gconv = P.sb("gconv", [128, cfg.GS * 12], F32)
    P.dma("sp", gconv, gconv[:, :], P.gin["gconv"], P.gin["gconv"].ap())
    gsc = P.sb("gsc", [1, cfg.GS * 2], F32)
    P.dma("sp", gsc, gsc[:, :], P.gin["gsc"], P.gin["gsc"].ap())
    gnw = P.sb("gnw", [128, 1], F32)
    P.dma("sp", gnw, gnw[:, :], P.gin["gnw"], P.gin["gnw"].ap())
    negA = P.sb("negA", [1, cfg.GS], F32)
    for j in range(cfg.GS):
        P.act(negA, negA[:, j:j + 1], gsc, gsc[:, 2 * j:2 * j + 1], AF.Exp)
    P.ts("dve", negA, negA[:, :], negA, negA[:, :], -1.0, None, ALU.mult)
    one1 = P.sb("one1", [1, 1], F32)
    P.memset("dve", one1, one1[:, :], 1.0)

    pfree = [P.ps("gps", [128, 256], F32) for _ in range(8)]

    def pp():
        b = pfree.pop(0)
        pfree.append(b)
        return b

    def get():
        while not pfree:
            yield
        return pfree.pop(0)

    def rel(b):
        pfree.append(b)

    T = {}

    def tb(name, shape=(128, 128), n=2):
        if name not in T:
            T[name] = ([P.sb("g_" + name, list(shape), F32) for _ in range(n)], [0])
        bufs, k = T[name]
        k[0] += 1
        return bufs[k[0] % len(bufs)]

    raw = P.sb("raw", [128, SEG + 3], F32)
    acc = P.sb("acc", [128, SEG], F32)
    SB = []
    for j in range(NSL):
        d = {}
        for nm in ("qT", "kT", "vT", "zs"):
            d[nm] = P.sb("g" + nm + str(j), [128, SEG], F32)
        d["ob"] = P.sb("gob" + str(j), [128, SEG], BF16)
        d["brow"] = P.sb("brow" + str(j), [1, SEG], F32)
        d["grow"] = P.sb("grow" + str(j), [1, SEG], F32)
        d["St"] = P.sb("gS" + str(j), [128, 128], F32)
        P.memset("dve", d["St"], d["St"][:, :], 0.0)
        SB.append(d)
    res = {}

    def chunkA(j, ci):
        sb = SB[j]
        qT, kT, vT, brow, grow = sb["qT"], sb["kT"], sb["vT"], sb["brow"], sb["grow"]
        sfx = str(j)
        csl = slice(ci * C, ci * C + C)
        pb = yield from get()
        P.mm(pb, pb[:, 0:1], grow, grow[:, csl], one1, one1[:, :])
        P.mm(pb, pb[:, 1:2], brow, brow[:, csl], one1, one1[:, :])
        yield
        cols = tb("cols" + sfx, (128, 8))
        P.copy("dve", cols, cols[:, 0:2], pb, pb[:, 0:2])
        rel(pb)
        yield
        pb = yield from get()
        P.mm(pb, pb[:, 0:1], Ui, Ui[:, :], cols, cols[:, 0:1])
        Ug = tb("Ug" + sfx)
        P.ts("dve", Ug, Ug[:, :], Ui, Ui[:, :], cols[:, 0:1], None, ALU.mult, extra_reads=[cols])
        yield
        P.copy("dve", cols, cols[:, 2:3], pb, pb[:, 0:1])
        rel(pb)
        pG = yield from get()
        P.mm(pG, pG[:, 0:128], ones, ones[:, :], Ug, Ug[:, :])
        yield
        Gsb = tb("Gsb" + sfx)
        P.copy("act", Gsb, Gsb[:, :], pG, pG[:, 0:128])
        rel(pG)
        yield
        dT = tb("dT" + sfx)
        P.ts("dve", dT, dT[:, :], Gsb, Gsb[:, :], cols[:, 2:3], 0.0, ALU.subtract, ALU.min, extra_reads=[cols])
        eG = tb("eG" + sfx)
        P.act(eG, eG[:, :], Gsb, Gsb[:, :], AF.Exp)
        yield
        gam = tb("gam" + sfx)
        P.act(gam, gam[:, :], dT, dT[:, :], AF.Exp)
        P.ts("dve", cols, cols[:, 4:5], Gsb, Gsb[:, 127:128], cols[:, 2:3], None, ALU.subtract, extra_reads=[cols])
        yield
        P.act(cols, cols[:, 3:4], cols, cols[:, 2:3], AF.Exp)
        P.act(cols, cols[:, 4:5], cols, cols[:, 4:5], AF.Exp)
        gami = tb("gami" + sfx)
        P.tt("pool", gami, gami[:, :], gam, gam[:, :], Ui, Ui[:, :], ALU.mult)
        gams = tb("gams" + sfx)
        P.tt("pool", gams, gams[:, :], gam, gam[:, :], Us, Us[:, :], ALU.mult)
        yield
        pk = yield from get()
        P.tr(pk, pk[:, 0:128], kT, kT[:, csl], ident, ident[:, :])
        yield
        kbG = tb("kbG" + sfx)
        P.ts("dve", kbG, kbG[:, :], pk, pk[:, 0:128], cols[:, 1:2], cols[:, 3:4], ALU.mult, ALU.mult, extra_reads=[cols])
        kd = tb("kd" + sfx)
        P.ts("dve", kd, kd[:, :], pk, pk[:, 0:128], cols[:, 4:5], None, ALU.mult, extra_reads=[cols])
        rel(pk)
        yield
        pv = yield from get()
        P.tr(pv, pv[:, 0:128], vT, vT[:, csl], ident, ident[:, :])
        yield
        vb = tb("vb" + sfx)
        P.ts("dve", vb, vb[:, :], pv, pv[:, 0:128], cols[:, 1:2], None, ALU.mult, extra_reads=[cols])
        rel(pv)
        yield
        pbr = yield from get()
        P.mm(pbr, pbr[:, 0:128], ones, ones[0:1, :], brow, brow[:, csl])
        yield
        kbT = tb("kbT" + sfx)
        P.tt("dve", kbT, kbT[:, :], kT, kT[:, csl], pbr, pbr[:, 0:128], ALU.mult)
        rel(pbr)
        yield
        pA = yield from get()
        P.mm(pA, pA[:, 0:128], kT, kT[:, csl], kbT, kbT[:, :])
        P.mm(pA, pA[:, 128:256], kT, kT[:, csl], qT, qT[:, csl])
        yield
        Pm = tb("Pm" + sfx, n=3)
        P.stt("dve", Pm, Pm[:, :], pA, pA[:, 0:128], -1.0, gams, gams[:, :], ALU.mult, ALU.mult)
        attnT = tb("attnT" + sfx)
        P.tt("dve", attnT, attnT[:, :], pA, pA[:, 128:256], gami, gami[:, :], ALU.mult)
        rel(pA)
        yield
        pq = yield from get()
        P.tr(pq, pq[:, 0:128], Pm, Pm[:, :], ident, ident[:, :])
        yield
        Qm = tb("Qm" + sfx, n=3)
        P.copy("act", Qm, Qm[:, :], pq, pq[:, 0:128])
        rel(pq)
        R = tb("R" + sfx, n=3)
        P.tt("pool", R, R[:, :], Pm, Pm[:, :], ident, ident[:, :], ALU.add)
        qgT = tb("qgT" + sfx)
        P.tt("pool", qgT, qgT[:, :], qT, qT[:, csl], eG, eG[:, :], ALU.mult)
        yield
        for k in range(1, 7):
            pQ = yield from get()
            P.mm(pQ, pQ[:, 0:128], Pm, Pm[:, :], Qm, Qm[:, :])
            yield
            Qn = tb("Qm" + sfx, n=3)
            P.copy("dve", Qn, Qn[:, :], pQ, pQ[:, 0:128])
            rel(pQ)
            yield
            if k < 6:
                pP = yield from get()
                P.mm(pP, pP[:, 0:128], Qm, Qm[:, :], Pm, Pm[:, :])
                yield
                Pn = tb("Pm" + sfx, n=3)
                P.copy("act", Pn, Pn[:, :], pP, pP[:, 0:128])
                rel(pP)
                yield
            pR = yield from get()
            P.mm(pR, pR[:, 0:128], Qn, Qn[:, :], R, R[:, :])
            yield
            Rn = tb("R" + sfx, n=3)
            P.tt("dve", Rn, Rn[:, :], R, R[:, :], pR, pR[:, 0:128], ALU.add)
            rel(pR)
            R, Qm = Rn, Qn
            if k < 6:
                Pm = Pn
            yield
        pu = yield from get()
        P.mm(pu, pu[:, 0:128], R, R[:, :], vb, vb[:, :])
        yield
        u = tb("u" + sfx)
        P.copy("act", u, u[:, :], pu, pu[:, 0:128])
        rel(pu)
        yield
        pw = yield from get()
        P.mm(pw, pw[:, 0:128], kbG, kbG[:, :], R, R[:, :])
        yield
        wT = tb("wT" + sfx)
        P.copy("act", wT, wT[:, :], pw, pw[:, 0:128])
        rel(pw)
        res[(j, ci)] = dict(u=u, wT=wT, qgT=qgT, attnT=attnT, kd=kd, eG=eG)
        yield

    def chunkB(j, ci):
        sb = SB[j]
        St, zs, ob = sb["St"], sb["zs"], sb["ob"]
        r = res.pop((j, ci))
        u, wT, qgT, attnT, kd, eG = r["u"], r["wT"], r["qgT"], r["attnT"], r["kd"], r["eG"]
        sfx = str(j)
        csl = slice(ci * C, ci * C + C)
        pa = yield from get()
        P.mm(pa, pa[:, 0:128], wT, wT[:, :], St, St[:, :])
        yield
        vn = tb("vn" + sfx)
        P.tt("dve", vn, vn[:, :], u, u[:, :], pa, pa[:, 0:128], ALU.subtract)
        rel(pa)
        yield
        pS = yield from get()
        P.mm(pS, pS[:, 0:128], kd, kd[:, :], vn, vn[:, :])
        po = yield from get()
        P.mm(po, po[:, 0:128], qgT, qgT[:, :], St, St[:, :], start=True, stop=False)
        P.mm(po, po[:, 0:128], attnT, attnT[:, :], vn, vn[:, :], start=False, stop=True)
        yield
        P.stt("dve", St, St[:, :], St, St[:, :], eG[:, 127:128], pS, pS[:, 0:128], ALU.mult, ALU.add, extra_reads=[eG])
        rel(pS)
        osb = tb("osb" + sfx)
        P.copy("act", osb, osb[:, :], po, po[:, 0:128])
        rel(po)
        yield
        osq = tb("osq" + sfx)
        P.act(osq, osq[:, :], osb, osb[:, :], AF.Square)
        yield
        oc = tb("oc" + sfx, (128, 2))
        P.op("dve", lambda en, oc=oc, osq=osq: en.reduce_sum(out=oc[:, 0:1], in_=osq[:, :], axis=AX.X),
             reads=[osq], writes=[oc])
        yield
        P.act(oc, oc[:, 0:1], oc, oc[:, 0:1], AF.Sqrt, bias=P.consts["eps"][:, 0:1], scale=1.0 / 128,
              extra_reads=[P.consts["eps"]])
        yield
        P.op("dve", lambda en, oc=oc: en.reciprocal(out=oc[:, 0:1], in_=oc[:, 0:1]), reads=[oc], writes=[oc])
        P.ts("dve", osb, osb[:, :], osb, osb[:, :], oc[:, 0:1], None, ALU.mult, extra_reads=[oc])
        yield
        pt = yield from get()
        P.tr(pt, pt[:, 0:128], osb, osb[:, :], ident, ident[:, :])
        yield
        P.stt("dve", ob, ob[:, csl], pt, pt[:, 0:128], gnw[:, 0:1], zs, zs[:, csl], ALU.mult, ALU.mult, extra_reads=[gnw])
        rel(pt)
        yield

    for g in range(NSEG):
        t0 = g * SEG
        for j in range(NSL):
            sb = SB[j]
            qT, kT, vT, zs, brow, grow = sb["qT"], sb["kT"], sb["vT"], sb["zs"], sb["brow"], sb["grow"]
            rq = cfg.o_g + 512 * j
            for b, dst in enumerate((qT, kT, vT)):
                row = rq + 128 * b
                if g == 0:
                    P.memset("pool", raw, raw[:, 0:3], 0.0)
                    P.dma("sp", raw, raw[:, 3:SEG + 3], proj, proj[row:row + 128, 0:SEG])
                else:
                    P.dma("sp", raw, raw[:, :], proj, proj[row:row + 128, t0 - 3:t0 + SEG])
                cb = (j * 3 + b) * 4
                P.ts("dve", acc, acc[:, :], raw, raw[:, 0:SEG], gconv[:, cb:cb + 1], None, ALU.mult, extra_reads=[gconv])
                for i in range(1, 4):
                    P.stt("dve", acc, acc[:, :], raw, raw[:, i:i + SEG], gconv[:, cb + i:cb + i + 1],
                          acc, acc[:, :], ALU.mult, ALU.add, extra_reads=[gconv])
                P.act(dst, dst[:, :], acc, acc[:, :], AF.Silu)
            P.dma("sp", acc, acc[:, :], proj, proj[rq + 384:rq + 512, t0:t0 + SEG])
            P.act(zs, zs[:, :], acc, acc[:, :], AF.Silu)
            for dst, sc in ((qT, 128 ** -0.5), (kT, 1.0)):
                P.act(acc, acc[:, :], dst, dst[:, :], AF.Square)
                for c0 in range(0, SEG, 256):
                    pb = pp()
                    P.mm(pb, pb[:, :], ones, ones[:, :], acc, acc[:, c0:c0 + 256])
                    rs = tb("rs", (128, 256))
                    P.act(rs, rs[:, :], pb, pb[:, :], AF.Sqrt, bias=P.consts["eps"][:, 0:1], extra_reads=[P.consts["eps"]])
                    P.op("dve", lambda en, rs=rs: en.reciprocal(out=rs[:, :], in_=rs[:, :]), reads=[rs], writes=[rs])
                    P.stt("dve", dst, dst[:, c0:c0 + 256], dst, dst[:, c0:c0 + 256], sc, rs, rs[:, :], ALU.mult, ALU.mult)
            srow = cfg.o_s + 2 * j
            P.dma("sp", brow, brow[:, :], proj, proj[srow:srow + 1, t0:t0 + SEG])
            P.act(brow, brow[:, :], brow, brow[:, :], AF.Sigmoid)
            P.dma("sp", grow, grow[:, :], proj, proj[srow + 1:srow + 2, t0:t0 + SEG])
            P.act(grow, grow[:, :], grow, grow[:, :], AF.Exp, bias=gsc[:, 2 * j + 1:2 * j + 2], extra_reads=[gsc])
            P.act(grow, grow[:, :], grow, grow[:, :], AF.Ln, bias=one1[:, 0:1], extra_reads=[one1])
            P.ts("dve", grow, grow[:, :], grow, grow[:, :], negA[:, j:j + 1], None, ALU.mult, extra_reads=[negA])
        NCH = SEG // C
        for ci in range(NCH + 1):
            gens = []
            for j in range(NSL):
                if ci >= 1:
                    gens.append(chunkB(j, ci - 1))
            for j in range(NSL):
                if ci < NCH:
                    gens.append(chunkA(j, ci))
            run_rr(gens)
        for j in range(NSL):
            P.dma("pool", mix, mix[128 * j:128 * (j + 1), t0:t0 + SEG], SB[j]["ob"], SB[j]["ob"][:, :])
    P.pop()


def run_rr(gens):
    gens = list(gens)
    while gens:
        nxt = []
        for g in gens:
            try:
                next(g)
                nxt.append(g)
            except StopIteration:
                pass
        gens = nxt


def emit_rwkv(P, cfg, proj, mix):
    S = cfg.S
    SEG = 512 if S >= 512 else S
    NSEG = S // SEG
    C = 64
    BW = min(256, SEG)
    NH = 3
    P.push()
    ident = P.sb("ident", [128, 128], F32)
    make_identity(P, ident)
    Ui = P.sb("Ui", [128, 128], F32)
    P.dma("sp", Ui, Ui[:, :], P.cst, P.cst[1, :, :])
    Us = P.sb("Us", [128, 128], F32)
    P.dma("sp", Us, Us[:, :], P.cst, P.cst[2, :, :])
    ones = P.sb("ones", [64, 64], F32)
    P.memset("dve", ones, ones[:, :], 1.0)
    rvec = P.sb("rvec", [64, 30], F32)
    P.dma("sp", rvec, rvec[:, :], P.rin["rvec"], P.rin["rvec"].ap())
    rmul = P.sb("rmul", [128, 6], F32)
    P.dma("sp", rmul, rmul[:, :], P.rin["rmul"], P.rin["rmul"].ap())
    wup = P.sb("wup", [128, cfg.RC], F32)
    P.dma("sp", wup, wup[:, :], P.rin["rwup"], P.rin["rwup"].ap())
    aup = P.sb("aup", [128, cfg.RC], F32)
    P.dma("sp", aup, aup[:, :], P.rin["raup"], P.rin["raup"].ap())
    gup = P.sb("gup", [128, 4, cfg.RC], F32)
    P.dma("sp", gup, gup[:, :, :], P.rin["rgup"], P.rin["rgup"].ap().rearrange("(t p) c -> p t c", p=128))
    omk = P.sb("omk", [64, 3], F32)
    for h in range(NH):
        P.ts("dve", omk, omk[:, h:h + 1], rvec, rvec[:, h * 10 + 6:h * 10 + 7], -1.0, 1.0, ALU.mult, ALU.add)
    eps12 = P.sb("eps12", [64, 1], F32)
    P.memset("dve", eps12, eps12[:, :], 1e-12)
    epsgn = P.sb("epsgn", [64, 1], F32)
    P.memset("dve", epsgn, epsgn[:, :], 64e-5)

    pfree = [P.ps("rps", [128, 512], F32) for _ in range(8)]

    def pp():
        b = pfree.pop(0)
        pfree.append(b)
        return b

    def get():
        while not pfree:
            yield
        return pfree.pop(0)

    def rel(b):
        pfree.append(b)

    T = {}

    def tb(name, shape=(64, 64), n=2, dt=F32):
        if name not in T:
            T[name] = ([P.sb("r_" + name, list(shape), dt) for _ in range(n)], [0])
        bufs, k = T[name]
        k[0] += 1
        return bufs[k[0] % len(bufs)]

    lraw = P.sb("lraw", [128, SEG + 1], F32)
    ldif = P.sb("ldif", [128, SEG], F32)
    twd = P.sb("twd", [128, SEG], F32)
    adp = P.sb("adp", [128, SEG], F32)
    sgd = P.sb("sgd", [128, 4, SEG], F32)
    hraw = P.sb("hraw", [64, SEG + 1], F32)
    hdif = P.sb("hdif", [64, SEG], F32)
    t1 = P.sb("t1", [64, SEG], F32)
    HB = []
    for h in range(NH):
        d = {}
        for nm in ("rS", "kS", "vS", "lwS", "aS", "gS", "kkS", "bS", "yS", "bon"):
            d[nm] = P.sb(nm + str(h), [64, SEG], F32)
        d["ob"] = P.sb("rob" + str(h), [64, SEG], BF16)
        d["H"] = P.sb("rH" + str(h), [64, 64], F32)
        P.memset("dve", d["H"], d["H"][:, :], 0.0)
        HB.append(d)
    I64, Ui64, Us64 = ident[0:64, 0:64], Ui[0:64, 0:64], Us[0:64, 0:64]

    def shift_load(dst_raw, rows, row0, t0, g):
        if g == 0:
            P.memset("pool", dst_raw, dst_raw[0:rows, 0:1], 0.0)
            P.dma("sp", dst_raw, dst_raw[0:rows, 1:SEG + 1], proj, proj[row0:row0 + rows, 0:SEG])
        else:
            P.dma("sp", dst_raw, dst_raw[0:rows, :], proj, proj[row0:row0 + rows, t0 - 1:t0 + SEG])

    res = {}

    def chunkA(h, ci):
        hb = HB[h]
        rS, kS, vS, lwS, aS, bS = hb["rS"], hb["kS"], hb["vS"], hb["lwS"], hb["aS"], hb["bS"]
        sfx = str(h)
        csl = slice(ci * C, ci * C + C)
        pl = yield from get()
        P.tr(pl, pl[0:64, 0:64], lwS, lwS[:, csl], ident, I64)
        yield
        lwt = tb("lwt" + sfx)
        P.copy("act", lwt, lwt[:, :], pl, pl[0:64, 0:64])
        rel(pl)
        yield
        pL = yield from get()
        P.mm(pL, pL[0:64, 0:64], lwt, lwt[:, :], Ui, Ui64)
        yield
        Lsb = tb("Lsb" + sfx)
        P.copy("act", Lsb, Lsb[:, :], pL, pL[0:64, 0:64])
        rel(pL)
        yield
        eLi = tb("eLi" + sfx)
        P.act(eLi, eLi[:, :], Lsb, Lsb[:, :], AF.Exp)
        eLx = tb("eLx" + sfx)
        P.tt("dve", eLx, eLx[:, :], Lsb, Lsb[:, :], lwS, lwS[:, csl], ALU.subtract)
        yield
        eLn = tb("eLn" + sfx)
        P.act(eLn, eLn[:, :], Lsb, Lsb[:, :], AF.Exp, scale=-1.0)
        yield
        P.act(eLx, eLx[:, :], eLx, eLx[:, :], AF.Exp)
        AR = tb("AR" + sfx, (64, 128))
        P.tt("pool", AR, AR[:, 64:128], rS, rS[:, csl], eLi, eLi[:, :], ALU.mult)
        yield
        bt = tb("bt" + sfx)
        P.tt("dve", bt, bt[:, :], bS, bS[:, csl], eLn, eLn[:, :], ALU.mult)
        kt_ = tb("kt" + sfx)
        P.tt("pool", kt_, kt_[:, :], kS, kS[:, csl], eLn, eLn[:, :], ALU.mult)
        yield
        P.tt("dve", AR, AR[:, 0:64], aS, aS[:, csl], eLx, eLx[:, :], ALU.mult)
        yield
        tok = {}
        for nm, (src, sap) in {"V": (vS, vS[:, csl]), "B": (bt, bt[:, :]), "K": (kt_, kt_[:, :]),
                               "A": (AR, AR[:, 0:64])}.items():
            pt = yield from get()
            P.tr(pt, pt[0:64, 0:64], src, sap, ident, I64)
            yield
            d = tb("tok" + nm + sfx)
            P.copy("act" if nm in ("B", "V") else "dve", d, d[:, :], pt, pt[0:64, 0:64])
            rel(pt)
            tok[nm] = d
            yield
        pAB = yield from get()
        P.mm(pAB, pAB[0:64, 0:128], bt, bt[:, :], AR, AR[:, :])
        yield
        Pm = tb("Pm" + sfx, n=3)
        P.tt("dve", Pm, Pm[:, :], pAB, pAB[0:64, 0:64], Us, Us64, ALU.mult)
        ArbT = tb("ArbT" + sfx)
        P.tt("dve", ArbT, ArbT[:, :], pAB, pAB[0:64, 64:128], Ui, Ui64, ALU.mult)
        rel(pAB)
        yield
        pAK = yield from get()
        P.mm(pAK, pAK[0:64, 0:128], kt_, kt_[:, :], AR, AR[:, :])
        yield
        AakT = tb("AakT" + sfx)
        P.tt("dve", AakT, AakT[:, :], pAK, pAK[0:64, 0:64], Us, Us64, ALU.mult)
        ArkT = tb("ArkT" + sfx)
        P.tt("dve", ArkT, ArkT[:, :], pAK, pAK[0:64, 64:128], Ui, Ui64, ALU.mult)
        rel(pAK)
        yield
        pq = yield from get()
        P.tr(pq, pq[0:64, 0:64], Pm, Pm[:, :], ident, I64)
        yield
        Qm = tb("Qm" + sfx, n=3)
        P.copy("act", Qm, Qm[:, :], pq, pq[0:64, 0:64])
        rel(pq)
        R = tb("R" + sfx, n=3)
        P.tt("pool", R, R[:, :], Pm, Pm[:, :], ident, I64, ALU.add)
        yield
        pX = yield from get()
        P.mm(pX, pX[0:64, 0:64], AakT, AakT[:, :], tok["V"], tok["V"][:, :])
        yield
        X = tb("X" + sfx)
        P.copy("act", X, X[:, :], pX, pX[0:64, 0:64])
        rel(pX)
        yield
        for k in range(1, 6):
            pQ = yield from get()
            P.mm(pQ, pQ[0:64, 0:64], Pm, Pm[:, :], Qm, Qm[:, :])
            yield
            Qn = tb("Qm" + sfx, n=3)
            P.copy("dve", Qn, Qn[:, :], pQ, pQ[0:64, 0:64])
            rel(pQ)
            yield
            if k < 5:
                pP = yield from get()
                P.mm(pP, pP[0:64, 0:64], Qm, Qm[:, :], Pm, Pm[:, :])
                yield
                Pn = tb("Pm" + sfx, n=3)
                P.copy("act", Pn, Pn[:, :], pP, pP[0:64, 0:64])
                rel(pP)
                yield
            pR = yield from get()
            P.mm(pR, pR[0:64, 0:64], Qn, Qn[:, :], R, R[:, :])
            yield
            Rn = tb("R" + sfx, n=3)
            P.tt("dve", Rn, Rn[:, :], R, R[:, :], pR, pR[0:64, 0:64], ALU.add)
            rel(pR)
            R, Qm = Rn, Qn
            if k < 5:
                Pm = Pn
            yield
        pW = yield from get()
        P.mm(pW, pW[0:64, 0:64], tok["A"], tok["A"][:, :], R, R[:, :])
        yield
        WdT = tb("WdT" + sfx)
        P.copy("act", WdT, WdT[:, :], pW, pW[0:64, 0:64])
        rel(pW)
        yield
        pU0 = yield from get()
        P.mm(pU0, pU0[0:64, 0:64], R, R[:, :], X, X[:, :])
        yield
        U0 = tb("U0" + sfx)
        P.copy("dve", U0, U0[:, :], pU0, pU0[0:64, 0:64])
        rel(pU0)
        res[(h, ci)] = dict(WdT=WdT, U0=U0, AR=AR, ArbT=ArbT, ArkT=ArkT, tok=tok, eLi=eLi)
        yield

    def chunkB(h, ci):
        hb = HB[h]
        H, yS = hb["H"], hb["yS"]
        r = res.pop((h, ci))
        WdT, U0, AR, ArbT, ArkT, tok, eLi = r["WdT"], r["U0"], r["AR"], r["ArbT"], r["ArkT"], r["tok"], r["eLi"]
        sfx = str(h)
        csl = slice(ci * C, ci * C + C)
        pU = yield from get()
        P.mm(pU, pU[0:64, 0:64], WdT, WdT[:, :], H, H[:, :])
        yield
        U = tb("U" + sfx)
        P.tt("dve", U, U[:, :], U0, U0[:, :], pU, pU[0:64, 0:64], ALU.add)
        rel(pU)
        yield
        pH = yield from get()
        P.mm(pH, pH[0:64, 0:64], tok["B"], tok["B"][:, :], U, U[:, :], start=True, stop=False)
        P.mm(pH, pH[0:64, 0:64], tok["K"], tok["K"][:, :], tok["V"], tok["V"][:, :], start=False, stop=True)
        pY = yield from get()
        P.mm(pY, pY[0:64, 0:64], H, H[:, :], AR, AR[:, 64:128], start=True, stop=False)
        P.mm(pY, pY[0:64, 0:64], U, U[:, :], ArbT, ArbT[:, :], start=False, stop=False)
        P.mm(pY, pY[0:64, 0:64], tok["V"], tok["V"][:, :], ArkT, ArkT[:, :], start=False, stop=True)
        yield
        P.ts("dve", H, H[:, :], H, H[:, :], eLi[:, 63:64], None, ALU.mult, extra_reads=[eLi])
        P.stt("dve", H, H[:, :], pH, pH[0:64, 0:64], eLi[:, 63:64], H, H[:, :], ALU.mult, ALU.add, extra_reads=[eLi])
        P.copy("act", yS, yS[:, csl], pY, pY[0:64, 0:64])
        rel(pH)
        rel(pY)
        yield

    for g in range(NSEG):
        t0 = g * SEG
        lt = [(cfg.o_l, 128, 0), (cfg.o_l + 128, 128, 1)] + \
             [(cfg.o_l + 256 + 128 * i, min(128, cfg.LG - 128 * i), 2 + i) for i in range(4)]
        for (row0, rows, ti) in lt:
            shift_load(lraw, rows, row0, t0, g)
            P.tt("dve", ldif, ldif[0:rows, :], lraw, lraw[0:rows, 0:SEG], lraw, lraw[0:rows, 1:SEG + 1], ALU.subtract)
            P.stt("dve", ldif, ldif[0:rows, :], ldif, ldif[0:rows, :], rmul[0:rows, ti:ti + 1], lraw, lraw[0:rows, 1:SEG + 1],
                  ALU.mult, ALU.add, extra_reads=[rmul])
            if ti == 0:
                P.act(twd, twd[:, :], ldif, ldif[:, :], AF.Tanh)
            elif ti == 1:
                P.copy("pool", adp, adp[:, :], ldif, ldif[:, :])
            else:
                if rows < 128:
                    P.memset("pool", sgd, sgd[:, ti - 2, :], 0.0)
                P.act(sgd, sgd[0:rows, ti - 2, :], ldif, ldif[0:rows, :], AF.Sigmoid)
        for h in range(NH):
            hb = HB[h]
            rS, kS, vS, lwS, aS, gS, kkS, bS, bon = (hb[n_] for n_ in ("rS", "kS", "vS", "lwS", "aS", "gS", "kkS", "bS", "bon"))
            vb_ = h * 10
            hc = slice(64 * h, 64 * h + 64)
            for bi_, dst in enumerate((rS, kS, vS)):
                shift_load(hraw, 64, cfg.o_r + cfg.RC * bi_ + 64 * h, t0, g)
                P.tt("dve", hdif, hdif[:, :], hraw, hraw[:, 0:SEG], hraw, hraw[:, 1:SEG + 1], ALU.subtract)
                P.stt("dve", dst, dst[:, :], hdif, hdif[:, :], rvec[:, vb_ + bi_:vb_ + bi_ + 1], hraw, hraw[:, 1:SEG + 1],
                      ALU.mult, ALU.add, extra_reads=[rvec])
            for b0 in range(0, SEG, BW):
                bs = slice(b0, b0 + BW)
                pw = pp()
                P.mm(pw, pw[0:64, 0:BW], wup, wup[:, hc], twd, twd[:, bs])
                P.act(lwS, lwS[:, bs], pw, pw[0:64, 0:BW], AF.Sigmoid, bias=rvec[:, vb_ + 3:vb_ + 4], extra_reads=[rvec])
                pa = pp()
                P.mm(pa, pa[0:64, 0:BW], aup, aup[:, hc], adp, adp[:, bs])
                P.act(aS, aS[:, bs], pa, pa[0:64, 0:BW], AF.Sigmoid, bias=rvec[:, vb_ + 4:vb_ + 5], extra_reads=[rvec])
                pg = pp()
                for kt in range(4):
                    P.mm(pg, pg[0:64, 0:BW], gup, gup[:, kt, hc], sgd, sgd[:, kt, bs], start=(kt == 0), stop=(kt == 3))
                P.copy("act", gS, gS[:, bs], pg, pg[0:64, 0:BW])
            P.ts("dve", lwS, lwS[:, :], lwS, lwS[:, :], -math.exp(-0.5), None, ALU.mult)
            P.ts("dve", kkS, kkS[:, :], kS, kS[:, :], rvec[:, vb_ + 5:vb_ + 6], None, ALU.mult, extra_reads=[rvec])
            P.act(t1, t1[:, :], kkS, kkS[:, :], AF.Square)
            for b0 in range(0, SEG, BW):
                bs = slice(b0, b0 + BW)
                pn = pp()
                P.mm(pn, pn[0:64, 0:BW], ones, ones[:, :], t1, t1[:, bs])
                rs = tb("rs", (64, BW))
                P.act(rs, rs[:, :], pn, pn[0:64, 0:BW], AF.Sqrt, bias=eps12[:, 0:1], extra_reads=[eps12])
                P.op("dve", lambda en, rs=rs: en.reciprocal(out=rs[:, :], in_=rs[:, :]), reads=[rs], writes=[rs])
                P.tt("dve", kkS, kkS[:, bs], kkS, kkS[:, bs], rs, rs[:, :], ALU.mult)
            P.ts("dve", t1, t1[:, :], aS, aS[:, :], rvec[:, vb_ + 6:vb_ + 7], omk[:, h:h + 1], ALU.mult, ALU.add,
                 extra_reads=[rvec, omk])
            P.tt("dve", kS, kS[:, :], kS, kS[:, :], t1, t1[:, :], ALU.mult)
            P.tt("pool", bS, bS[:, :], kkS, kkS[:, :], aS, aS[:, :], ALU.mult)
            P.ts("pool", aS, aS[:, :], kkS, kkS[:, :], -1.0, None, ALU.mult)
            P.stt("dve", t1, t1[:, :], rS, rS[:, :], rvec[:, vb_ + 7:vb_ + 8], kS, kS[:, :], ALU.mult, ALU.mult,
                  extra_reads=[rvec])
            for b0 in range(0, SEG, BW):
                bs = slice(b0, b0 + BW)
                pn = pp()
                P.mm(pn, pn[0:64, 0:BW], ones, ones[:, :], t1, t1[:, bs])
                P.tt("dve", bon, bon[:, bs], pn, pn[0:64, 0:BW], vS, vS[:, bs], ALU.mult)
        NCH = SEG // C
        for ci in range(NCH + 1):
            gens = []
            for h in range(NH):
                if ci >= 1:
                    gens.append(chunkB(h, ci - 1))
            for h in range(NH):
                if ci < NCH:
                    gens.append(chunkA(h, ci))
            run_rr(gens)
        for h in range(NH):
            hb = HB[h]
            yS, gS, bon, obuf = hb["yS"], hb["gS"], hb["bon"], hb["ob"]
            vb_ = h * 10
            for b0 in range(0, SEG, BW):
                bs = slice(b0, b0 + BW)
                pm = pp()
                P.mm(pm, pm[0:64, 0:BW], ones, ones[:, :], yS, yS[:, bs])
                P.stt("dve", t1, t1[:, bs], pm, pm[0:64, 0:BW], -1.0 / 64, yS, yS[:, bs], ALU.mult, ALU.add)
                sq = tb("sq", (64, BW))
                P.act(sq, sq[:, :], t1, t1[:, bs], AF.Square)
                pv_ = pp()
                P.mm(pv_, pv_[0:64, 0:BW], ones, ones[:, :], sq, sq[:, :])
                rs = tb("rs", (64, BW))
                P.act(rs, rs[:, :], pv_, pv_[0:64, 0:BW], AF.Sqrt, bias=epsgn[:, 0:1], scale=1.0 / 64, extra_reads=[epsgn])
                P.op("dve", lambda en, rs=rs: en.reciprocal(out=rs[:, :], in_=rs[:, :]), reads=[rs], writes=[rs])
                P.tt("dve", t1, t1[:, bs], t1, t1[:, bs], rs, rs[:, :], ALU.mult)
                P.ts("dve", t1, t1[:, bs], t1, t1[:, bs], rvec[:, vb_ + 8:vb_ + 9], rvec[:, vb_ + 9:vb_ + 10], ALU.mult, ALU.add,
                     extra_reads=[rvec])
                P.tt("pool", t1, t1[:, bs], t1, t1[:, bs], bon, bon[:, bs], ALU.add)
                P.tt("pool", obuf, obuf[:, bs], t1, t1[:, bs], gS, gS[:, bs], ALU.mult)
            P.dma("pool", mix, mix[384 + 64 * h:384 + 64 * h + 64, t0:t0 + SEG], obuf, obuf[:, :])
    P.pop()


def emit_swa(P, cfg, proj, mix, bm):
    S = cfg.S
    SEG = 2048
    NSEG = S // SEG
    scale = 128 ** -0.5
    qr, kr, vr = cfg.o_a, cfg.o_a + 128, cfg.o_a + 256
    P.push()
    ident = P.sb("ident", [128, 128], F32)
    make_identity(P, ident)
    ones_c = P.sb("ones_c", [128, 1], F32)
    P.memset("dve", ones_c, ones_c[:, :], 1.0)
    ones_r = P.sb("ones_r", [1, 128], F32)
    P.memset("dve", ones_r, ones_r[:, :], 1.0)
    ones_b = P.sb("ones_b", [128, 128], BF16)
    P.memset("dve", ones_b, ones_b[:, :], 1.0)
    bms = P.sb("bms", [128, 6, 128], F32)
    for p in range(3):
        for t in range(2):
            P.dma("sp", bms, bms[:, p * 2 + t, :], bm, bm[p, t, :, :])
    mx = P.sb("mx", [1, 2], F32)
    P.memset("dve", mx, mx[:, :], 0.0)
    P.push()
    ld = [P.sb("ld", [128, 512], F32) for _ in range(2)]
    sq = [P.sb("sq", [128, 512], F32) for _ in range(2)]
    nps = [P.ps("nps", [1, 512], F32) for _ in range(2)]
    bmx = P.sb("bmx", [1, 1], F32)
    it = 0
    for which, row in enumerate((qr, kr)):
        for t0 in range(0, S, 512):
            a, q, pb = ld[it % 2], sq[it % 2], nps[it % 2]
            it += 1
            P.dma("sp", a, a[:, :], proj, proj[row:row + 128, t0:t0 + 512])
            P.act(q, q[:, :], a, a[:, :], AF.Square)
            P.mm(pb, pb[:, :], ones_c, ones_c[:, :], q, q[:, :])
            P.op("dve", lambda en, pb=pb: en.reduce_max(out=bmx[:, :], in_=pb[:, :], axis=AX.X), reads=[pb], writes=[bmx])
            P.tt("dve", mx, mx[:, which:which + 1], mx, mx[:, which:which + 1], bmx, bmx[:, :], ALU.max)
    P.pop()
    negm1 = P.sb("negm1", [1, 1], F32)
    P.tt("dve", negm1, negm1[:, :], mx, mx[:, 0:1], mx, mx[:, 1:2], ALU.add)
    P.ts("dve", negm1, negm1[:, :], negm1, negm1[:, :], -0.5 * scale, None, ALU.mult)
    negm = P.sb("negm", [128, 1], F32)
    P.push()
    pb = P.ps("nmps", [128, 1], F32)
    P.mm(pb, pb[:, :], ones_r, ones_r[:, :], negm1, negm1[:, :])
    P.copy("dve", negm, negm[:, :], pb, pb[:, :])
    P.pop()
    qT = P.sb("qT", [128, SEG], BF16)
    kT = [P.sb("kT", [128, SEG], BF16) for _ in range(2)]
    vT = [P.sb("vT", [128, SEG], F32) for _ in range(2)]
    ldq = P.sb("ldq", [128, SEG], F32)
    NUM = P.sb("NUM", [128, SEG], F32)
    DEN = P.sb("DEN", [128, SEG], F32)
    ob = P.sb("ob", [128, SEG], BF16)
    Vt = [P.sb("Vt", [128, 128], BF16) for _ in range(3)]
    tmp = [P.sb("tmp", [128, 128], F32) for _ in range(3)]
    pT = [P.sb("pT", [128, 128], BF16) for _ in range(3)]
    trp = [P.ps("trp", [128, 128], F32) for _ in range(2)]
    sp_ = [P.ps("sps", [128, 128], F32) for _ in range(2)]
    nump = [P.ps("nump", [128, 128], F32) for _ in range(2)]
    denp = [P.ps("denp", [128, 128], F32) for _ in range(2)]
    vi = 0
    bi = 0
    for g in range(NSEG):
        t0 = g * SEG
        cur = g % 2
        P.dma("sp", ldq, ldq[:, :], proj, proj[qr:qr + 128, t0:t0 + SEG])
        P.act(qT, qT[:, :], ldq, ldq[:, :], AF.Copy, scale=scale)
        P.dma("sp", ldq, ldq[:, :], proj, proj[kr:kr + 128, t0:t0 + SEG])
        P.copy("pool", kT[cur], kT[cur][:, :], ldq, ldq[:, :])
        P.dma("sp", vT[cur], vT[cur][:, :], proj, proj[vr:vr + 128, t0:t0 + SEG])
        P.memset("pool", NUM, NUM[:, :], 0.0)
        P.memset("pool", DEN, DEN[:, :], 0.0)
        for p, d in enumerate(SWA_DIL):
            span = 128 * d
            for nbl in range(SEG // span):
                for r in range(d):
                    base = nbl * span + r
                    qsl = slice(base, base + 127 * d + 1, d)
                    tiles = []
                    if nbl >= 1:
                        tiles.append((0, cur, slice(base - span, base - span + 127 * d + 1, d)))
                    elif g >= 1:
                        pbase = SEG - span + r
                        tiles.append((0, 1 - cur, slice(pbase, pbase + 127 * d + 1, d)))
                    tiles.append((1, cur, qsl))
                    np_, dp_ = nump[bi % 2], denp[bi % 2]
                    bi += 1
                    for ti, (tt_, ring, ksl) in enumerate(tiles):
                        tp, sps, V, tm, pt = trp[vi % 2], sp_[vi % 2], Vt[vi % 3], tmp[vi % 3], pT[vi % 3]
                        vi += 1
                        P.tr(tp, tp[:, :], vT[ring], vT[ring][:, ksl], ident, ident[:, :])
                        P.copy("pool", V, V[:, :], tp, tp[:, :]) if False else P.copy("act", V, V[:, :], tp, tp[:, :])
                        P.mm(sps, sps[:, :], kT[ring], kT[ring][:, ksl], qT, qT[:, qsl])
                        P.tt("dve", tm, tm[:, :], sps, sps[:, :], bms, bms[:, p * 2 + tt_, :], ALU.add)
                        P.act(pt, pt[:, :], tm, tm[:, :], AF.Exp, bias=negm[:, 0:1], extra_reads=[negm])
                        first, last = ti == 0, ti == len(tiles) - 1
                        P.mm(np_, np_[:, :], V, V[:, :], pt, pt[:, :], start=first, stop=last)
                        P.mm(dp_, dp_[:, :], ones_b, ones_b[:, :], pt, pt[:, :], start=first, stop=last)
                    P.tt("dve", NUM, NUM[:, qsl], NUM, NUM[:, qsl], np_, np_[:, :], ALU.add)
                    P.tt("dve", DEN, DEN[:, qsl], DEN, DEN[:, qsl], dp_, dp_[:, :], ALU.add)
        P.op("dve", lambda en: en.reciprocal(out=DEN[:, :], in_=DEN[:, :]), reads=[DEN], writes=[DEN])
        P.tt("dve", ob, ob[:, :], NUM, NUM[:, :], DEN, DEN[:, :], ALU.mult)
        P.dma("pool", mix, mix[256:384, t0:t0 + SEG], ob, ob[:, :])
    P.pop()


def make_identity(P, ident):
    P.dma("sp", ident, ident[:, :], P.cst, P.cst[0, :, :])


def host_consts():
    c = np.zeros((4, 128, 128), np.float32)
    j = np.arange(128)[:, None]
    i = np.arange(128)[None, :]
    c[0] = (i == j)
    c[1] = (j <= i)
    c[2] = (j < i)
    c[3] = ((i // 64) == (j // 64))
    return c
```

```python
import math
from contextlib import ExitStack

import numpy as np
import ml_dtypes

import concourse.bass as bass
import concourse.mybir as mybir
from concourse.bass_utils import run_bass_kernel_spmd

F32 = mybir.dt.float32
BF16 = mybir.dt.bfloat16
AF = mybir.ActivationFunctionType
ALU = mybir.AluOpType
AX = mybir.AxisListType
NPBF = ml_dtypes.bfloat16


class Cfg:
    def __init__(self, D=4096, S=16384, NR=8, DFF=11008):
        self.D, self.S, self.NR, self.DFF = D, S, NR, DFF
        self.GD, self.GW = 128, 3 * D // 8
        self.GH = self.GW // 128
        self.AD, self.AW = 128, D // 4
        self.AH = self.AW // 128
        self.RD = 64
        self.RW = D - self.GW - self.AW
        self.RH = self.RW // 64
        self.LW, self.LA, self.LG = 128, 128, 480
        self.GDN_IN = 4 * self.GW + 2 * self.GH
        self.SWA_IN = 3 * self.AW
        self.RWKV_IN = 3 * self.RW + self.LW + self.LA + self.LG
        self.IN_W = self.GDN_IN + self.SWA_IN + self.RWKV_IN
        self.GS = -(-self.GH // NR)
        assert self.AH == NR and self.RH == 3 * NR and self.GS == 2
        self.DC = D // NR
        self.FC = DFF // NR
        self.RC = 3 * 64
        self.MIXC = self.GS * 128 + 128 + self.RC
        self.o_g = 0
        self.o_a = 512 * self.GS
        self.o_r = self.o_a + 384
        self.o_l = self.o_r + 3 * self.RC
        self.o_s = self.o_l + self.LW + self.LA + self.LG
        self.PC = self.o_s + 2 * self.GS


class Sem:
    def __init__(self, h, name):
        self.h, self.name, self.cnt = h, name, 0


class Buf:
    def __init__(self, name, t, space):
        self.name, self.t, self.space = name, t, space
        self.last_w = None
        self.readers = {}
        self.dsem = None
        self.last_w_dma = False

    def __getitem__(self, idx):
        return self.t[idx]

    def ap(self):
        return self.t.ap() if hasattr(self.t, "ap") else self.t[:]


class Prog:
    def __init__(self, nc):
        self.nc = nc
        self.root = ExitStack()
        self.stacks = [self.root]
        self.eng = {"pe": nc.tensor, "act": nc.scalar, "dve": nc.vector, "pool": nc.gpsimd, "sp": nc.sync}
        self.esem = {}
        for e in ("pe", "act", "dve", "pool"):
            self.esem[e] = Sem(self.root.enter_context(nc.semaphore("es_" + e)), e)
        self.seen = {e: {} for e in self.eng}
        self.free_dsems = []
        self.all_dsems = []
        self.scope_dsems = [[]]
        self.uid = 0
        self.pe_pending = False

    def _nm(self, name):
        self.uid += 1
        return f"{name}_{self.uid}"

    def sb(self, name, shape, dtype=F32):
        t = self.stacks[-1].enter_context(self.nc.sbuf_tensor(self._nm(name), list(shape), dtype))
        return Buf(name, t, "sb")

    def ps(self, name, shape, dtype=F32):
        t = self.stacks[-1].enter_context(self.nc.psum_tensor(self._nm(name), list(shape), dtype))
        return Buf(name, t, "ps")

    def dram(self, name, shape, dtype=F32, kind="Internal"):
        t = self.nc.dram_tensor(name, list(shape), dtype, kind=kind)
        return Buf(name, t, "dram")

    def _get_dsem(self, buf):
        if buf.dsem is None:
            if self.free_dsems:
                s = self.free_dsems.pop()
            else:
                s = Sem(self.root.enter_context(self.nc.semaphore(self._nm("ds"))), "ds")
                self.all_dsems.append(s)
            buf.dsem = s
            if buf.space != "dram":
                self.scope_dsems[-1].append(s)
        return buf.dsem

    def push(self):
        st = ExitStack()
        self.stacks.append(st)
        self.scope_dsems.append([])

    def pop(self):
        self.barrier()
        self.stacks.pop().close()
        self.free_dsems.extend(self.scope_dsems.pop())

    def barrier(self):
        toks = [(s, s.cnt) for s in self.esem.values() if s.cnt > 0]
        toks += [(s, 16 * s.cnt) for s in self.all_dsems if s.cnt > 0]
        for e in self.eng:
            self._wait(e, toks)

    def _wait(self, e, toks):
        seen = self.seen[e]
        own = self.esem.get(e)
        for s, v in toks:
            if e == "pe" and s is own:
                continue
            if seen.get(s, 0) < v:
                self.eng[e].wait_ge(s.h, v)
                seen[s] = v

    def _deps(self, reads, writes, dma_dst=None):
        toks = []
        for b in reads:
            if b.last_w is not None:
                toks.append(b.last_w)
        for b in writes:
            if b.last_w is not None and not (b is dma_dst and b.last_w_dma):
                toks.append(b.last_w)
            toks.extend(b.readers.items())
        return toks

    def _commit(self, tok, reads, writes, is_dma=False):
        for b in writes:
            b.last_w = tok
            b.last_w_dma = is_dma
            b.readers = {}
        for b in reads:
            s, v = tok
            if b.readers.get(s, 0) < v:
                b.readers[s] = v

    def op(self, e, fn, reads=(), writes=(), inc=True):
        self._wait(e, self._deps(reads, writes))
        ins = fn(self.eng[e])
        s = self.esem[e]
        if inc:
            s.cnt += 1
            ins.then_inc(s.h, 1)
            tok = (s, s.cnt)
        else:
            tok = (s, s.cnt + 1)
        self._commit(tok, reads, writes)
        return ins

    def dma(self, q, dst, dst_ap, src, src_ap):
        self._wait(q, self._deps([src], [dst], dma_dst=dst))
        s = self._get_dsem(dst)
        ins = self.eng[q].dma_start(out=dst_ap, in_=src_ap)
        s.cnt += 1
        ins.then_inc(s.h, 16)
        self._commit((s, 16 * s.cnt), [src], [dst], is_dma=True)
        return ins

    def finish(self, out_bufs):
        toks = [b.last_w for b in out_bufs if b.last_w is not None]
        self._wait("sp", toks)
        self.barrier()
        while len(self.stacks) > 1:
            self.stacks.pop().close()
        self.root.close()

    def mm(self, ps, ps_ap, a, a_ap, b, b_ap, start=True, stop=True, inc=None):
        if inc is None:
            inc = stop
        return self.op("pe", lambda e: e.matmul(ps_ap, a_ap, b_ap, start=start, stop=stop),
                       reads=[a, b], writes=[ps], inc=inc)

    def tr(self, ps, ps_ap, a, a_ap, ident, ident_ap):
        return self.op("pe", lambda e: e.transpose(ps_ap, a_ap, ident_ap), reads=[a, ident], writes=[ps])

    def act(self, out, out_ap, in_, in_ap, func, bias=None, scale=1.0, extra_reads=(), e="act", accum=None):
        kw = {}
        if bias is not None:
            kw["bias"] = bias
        if accum is not None:
            kw["accum_out"] = accum[1]
        w = [out] + ([accum[0]] if accum is not None else [])
        return self.op(e, lambda en: en.activation(out=out_ap, in_=in_ap, func=func, scale=scale, **kw),
                       reads=[in_] + list(extra_reads), writes=w)

    def ts(self, e, out, out_ap, in_, in_ap, s1, s2, op0, op1=None, extra_reads=()):
        if op1 is None:
            f = lambda en: en.tensor_scalar(out=out_ap, in0=in_ap, scalar1=s1, scalar2=None, op0=op0)
        else:
            f = lambda en: en.tensor_scalar(out=out_ap, in0=in_ap, scalar1=s1, scalar2=s2, op0=op0, op1=op1)
        return self.op(e, f, reads=[in_] + list(extra_reads), writes=[out])

    def tt(self, e, out, out_ap, a, a_ap, b, b_ap, op):
        return self.op(e, lambda en: en.tensor_tensor(out=out_ap, in0=a_ap, in1=b_ap, op=op),
                       reads=[a, b], writes=[out])

    def stt(self, e, out, out_ap, a, a_ap, scalar, b, b_ap, op0, op1, extra_reads=()):
        return self.op(e, lambda en: en.scalar_tensor_tensor(out=out_ap, in0=a_ap, scalar=scalar, in1=b_ap,
                                                              op0=op0, op1=op1),
                       reads=[a, b] + list(extra_reads), writes=[out])

    def copy(self, e, out, out_ap, in_, in_ap):
        if e == "act":
            return self.op(e, lambda en: en.copy(out=out_ap, in_=in_ap), reads=[in_], writes=[out])
        return self.op(e, lambda en: en.tensor_copy(out=out_ap, in_=in_ap), reads=[in_], writes=[out])

    def memset(self, e, out, out_ap, val):
        return self.op(e, lambda en: en.memset(out_ap, val), reads=[], writes=[out])


def ktiles(K):
    return [(k0, min(128, K - k0)) for k0 in range(0, K, 128)]


def linear(P, xT, K, S, w, Mc, m_tiles, TB, GC, epilogue, pre_group=None, n_ps=2, ps_bufs=None):
    kts = ktiles(K)
    KT = len(kts)
    groups, cur, cw = [], [], 0
    for mi, (m0, msz) in enumerate(m_tiles):
        if cw + msz > GC and cur:
            groups.append(cur)
            cur, cw = [], 0
        cur.append((mi, m0, msz))
        cw += msz
    if cur:
        groups.append(cur)
    P.push()
    wbf = P.sb("wbf", [128, KT, GC], BF16)
    KC = 4
    wst = [P.sb("wst", [128, KC, GC], F32) for _ in range(2)]
    xb = [P.sb("xblk", [128, KT, TB], BF16) for _ in range(2)]
    if ps_bufs is None:
        ps_bufs = [P.ps("linps", [128, 512], F32) for _ in range(n_ps)]
    xap = xT.ap()
    wap = w.ap()
    nblk = S // TB
    it = 0
    pi = 0
    for grp in groups:
        g0 = grp[0][1]
        gw = sum(g[2] for g in grp)
        ci = 0
        for kc0 in range(0, KT, KC):
            kc1 = min(KT, kc0 + KC)
            st = wst[ci % 2]
            for kt in range(kc0, kc1):
                k0, ksz = kts[kt]
                P.dma("sp", st, st[0:ksz, kt - kc0, 0:gw], w, wap[k0:k0 + ksz, g0:g0 + gw])
            full = [kt for kt in range(kc0, kc1) if kts[kt][1] == 128]
            part = [kt for kt in range(kc0, kc1) if kts[kt][1] != 128]
            ce = "dve" if ci % 2 == 0 else "pool"
            if full:
                a, b = full[0], full[-1] + 1
                P.copy(ce, wbf, wbf[:, a:b, 0:gw], st, st[:, a - kc0:b - kc0, 0:gw])
            for kt in part:
                ksz = kts[kt][1]
                P.copy(ce, wbf, wbf[0:ksz, kt, 0:gw], st, st[0:ksz, kt - kc0, 0:gw])
            ci += 1
        if pre_group is not None:
            pre_group(grp)
        for bi in range(nblk):
            t0 = bi * TB
            x = xb[it % 2]
            it += 1
            for kt, (k0, ksz) in enumerate(kts):
                pass
            nfull = sum(1 for k in kts if k[1] == 128)
            for ka in range(0, nfull, 8):
                kb = min(nfull, ka + 8)
                P.dma("sp", x, x[:, ka:kb, :], xT,
                      xap[ka * 128:kb * 128, t0:t0 + TB].rearrange("(kt p) s -> p kt s", p=128))
            if nfull < KT:
                k0, ksz = kts[-1]
                P.dma("sp", x, x[0:ksz, KT - 1, :], xT, xap[k0:k0 + ksz, t0:t0 + TB])
            for (mi, m0, msz) in grp:
                pb = ps_bufs[pi % len(ps_bufs)]
                pi += 1
                for kt, (k0, ksz) in enumerate(kts):
                    P.mm(pb, pb[0:msz, 0:TB], wbf, wbf[0:ksz, kt, m0 - g0:m0 - g0 + msz], x, x[0:ksz, kt, :],
                         start=(kt == 0), stop=(kt == KT - 1))
                epilogue(mi, (m0, msz), t0, pb)
    P.pop()


def load_rstd(P, part, NR, S, D, eps):
    rstd = P.sb("rstd", [128, S], F32)
    P.push()
    ones = P.sb("ones", [NR, 128], F32)
    P.memset("dve", ones, ones[:, :], 1.0)
    pt = P.sb("part", [NR, S], F32)
    P.dma("sp", pt, pt[:, :], part, part.ap())
    ps = [P.ps("rsps", [128, 512], F32) for _ in range(2)]
    for i, t0 in enumerate(range(0, S, 512)):
        pb = ps[i % 2]
        P.mm(pb, pb[:, :], ones, ones[:, :], pt, pt[:, t0:t0 + 512])
        P.act(rstd, rstd[:, t0:t0 + 512], pb, pb[:, :], AF.Sqrt, bias=eps_ap(P, eps), scale=1.0 / D,
              extra_reads=[P.consts["eps"]])
        P.op("dve", lambda en, t0=t0: en.reciprocal(out=rstd[:, t0:t0 + 512], in_=rstd[:, t0:t0 + 512]),
             reads=[rstd], writes=[rstd])
    P.pop()
    return rstd


def eps_ap(P, eps):
    return P.consts["eps"][:, 0:1]


def setup_consts(P, eps):
    P.consts = {}
    e = P.sb("epsc", [128, 1], F32)
    P.memset("dve", e, e[:, :], eps)
    P.consts["eps"] = e


def norm_prep_epilogue(P, cfg, xnew, xn_ap, msz, m0, t0, TB, normw, x_out, xw_out, sq_ps, first, last, part_sb):
    pass


def build_prep(cfg):
    nc = bass.Bass("TRN2", target_bir_lowering=False)
    P = Prog(nc)
    DC, S = cfg.DC, cfg.S
    x = P.dram("x", [DC, S], F32, "ExternalInput")
    nw = P.dram("nw", [128, DC // 128], F32, "ExternalInput")
    xw = P.dram("xw", [DC, S], BF16, "ExternalOutput")
    part = P.dram("part", [1, S], F32, "ExternalOutput")
    emit_norm_prep(P, cfg, x, nw, xw, part)
    P.finish([xw, part])
    return nc


def emit_norm_prep(P, cfg, x, nw, xw, part):
    DC, S = cfg.DC, cfg.S
    MT = DC // 128
    TB = 512
    P.push()
    nws = P.sb("nws", [128, MT], F32)
    P.dma("sp", nws, nws[:, :], nw, nw.ap())
    ones = P.sb("ones1", [128, 1], F32)
    P.memset("dve", ones, ones[:, :], 1.0)
    xs = [P.sb("xs", [128, MT, TB], F32) for _ in range(2)]
    sq = [P.sb("sq", [128, MT, TB], F32) for _ in range(2)]
    xo = [P.sb("xo", [128, MT, TB], BF16) for _ in range(2)]
    po = [P.sb("po", [1, TB], F32) for _ in range(2)]
    pss = [P.ps("pss", [1, TB], F32) for _ in range(2)]
    xa = x.ap().rearrange("(m p) s -> p m s", p=128)
    xwa = xw.ap().rearrange("(m p) s -> p m s", p=128)
    for i, t0 in enumerate(range(0, S, TB)):
        a, q, o, pb, pp = xs[i % 2], sq[i % 2], xo[i % 2], pss[i % 2], po[i % 2]
        P.dma("sp", a, a[:, :, :], x, xa[:, :, t0:t0 + TB])
        P.act(q, q[:, :, :], a, a[:, :, :], AF.Square)
        for m in range(MT):
            P.mm(pb, pb[:, :], ones, ones[:, :], q, q[:, m, :], start=(m == 0), stop=(m == MT - 1))
            P.ts("dve" if m % 2 == 0 else "pool", o, o[:, m, :], a, a[:, m, :], nws[:, m:m + 1], None, ALU.mult,
                 extra_reads=[nws])
        P.copy("dve", pp, pp[:, :], pb, pb[:, :])
        P.dma("pool", xw, xwa[:, :, t0:t0 + TB], o, o[:, :, :])
        P.dma("pool", part, part[0:1, t0:t0 + TB], pp, pp[:, :])
    P.pop()


def build_res(cfg, K, final=False):
    nc = bass.Bass("TRN2", target_bir_lowering=False)
    P = Prog(nc)
    DC, S = cfg.DC, cfg.S
    MT = DC // 128
    xin = P.dram("xin", [K, S], BF16, "ExternalInput")
    w = P.dram("w", [K, DC], F32, "ExternalInput")
    x = P.dram("x", [DC, S], F32, "ExternalInput")
    nw = P.dram("nw", [128, DC // 128], F32, "ExternalInput")
    xo = P.dram("xo", [DC, S], F32, "ExternalOutput")
    xw = P.dram("xw", [DC, S], BF16, "ExternalOutput")
    part = P.dram("part", [1, S], F32, "ExternalOutput")
    big = K > 6000
    TB = 256 if big else 512
    GC = 256 if big else 512
    P.push()
    nws = P.sb("nws", [128, MT], F32)
    P.dma("sp", nws, nws[:, :], nw, nw.ap())
    xr = [P.sb("xr", [128, TB], F32) for _ in range(3)]
    xn = [P.sb("xn", [128, TB], F32) for _ in range(3)]
    xb = [P.sb("xb", [128, TB], BF16) for _ in range(3)]
    cnt = [0]

    def epi(mi, mt, t0, pb):
        m0, msz = mt
        i = cnt[0] % 3
        cnt[0] += 1
        a, n, b = xr[i], xn[i], xb[i]
        P.dma("act", a, a[:, :], x, x[m0:m0 + msz, t0:t0 + TB])
        P.tt("dve", n, n[:, :], pb, pb[:, 0:TB], a, a[:, :], ALU.add)
        P.dma("pool", xo, xo[m0:m0 + msz, t0:t0 + TB], n, n[:, :])
        P.ts("pool", b, b[:, :], n, n[:, :], nws[:, mi:mi + 1], None, ALU.mult, extra_reads=[nws])
        P.dma("pool", xw, xw[m0:m0 + msz, t0:t0 + TB], b, b[:, :])

    linear(P, xin, K, S, w, DC, [(m * 128, 128) for m in range(MT)], TB, GC, epi)
    P.pop()
    emit_sumsq(P, cfg, xo, part)
    P.finish([xo, xw, part])
    return nc


def emit_sumsq(P, cfg, x, part):
    DC, S = cfg.DC, cfg.S
    MT = DC // 128
    TB = 512
    P.push()
    ones = P.sb("ones1", [128, 1], F32)
    P.memset("dve", ones, ones[:, :], 1.0)
    xs = [P.sb("xs", [128, MT, TB], F32) for _ in range(2)]
    sq = [P.sb("sq", [128, MT, TB], F32) for _ in range(2)]
    po = [P.sb("po", [1, TB], F32) for _ in range(2)]
    pss = [P.ps("pss", [1, TB], F32) for _ in range(2)]
    xa = x.ap().rearrange("(m p) s -> p m s", p=128)
    for i, t0 in enumerate(range(0, S, TB)):
        a, q, pb, pp = xs[i % 2], sq[i % 2], pss[i % 2], po[i % 2]
        P.dma("sp", a, a[:, :, :], x, xa[:, :, t0:t0 + TB])
        P.act(q, q[:, :, :], a, a[:, :, :], AF.Square)
        for m in range(MT):
            P.mm(pb, pb[:, :], ones, ones[:, :], q, q[:, m, :], start=(m == 0), stop=(m == MT - 1))
        P.copy("dve", pp, pp[:, :], pb, pb[:, :])
        P.dma("pool", part, part[0:1, t0:t0 + TB], pp, pp[:, :])
    P.pop()


def build_ffn(cfg):
    nc = bass.Bass("TRN2", target_bir_lowering=False)
    P = Prog(nc)
    D, S, FC, NR = cfg.D, cfg.S, cfg.FC, cfg.NR
    xin = P.dram("xin", [D, S], BF16, "ExternalInput")
    parts = P.dram("parts", [NR, S], F32, "ExternalInput")
    wg = P.dram("wg", [D, FC], F32, "ExternalInput")
    wu = P.dram("wu", [D, FC], F32, "ExternalInput")
    cw = P.dram("cw", [FC, 4], F32, "ExternalInput")
    out = P.dram("out", [FC, S], BF16, "ExternalOutput")
    setup_consts(P, 1e-6)
    rstd = load_rstd(P, parts, NR, S, D, 1e-6)
    mts = ktiles(FC)
    gsc = P.dram("gsc", [FC, S], F32)
    TB = 512
    P.push()
    cws = P.sb("cws", [128, len(mts), 4], F32)
    for mi, (m0, msz) in enumerate(mts):
        P.dma("sp", cws, cws[0:msz, mi, :], cw, cw[m0:m0 + msz, :])
    gbuf = [P.sb("gbuf", [128, TB + 2], F32) for _ in range(2)]
    acc = [P.sb("acc", [128, TB], F32) for _ in range(2)]
    cnt = [0]
    halo = {}

    def epi_gate(mi, mt, t0, pb):
        m0, msz = mt
        i = cnt[0] % 2
        cnt[0] += 1
        g, a = gbuf[i], acc[i]
        prev = halo.get(mi)
        if t0 == 0:
            P.memset("pool", g, g[:, 0:2], 0.0)
        else:
            pg = prev
            P.copy("pool", g, g[0:msz, 0:2], pg, pg[0:msz, TB:TB + 2])
        P.tt("dve", g, g[0:msz, 2:TB + 2], pb, pb[0:msz, 0:TB], rstd, rstd[0:msz, t0:t0 + TB], ALU.mult)
        halo[mi] = g
        P.ts("dve", a, a[0:msz, :], g, g[0:msz, 0:TB], cws[0:msz, mi, 0:1], cws[0:msz, mi, 3:4], ALU.mult, ALU.add,
             extra_reads=[cws])
        P.stt("dve", a, a[0:msz, :], g, g[0:msz, 1:TB + 1], cws[0:msz, mi, 1:2], a, a[0:msz, :], ALU.mult, ALU.add,
              extra_reads=[cws])
        P.stt("dve", a, a[0:msz, :], g, g[0:msz, 2:TB + 2], cws[0:msz, mi, 2:3], a, a[0:msz, :], ALU.mult, ALU.add,
              extra_reads=[cws])
        P.act(a, a[0:msz, :], a, a[0:msz, :], AF.Silu)
        P.dma("pool", gsc, gsc[m0:m0 + msz, t0:t0 + TB], a, a[0:msz, :])

    halo_sb = P.sb("halo", [128, len(mts), 2], F32)

    def epi_gate2(mi, mt, t0, pb):
        m0, msz = mt
        i = cnt[0] % 2
        cnt[0] += 1
        g, a = gbuf[i], acc[i]
        if t0 == 0:
            P.memset("pool", g, g[:, 0:2], 0.0)
        else:
            P.copy("pool", g, g[0:msz, 0:2], halo_sb, halo_sb[0:msz, mi, :])
        P.tt("dve", g, g[0:msz, 2:TB + 2], pb, pb[0:msz, 0:TB], rstd, rstd[0:msz, t0:t0 + TB], ALU.mult)
        P.copy("pool", halo_sb, halo_sb[0:msz, mi, :], g, g[0:msz, TB:TB + 2])
        P.ts("dve", a, a[0:msz, :], g, g[0:msz, 0:TB], cws[0:msz, mi, 0:1], cws[0:msz, mi, 3:4], ALU.mult, ALU.add,
             extra_reads=[cws])
        P.stt("dve", a, a[0:msz, :], g, g[0:msz, 1:TB + 1], cws[0:msz, mi, 1:2], a, a[0:msz, :], ALU.mult, ALU.add,
              extra_reads=[cws])
        P.stt("dve", a, a[0:msz, :], g, g[0:msz, 2:TB + 2], cws[0:msz, mi, 2:3], a, a[0:msz, :], ALU.mult, ALU.add,
              extra_reads=[cws])
        P.act(a, a[0:msz, :], a, a[0:msz, :], AF.Silu)
        P.dma("pool", gsc, gsc[m0:m0 + msz, t0:t0 + TB], a, a[0:msz, :])

    linear(P, xin, D, S, wg, FC, mts, TB, 512, epi_gate2)

    gl = [P.sb("gl", [128, TB], F32) for _ in range(2)]
    ub = [P.sb("ub", [128, TB], F32) for _ in range(2)]
    ob = [P.sb("ob", [128, TB], BF16) for _ in range(2)]

    def epi_up(mi, mt, t0, pb):
        m0, msz = mt
        i = cnt[0] % 2
        cnt[0] += 1
        g, u, o = gl[i], ub[i], ob[i]
        P.dma("act", g, g[0:msz, :], gsc, gsc[m0:m0 + msz, t0:t0 + TB])
        P.tt("dve", u, u[0:msz, :], pb, pb[0:msz, 0:TB], rstd, rstd[0:msz, t0:t0 + TB], ALU.mult)
        P.tt("pool", o, o[0:msz, :], u, u[0:msz, :], g, g[0:msz, :], ALU.mult)
        P.dma("pool", out, out[m0:m0 + msz, t0:t0 + TB], o, o[0:msz, :])

    linear(P, xin, D, S, wu, FC, mts, TB, 512, epi_up)
    P.pop()
    P.finish([out])
    return nc


def build_final(cfg):
    nc = bass.Bass("TRN2", target_bir_lowering=False)
    P = Prog(nc)
    DC, S, NR, D = cfg.DC, cfg.S, cfg.NR, cfg.D
    MT = DC // 128
    x = P.dram("x", [DC, S], F32, "ExternalInput")
    nw = P.dram("nw", [128, DC // 128], F32, "ExternalInput")
    parts = P.dram("parts", [NR, S], F32, "ExternalInput")
    out = P.dram("out", [DC, S], F32, "ExternalOutput")
    setup_consts(P, 1e-6)
    rstd = load_rstd(P, parts, NR, S, D, 1e-6)
    TB = 512
    P.push()
    nws = P.sb("nws", [128, MT], F32)
    P.dma("sp", nws, nws[:, :], nw, nw.ap())
    xs = [P.sb("xs", [128, TB], F32) for _ in range(3)]
    i = 0
    for m in range(MT):
        for t0 in range(0, S, TB):
            a = xs[i % 3]
            i += 1
            P.dma("sp", a, a[:, :], x, x[m * 128:(m + 1) * 128, t0:t0 + TB])
            P.stt("dve", a, a[:, :], a, a[:, :], nws[:, m:m + 1], rstd, rstd[:, t0:t0 + TB], ALU.mult, ALU.mult,
                  extra_reads=[nws])
            P.dma("pool", out, out[m * 128:(m + 1) * 128, t0:t0 + TB], a, a[:, :])
    P.pop()
    P.finish([out])
    return nc


def launch(nc, in_maps, n):
    res = run_bass_kernel_spmd(nc, in_maps, core_ids=list(range(n)))
    return res.results


def colslice(v, c, n):
    return np.ascontiguousarray(v[c * n:(c + 1) * n])


def run_model(cfg, inp, skip_mixers=False, debug=None):
    NR, S, D, DC, FC, DFF = cfg.NR, cfg.S, cfg.D, cfg.DC, cfg.FC, cfg.DFF
    f32 = np.float32
    L = inp["w_in"].shape[0]
    xT = np.ascontiguousarray(np.asarray(inp["x"], f32)[0].T)
    xs = [colslice(xT, c, DC) for c in range(NR)]
    nws = lambda v: [np.ascontiguousarray(colslice(np.asarray(v, f32), c, DC).reshape(DC // 128, 128).T) for c in range(NR)]

    nc_prep = build_prep(cfg)
    nw = nws(inp["attn_norm"][0])
    r = launch(nc_prep, [{"x": xs[c], "nw": nw[c]} for c in range(NR)], NR)
    xw = np.concatenate([r[c]["xw"] for c in range(NR)], 0)
    parts = np.concatenate([r[c]["part"] for c in range(NR)], 0)

    nc_attn = None if skip_mixers else build_attn(cfg)
    nc_res_a = build_res(cfg, NR * cfg.MIXC)
    nc_ffn = build_ffn(cfg)
    nc_res_f = build_res(cfg, DFF)
    for l in range(L):
        if skip_mixers:
            mixT = np.zeros((NR * cfg.MIXC, S), NPBF)
        else:
            maps = [attn_inputs(cfg, inp, l, c, xw, parts) for c in range(NR)]
            r = launch(nc_attn, maps, NR)
            if debug is not None:
                debug.append(r)
            mixT = np.concatenate([r[c]["mix"] for c in range(NR)], 0)
        wo = wout_shards(cfg, np.asarray(inp["w_out"][l], f32))
        nw = nws(inp["ffn_norm"][l])
        r = launch(nc_res_a, [{"xin": mixT, "w": wo[c], "x": xs[c], "nw": nw[c]} for c in range(NR)], NR)
        xs = [r[c]["xo"] for c in range(NR)]
        xw = np.concatenate([r[c]["xw"] for c in range(NR)], 0)
        parts = np.concatenate([r[c]["part"] for c in range(NR)], 0)

        wg = np.asarray(inp["w_ffn_gate"][l], f32)
        wu = np.asarray(inp["w_ffn_up"][l], f32)
        cwl = np.asarray(inp["ffn_conv"][l], f32)
        cb = np.asarray(inp["ffn_conv_b"][l], f32)
        maps = []
        for c in range(NR):
            sl = slice(c * FC, (c + 1) * FC)
            cw = np.ascontiguousarray(np.concatenate([cwl[:, sl].T, cb[sl][:, None]], 1))
            maps.append({"xin": xw, "parts": parts, "wg": np.ascontiguousarray(wg[:, sl]),
                         "wu": np.ascontiguousarray(wu[:, sl]), "cw": cw})
        r = launch(nc_ffn, maps, NR)
        actT = np.concatenate([r[c]["out"] for c in range(NR)], 0)

        wd = np.asarray(inp["w_ffn_down"][l], f32)
        nxt = inp["attn_norm"][l + 1] if l + 1 < L else inp["final_norm"]
        nw = nws(nxt)
        r = launch(nc_res_f, [{"xin": actT, "w": np.ascontiguousarray(wd[:, c * DC:(c + 1) * DC]), "x": xs[c],
                               "nw": nw[c]} for c in range(NR)], NR)
        xs = [r[c]["xo"] for c in range(NR)]
        xw = np.concatenate([r[c]["xw"] for c in range(NR)], 0)
        parts = np.concatenate([r[c]["part"] for c in range(NR)], 0)

    nc_fin = build_final(cfg)
    nw = nws(inp["final_norm"])
    r = launch(nc_fin, [{"x": xs[c], "nw": nw[c], "parts": parts} for c in range(NR)], NR)
    outT = np.concatenate([r[c]["out"] for c in range(NR)], 0)
    return np.ascontiguousarray(outT.T)[None].astype(np.float32)


def gdn_head(cfg, c, j):
    h = c + cfg.NR * j
    return h if h < cfg.GH else None


def wout_shards(cfg, wo):
    NR = cfg.NR
    rows = []
    for r in range(NR):
        for j in range(cfg.GS):
            h = gdn_head(cfg, r, j)
            rows.append(wo[h * 128:(h + 1) * 128] if h is not None else np.zeros((128, cfg.D), np.float32))
        rows.append(wo[cfg.GW + r * 128: cfg.GW + (r + 1) * 128])
        b = cfg.GW + cfg.AW + r * cfg.RC
        rows.append(wo[b:b + cfg.RC])
    full = np.concatenate(rows, 0)
    return [np.ascontiguousarray(full[:, c * cfg.DC:(c + 1) * cfg.DC]) for c in range(NR)]


def kernel(**inputs):
    return run_model(Cfg(), inputs)


def t5_bucket_np(dist):
    exact = 16
    d = np.maximum(dist, 1).astype(np.float32)
    log_b = exact + (np.log(d / np.float32(exact)) / np.float32(math.log(2048 / exact)) * np.float32(32 - exact)).astype(np.int32)
    return np.where(dist < exact, dist, np.minimum(log_b, 31))


SWA_DIL = (1, 4, 16)


def swa_bias_tiles(rel_bias_col):
    out = np.zeros((3, 2, 128, 128), np.float32)
    j = np.arange(128)[:, None]
    i = np.arange(128)[None, :]
    for p, d in enumerate(SWA_DIL):
        for t in range(2):
            steps = (i + 128 - j) if t == 0 else (i - j)
            valid = (steps >= 0) & (steps <= 128)
            idx = t5_bucket_np(np.maximum(steps, 0) * d)
            out[p, t] = np.where(valid, rel_bias_col[idx], np.float32(-30000.0))
    return out


def attn_inputs(cfg, inp, l, c, xw, parts):
    f32 = np.float32
    NR = cfg.NR
    W = np.asarray(inp["w_in"][l], f32)
    D = cfg.D
    cols = []
    GW, GH = cfg.GW, cfg.GH
    zc = np.zeros((D, 128), f32)
    for j in range(cfg.GS):
        h = gdn_head(cfg, c, j)
        for blk in range(4):
            cols.append(W[:, blk * GW + h * 128: blk * GW + (h + 1) * 128] if h is not None else zc)
    b = cfg.GDN_IN
    for blk in range(3):
        cols.append(W[:, b + blk * cfg.AW + c * 128: b + blk * cfg.AW + (c + 1) * 128])
    b = cfg.GDN_IN + cfg.SWA_IN
    for blk in range(3):
        cols.append(W[:, b + blk * cfg.RW + c * cfg.RC: b + blk * cfg.RW + (c + 1) * cfg.RC])
    cols.append(W[:, b + 3 * cfg.RW: b + 3 * cfg.RW + cfg.LW + cfg.LA + cfg.LG])
    z1 = np.zeros((D, 1), f32)
    for j in range(cfg.GS):
        h = gdn_head(cfg, c, j)
        cols.append(W[:, 4 * GW + h: 4 * GW + h + 1] if h is not None else z1)
        cols.append(W[:, 4 * GW + GH + h: 4 * GW + GH + h + 1] if h is not None else z1)
    w = np.ascontiguousarray(np.concatenate(cols, 1))
    assert w.shape[1] == cfg.PC
    m = {"xin": xw, "parts": parts, "w": w}
    m["cst"] = host_consts()
    m["bm"] = swa_bias_tiles(np.asarray(inp["rel_bias"], f32)[:, c])
    m.update(gdn_inputs(cfg, inp, l, c))
    m.update(rwkv_inputs(cfg, inp, l, c))
    return m


def gdn_inputs(cfg, inp, l, c):
    f32 = np.float32
    conv = np.asarray(inp["gdn_conv"][l], f32)
    gconv = np.zeros((128, cfg.GS * 3 * 4), f32)
    gsc = np.zeros((1, cfg.GS * 2), f32)
    for j in range(cfg.GS):
        h = gdn_head(cfg, c, j)
        if h is None:
            continue
        for b in range(3):
            gconv[:, (j * 3 + b) * 4:(j * 3 + b + 1) * 4] = conv[:, b * cfg.GW + h * 128: b * cfg.GW + (h + 1) * 128].T
        gsc[0, 2 * j] = np.asarray(inp["gdn_a_log"], f32)[l, h]
        gsc[0, 2 * j + 1] = np.asarray(inp["gdn_dt_bias"], f32)[l, h]
    gnw = np.ascontiguousarray(np.asarray(inp["gdn_norm"], f32)[l][:, None])
    return {"gconv": gconv, "gsc": gsc, "gnw": gnw}


def rwkv_inputs(cfg, inp, l, c):
    f32 = np.float32
    RW, RC = cfg.RW, cfg.RC
    ch = slice(c * RC, (c + 1) * RC)
    mu = np.asarray(inp["rwkv_mu"], f32)[l]
    vec = np.zeros((64, 3, 10), f32)
    srcs = [mu[0:RW][ch], mu[RW:2 * RW][ch], mu[2 * RW:3 * RW][ch],
            np.asarray(inp["rwkv_w0"], f32)[l][ch], np.asarray(inp["rwkv_a0"], f32)[l][ch],
            np.asarray(inp["rwkv_k_k"], f32)[l][ch], np.asarray(inp["rwkv_k_a"], f32)[l][ch],
            np.asarray(inp["rwkv_r_k"], f32)[l].reshape(-1)[ch],
            np.asarray(inp["rwkv_ln_w"], f32)[l][ch], np.asarray(inp["rwkv_ln_b"], f32)[l][ch]]
    for i, v in enumerate(srcs):
        vec[:, :, i] = v.reshape(3, 64).T
    mul = np.zeros((128, 6), f32)
    ml = mu[3 * RW:]
    for t in range(6):
        seg = ml[t * 128:(t + 1) * 128]
        mul[:len(seg), t] = seg
    gup = np.zeros((512, RC), f32)
    gup[:cfg.LG] = np.asarray(inp["rwkv_g_up"], f32)[l][:, ch]
    return {"rvec": np.ascontiguousarray(vec.reshape(64, 30)), "rmul": mul,
            "rwup": np.ascontiguousarray(np.asarray(inp["rwkv_w_up"], f32)[l][:, ch]),
            "raup": np.ascontiguousarray(np.asarray(inp["rwkv_a_up"], f32)[l][:, ch]),
            "rgup": gup}


def build_attn(cfg, parts_enabled=("swa", "gdn", "rwkv")):
    nc = bass.Bass("TRN2", target_bir_lowering=False)
    P = Prog(nc)
    D, S, NR, PC = cfg.D, cfg.S, cfg.NR, cfg.PC
    xin = P.dram("xin", [D, S], BF16, "ExternalInput")
    parts = P.dram("parts", [NR, S], F32, "ExternalInput")
    w = P.dram("w", [D, PC], F32, "ExternalInput")
    bm = P.dram("bm", [3, 2, 128, 128], F32, "ExternalInput")
    P.cst = P.dram("cst", [4, 128, 128], F32, "ExternalInput")
    P.rin = {"rvec": P.dram("rvec", [64, 30], F32, "ExternalInput"),
             "rmul": P.dram("rmul", [128, 6], F32, "ExternalInput"),
             "rwup": P.dram("rwup", [128, cfg.RC], F32, "ExternalInput"),
             "raup": P.dram("raup", [128, cfg.RC], F32, "ExternalInput"),
             "rgup": P.dram("rgup", [512, cfg.RC], F32, "ExternalInput")}
    P.gin = {"gconv": P.dram("gconv", [128, cfg.GS * 12], F32, "ExternalInput"),
             "gsc": P.dram("gsc", [1, cfg.GS * 2], F32, "ExternalInput"),
             "gnw": P.dram("gnw", [128, 1], F32, "ExternalInput")}
    mix = P.dram("mix", [cfg.MIXC, S], BF16, "ExternalOutput")
    proj = P.dram("proj", [PC, S], F32)
    setup_consts(P, 1e-6)
    P.push()
    rstd = load_rstd(P, parts, NR, S, D, 1e-6)
    TB = 512
    pj = [P.sb("pj", [128, TB], F32) for _ in range(3)]
    cnt = [0]

    def epi(mi, mt, t0, pb):
        m0, msz = mt
        a = pj[cnt[0] % 3]
        cnt[0] += 1
        P.tt("dve", a, a[0:msz, :], pb, pb[0:msz, 0:TB], rstd, rstd[0:msz, t0:t0 + TB], ALU.mult)
        P.dma("pool", proj, proj[m0:m0 + msz, t0:t0 + TB], a, a[0:msz, :])

    linear(P, xin, D, S, w, PC, ktiles(PC), TB, 512, epi)
    P.pop()
    if "gdn" in parts_enabled and "rwkv" in parts_enabled and COSCHED:
        emit_gdn_rwkv(P, cfg, proj, mix)
    else:
        emit_gdn(P, cfg, proj, mix) if "gdn" in parts_enabled else emit_zero_rows(P, cfg, mix, 0, 256)
        emit_rwkv(P, cfg, proj, mix) if "rwkv" in parts_enabled else emit_zero_rows(P, cfg, mix, 384, 192)
    emit_swa(P, cfg, proj, mix, bm) if "swa" in parts_enabled else emit_zero_rows(P, cfg, mix, 256, 128)
    P.finish([mix])
    return nc


COSCHED = False


def emit_gdn_rwkv(P, cfg, proj, mix):
    S = cfg.S
    GSEG = min(512, S)
    RSEG = min(256, S)
    P.push()
    sh = mixer_shared(P, 2, 6)
    gd = emit_gdn(P, cfg, proj, mix, shared=sh, SEG=GSEG)
    rw = emit_rwkv(P, cfg, proj, mix, shared=sh, SEG=RSEG)
    nsub = GSEG // RSEG

    def rw_super(G):
        for k in range(nsub):
            g = G * nsub + k
            rw["pre"](g)
            yield
            yield from rw["pipe"](g)
            rw["post"](g)
            yield

    for G in range(S // GSEG):
        gd["pre"](G)
        run_rr([gd["pipe"](G), rw_super(G)])
        gd["post"](G)
    P.pop()


def emit_zero_rows(P, cfg, mix, r0, n):
    P.push()
    z = P.sb("z", [128, 2048], BF16)
    P.memset("dve", z, z[:, :], 0.0)
    for a in range(r0, r0 + n, 128):
        m = min(128, r0 + n - a)
        for t0 in range(0, cfg.S, 2048):
            P.dma("pool", mix, mix[a:a + m, t0:t0 + 2048], z, z[0:m, :])
    P.pop()


def mixer_shared(P, nburst, nchain):
    sh = {}
    ident = P.sb("ident", [128, 128], F32)
    make_identity(P, ident)
    Ui = P.sb("Ui", [128, 128], F32)
    P.dma("sp", Ui, Ui[:, :], P.cst, P.cst[1, :, :])
    Us = P.sb("Us", [128, 128], F32)
    P.dma("sp", Us, Us[:, :], P.cst, P.cst[2, :, :])
    sh["ident"], sh["Ui"], sh["Us"] = ident, Ui, Us
    burst = [P.ps("bps", [128, 512], F32) for _ in range(nburst)]
    pfree = [P.ps("cps", [128, 512], F32) for _ in range(nchain)]

    def pp():
        b = burst.pop(0)
        burst.append(b)
        return b

    def get():
        while not pfree:
            yield
        return pfree.pop(0)

    def rel(b):
        pfree.append(b)

    sh["pp"], sh["get"], sh["rel"] = pp, get, rel
    return sh


def run_rr_gen(gens):
    gens = list(gens)
    while gens:
        nxt = []
        for g in gens:
            try:
                next(g)
                nxt.append(g)
            except StopIteration:
                pass
        gens = nxt
        yield


def emit_gdn(P, cfg, proj, mix, shared=None, SEG=None):
    S = cfg.S
    if SEG is None:
        SEG = 1024 if S >= 1024 else S
    NSEG = S // SEG
    C = 128
    NSL = cfg.GS
    own = shared is None
    if own:
        P.push()
        shared = mixer_shared(P, 1, 7)
    ident, Ui, Us = shared["ident"], shared["Ui"], shared["Us"]
    pp, get, rel = shared["pp"], shared["get"], shared["rel"]
    ones = P.sb("ones", [128, 128], F32)
    P.memset("dve", ones, ones[:, :], 1.0)
    gconv = P.sb("gconv", [128, cfg.GS * 12], F32)
    P.dma("sp", gconv, gconv[:, :], P.gin["gconv"], P.gin["gconv"].ap())
    gsc = P.sb("gsc", [1, cfg.GS * 2], F32)
    P.dma("sp", gsc, gsc[:, :], P.gin["gsc"], P.gin["gsc"].ap())
    gnw = P.sb("gnw", [128, 1], F32)
    P.dma("sp", gnw, gnw[:, :], P.gin["gnw"], P.gin["gnw"].ap())
    negA = P.sb("negA", [1, cfg.GS], F32)
    for j in range(cfg.GS):
        P.act(negA, negA[:, j:j + 1], gsc, gsc[:, 2 * j:2 * j + 1], AF.Exp)
    P.ts("dve", negA, negA[:, :], negA, negA[:, :], -1.0, None, ALU.mult)
    one1 = P.sb("one1", [1, 1], F32)
    P.memset("dve", one1, one1[:, :], 1.0)

    T = {}

    def tb(name, shape=(128, 128), n=2):
        if name not in T:
            T[name] = ([P.sb("g_" + name, list(shape), F32) for _ in range(n)], [0])
        bufs, k = T[name]
        k[0] += 1
        return bufs[k[0] % len(bufs)]

    raw = P.sb("raw", [128, SEG + 3], F32)
    acc = P.sb("acc", [128, SEG], F32)
    SB = []
    for j in range(NSL):
        d = {}
        for nm in ("qT", "kT", "vT", "zs"):
            d[nm] = P.sb("g" + nm + str(j), [128, SEG], F32)
        d["ob"] = P.sb("gob" + str(j), [128, SEG], BF16)
        d["brow"] = P.sb("brow" + str(j), [1, SEG], F32)
        d["grow"] = P.sb("grow" + str(j), [1, SEG], F32)
        d["St"] = P.sb("gS" + str(j), [128, 128], F32)
        P.memset("dve", d["St"], d["St"][:, :], 0.0)
        SB.append(d)
    res = {}

    def chunkA(j, ci):
        sb = SB[j]
        qT, kT, vT, brow, grow = sb["qT"], sb["kT"], sb["vT"], sb["brow"], sb["grow"]
        sfx = str(j)
        csl = slice(ci * C, ci * C + C)
        pb = yield from get()
        P.mm(pb, pb[:, 0:1], grow, grow[:, csl], one1, one1[:, :])
        P.mm(pb, pb[:, 1:2], brow, brow[:, csl], one1, one1[:, :])
        yield
        cols = tb("cols" + sfx, (128, 8))
        P.copy("dve", cols, cols[:, 0:2], pb, pb[:, 0:2])
        rel(pb)
        yield
        pb = yield from get()
        P.mm(pb, pb[:, 0:1], Ui, Ui[:, :], cols, cols[:, 0:1])
        Ug = tb("Ug" + sfx)
        P.ts("dve", Ug, Ug[:, :], Ui, Ui[:, :], cols[:, 0:1], None, ALU.mult, extra_reads=[cols])
        yield
        P.copy("dve", cols, cols[:, 2:3], pb, pb[:, 0:1])
        rel(pb)
        pG = yield from get()
        P.mm(pG, pG[:, 0:128], ones, ones[:, :], Ug, Ug[:, :])
        yield
        Gsb = tb("Gsb" + sfx)
        P.copy("act", Gsb, Gsb[:, :], pG, pG[:, 0:128])
        rel(pG)
        yield
        dT = tb("dT" + sfx)
        P.ts("dve", dT, dT[:, :], Gsb, Gsb[:, :], cols[:, 2:3], 0.0, ALU.subtract, ALU.min, extra_reads=[cols])
        eG = tb("eG" + sfx)
        P.act(eG, eG[:, :], Gsb, Gsb[:, :], AF.Exp)
        yield
        gam = tb("gam" + sfx)
        P.act(gam, gam[:, :], dT, dT[:, :], AF.Exp)
        P.ts("dve", cols, cols[:, 4:5], Gsb, Gsb[:, 127:128], cols[:, 2:3], None, ALU.subtract, extra_reads=[cols])
        yield
        P.act(cols, cols[:, 3:4], cols, cols[:, 2:3], AF.Exp)
        P.act(cols, cols[:, 4:5], cols, cols[:, 4:5], AF.Exp)
        gami = tb("gami" + sfx)
        P.tt("pool", gami, gami[:, :], gam, gam[:, :], Ui, Ui[:, :], ALU.mult)
        gams = tb("gams" + sfx)
        P.tt("pool", gams, gams[:, :], gam, gam[:, :], Us, Us[:, :], ALU.mult)
        yield
        pk = yield from get()
        P.tr(pk, pk[:, 0:128], kT, kT[:, csl], ident, ident[:, :])
        yield
        kbG = tb("kbG" + sfx)
        P.ts("dve", kbG, kbG[:, :], pk, pk[:, 0:128], cols[:, 1:2], cols[:, 3:4], ALU.mult, ALU.mult, extra_reads=[cols])
        kd = tb("kd" + sfx)
        P.ts("dve", kd, kd[:, :], pk, pk[:, 0:128], cols[:, 4:5], None, ALU.mult, extra_reads=[cols])
        rel(pk)
        yield
        pv = yield from get()
        P.tr(pv, pv[:, 0:128], vT, vT[:, csl], ident, ident[:, :])
        yield
        vb = tb("vb" + sfx)
        P.ts("dve", vb, vb[:, :], pv, pv[:, 0:128], cols[:, 1:2], None, ALU.mult, extra_reads=[cols])
        rel(pv)
        yield
        pbr = yield from get()
        P.mm(pbr, pbr[:, 0:128], ones, ones[0:1, :], brow, brow[:, csl])
        yield
        kbT = tb("kbT" + sfx)
        P.tt("dve", kbT, kbT[:, :], kT, kT[:, csl], pbr, pbr[:, 0:128], ALU.mult)
        rel(pbr)
        yield
        pA = yield from get()
        P.mm(pA, pA[:, 0:128], kT, kT[:, csl], kbT, kbT[:, :])
        P.mm(pA, pA[:, 128:256], kT, kT[:, csl], qT, qT[:, csl])
        yield
        Pm = tb("Pm" + sfx, n=3)
        P.stt("dve", Pm, Pm[:, :], pA, pA[:, 0:128], -1.0, gams, gams[:, :], ALU.mult, ALU.mult)
        attnT = tb("attnT" + sfx)
        P.tt("dve", attnT, attnT[:, :], pA, pA[:, 128:256], gami, gami[:, :], ALU.mult)
        rel(pA)
        yield
        pq = yield from get()
        P.tr(pq, pq[:, 0:128], Pm, Pm[:, :], ident, ident[:, :])
        yield
        Qm = tb("Qm" + sfx, n=3)
        P.copy("act", Qm, Qm[:, :], pq, pq[:, 0:128])
        rel(pq)
        R = tb("R" + sfx, n=3)
        P.tt("pool", R, R[:, :], Pm, Pm[:, :], ident, ident[:, :], ALU.add)
        qgT = tb("qgT" + sfx)
        P.tt("pool", qgT, qgT[:, :], qT, qT[:, csl], eG, eG[:, :], ALU.mult)
        yield
        for k in range(1, 7):
            pQ = yield from get()
            P.mm(pQ, pQ[:, 0:128], Pm, Pm[:, :], Qm, Qm[:, :])
            yield
            Qn = tb("Qm" + sfx, n=3)
            P.copy("dve", Qn, Qn[:, :], pQ, pQ[:, 0:128])
            rel(pQ)
            yield
            if k < 6:
                pP = yield from get()
                P.mm(pP, pP[:, 0:128], Qm, Qm[:, :], Pm, Pm[:, :])
                yield
                Pn = tb("Pm" + sfx, n=3)
                P.copy("act", Pn, Pn[:, :], pP, pP[:, 0:128])
                rel(pP)
                yield
            pR = yield from get()
            P.mm(pR, pR[:, 0:128], Qn, Qn[:, :], R, R[:, :])
            yield
            Rn = tb("R" + sfx, n=3)
            P.tt("dve", Rn, Rn[:, :], R, R[:, :], pR, pR[:, 0:128], ALU.add)
            rel(pR)
            R, Qm = Rn, Qn
            if k < 6:
                Pm = Pn
            yield
        pu = yield from get()
        P.mm(pu, pu[:, 0:128], R, R[:, :], vb, vb[:, :])
        yield
        u = tb("u" + sfx)
        P.copy("act", u, u[:, :], pu, pu[:, 0:128])
        rel(pu)
        yield
        pw = yield from get()
        P.mm(pw, pw[:, 0:128], kbG, kbG[:, :], R, R[:, :])
        yield
        wT = tb("wT" + sfx)
        P.copy("act", wT, wT[:, :], pw, pw[:, 0:128])
        rel(pw)
        res[(j, ci)] = dict(u=u, wT=wT, qgT=qgT, attnT=attnT, kd=kd, eG=eG)
        yield

    def chunkB(j, ci):
        sb = SB[j]
        St, zs, ob = sb["St"], sb["zs"], sb["ob"]
        r = res.pop((j, ci))
        u, wT, qgT, attnT, kd, eG = r["u"], r["wT"], r["qgT"], r["attnT"], r["kd"], r["eG"]
        sfx = str(j)
        csl = slice(ci * C, ci * C + C)
        pa = yield from get()
        P.mm(pa, pa[:, 0:128], wT, wT[:, :], St, St[:, :])
        yield
        vn = tb("vn" + sfx)
        P.tt("dve", vn, vn[:, :], u, u[:, :], pa, pa[:, 0:128], ALU.subtract)
        rel(pa)
        yield
        pS = yield from get()
        P.mm(pS, pS[:, 0:128], kd, kd[:, :], vn, vn[:, :])
        po = yield from get()
        P.mm(po, po[:, 0:128], qgT, qgT[:, :], St, St[:, :], start=True, stop=False)
        P.mm(po, po[:, 0:128], attnT, attnT[:, :], vn, vn[:, :], start=False, stop=True)
        yield
        P.stt("dve", St, St[:, :], St, St[:, :], eG[:, 127:128], pS, pS[:, 0:128], ALU.mult, ALU.add, extra_reads=[eG])
        rel(pS)
        osb = tb("osb" + sfx)
        P.copy("act", osb, osb[:, :], po, po[:, 0:128])
        rel(po)
        yield
        osq = tb("osq" + sfx)
        P.act(osq, osq[:, :], osb, osb[:, :], AF.Square)
        yield
        oc = tb("oc" + sfx, (128, 2))
        P.op("dve", lambda en, oc=oc, osq=osq: en.reduce_sum(out=oc[:, 0:1], in_=osq[:, :], axis=AX.X),
             reads=[osq], writes=[oc])
        yield
        P.act(oc, oc[:, 0:1], oc, oc[:, 0:1], AF.Sqrt, bias=P.consts["eps"][:, 0:1], scale=1.0 / 128,
              extra_reads=[P.consts["eps"]])
        yield
        P.op("dve", lambda en, oc=oc: en.reciprocal(out=oc[:, 0:1], in_=oc[:, 0:1]), reads=[oc], writes=[oc])
        P.ts("dve", osb, osb[:, :], osb, osb[:, :], oc[:, 0:1], None, ALU.mult, extra_reads=[oc])
        yield
        pt = yield from get()
        P.tr(pt, pt[:, 0:128], osb, osb[:, :], ident, ident[:, :])
        yield
        P.stt("dve", ob, ob[:, csl], pt, pt[:, 0:128], gnw[:, 0:1], zs, zs[:, csl], ALU.mult, ALU.mult, extra_reads=[gnw])
        rel(pt)
        yield

    def pre(g):
        t0 = g * SEG
        for j in range(NSL):
            sb = SB[j]
            qT, kT, vT, zs, brow, grow = sb["qT"], sb["kT"], sb["vT"], sb["zs"], sb["brow"], sb["grow"]
            rq = cfg.o_g + 512 * j
            for b, dst in enumerate((qT, kT, vT)):
                row = rq + 128 * b
                if g == 0:
                    P.memset("pool", raw, raw[:, 0:3], 0.0)
                    P.dma("sp", raw, raw[:, 3:SEG + 3], proj, proj[row:row + 128, 0:SEG])
                else:
                    P.dma("sp", raw, raw[:, :], proj, proj[row:row + 128, t0 - 3:t0 + SEG])
                cb = (j * 3 + b) * 4
                P.ts("dve", acc, acc[:, :], raw, raw[:, 0:SEG], gconv[:, cb:cb + 1], None, ALU.mult, extra_reads=[gconv])
                for i in range(1, 4):
                    P.stt("dve", acc, acc[:, :], raw, raw[:, i:i + SEG], gconv[:, cb + i:cb + i + 1],
                          acc, acc[:, :], ALU.mult, ALU.add, extra_reads=[gconv])
                P.act(dst, dst[:, :], acc, acc[:, :], AF.Silu)
            P.dma("sp", acc, acc[:, :], proj, proj[rq + 384:rq + 512, t0:t0 + SEG])
            P.act(zs, zs[:, :], acc, acc[:, :], AF.Silu)
            for dst, sc in ((qT, 128 ** -0.5), (kT, 1.0)):
                P.act(acc, acc[:, :], dst, dst[:, :], AF.Square)
                for c0 in range(0, SEG, 256):
                    pb = pp()
                    P.mm(pb, pb[:, 0:256], ones, ones[:, :], acc, acc[:, c0:c0 + 256])
                    rs = tb("rs", (128, 256))
                    P.act(rs, rs[:, :], pb, pb[:, 0:256], AF.Sqrt, bias=P.consts["eps"][:, 0:1], extra_reads=[P.consts["eps"]])
                    P.op("dve", lambda en, rs=rs: en.reciprocal(out=rs[:, :], in_=rs[:, :]), reads=[rs], writes=[rs])
                    P.stt("dve", dst, dst[:, c0:c0 + 256], dst, dst[:, c0:c0 + 256], sc, rs, rs[:, :], ALU.mult, ALU.mult)
            srow = cfg.o_s + 2 * j
            P.dma("sp", brow, brow[:, :], proj, proj[srow:srow + 1, t0:t0 + SEG])
            P.act(brow, brow[:, :], brow, brow[:, :], AF.Sigmoid)
            P.dma("sp", grow, grow[:, :], proj, proj[srow + 1:srow + 2, t0:t0 + SEG])
            P.act(grow, grow[:, :], grow, grow[:, :], AF.Exp, bias=gsc[:, 2 * j + 1:2 * j + 2], extra_reads=[gsc])
            P.act(grow, grow[:, :], grow, grow[:, :], AF.Ln, bias=one1[:, 0:1], extra_reads=[one1])
            P.ts("dve", grow, grow[:, :], grow, grow[:, :], negA[:, j:j + 1], None, ALU.mult, extra_reads=[negA])

    def pipe(g):
        NCH = SEG // C
        for ci in range(NCH + 1):
            gens = []
            for j in range(NSL):
                if ci >= 1:
                    gens.append(chunkB(j, ci - 1))
            for j in range(NSL):
                if ci < NCH:
                    gens.append(chunkA(j, ci))
            yield from run_rr_gen(gens)

    def post(g):
        t0 = g * SEG
        for j in range(NSL):
            P.dma("pool", mix, mix[128 * j:128 * (j + 1), t0:t0 + SEG], SB[j]["ob"], SB[j]["ob"][:, :])

    if not own:
        return dict(pre=pre, pipe=pipe, post=post, SEG=SEG)
    for g in range(NSEG):
        pre(g)
        run_rr([pipe(g)])
        post(g)
    P.pop()


def run_rr(gens):
    gens = list(gens)
    while gens:
        nxt = []
        for g in gens:
            try:
                next(g)
                nxt.append(g)
            except StopIteration:
                pass
        gens = nxt


def emit_rwkv(P, cfg, proj, mix, shared=None, SEG=None):
    S = cfg.S
    if SEG is None:
        SEG = 512 if S >= 512 else S
    NSEG = S // SEG
    C = 128
    BW = min(256, SEG)
    NH = 3
    own = shared is None
    if own:
        P.push()
        shared = mixer_shared(P, 1, 7)
    ident, Ui, Us = shared["ident"], shared["Ui"], shared["Us"]
    pp, get, rel = shared["pp"], shared["get"], shared["rel"]
    ones = P.sb("ones", [64, 64], F32)
    P.memset("dve", ones, ones[:, :], 1.0)
    rvec = P.sb("rvec", [64, 30], F32)
    P.dma("sp", rvec, rvec[:, :], P.rin["rvec"], P.rin["rvec"].ap())
    rmul = P.sb("rmul", [128, 6], F32)
    P.dma("sp", rmul, rmul[:, :], P.rin["rmul"], P.rin["rmul"].ap())
    wup = P.sb("wup", [128, cfg.RC], F32)
    P.dma("sp", wup, wup[:, :], P.rin["rwup"], P.rin["rwup"].ap())
    aup = P.sb("aup", [128, cfg.RC], F32)
    P.dma("sp", aup, aup[:, :], P.rin["raup"], P.rin["raup"].ap())
    gup = P.sb("gup", [128, 4, cfg.RC], F32)
    P.dma("sp", gup, gup[:, :, :], P.rin["rgup"], P.rin["rgup"].ap().rearrange("(t p) c -> p t c", p=128))
    omk = P.sb("omk", [64, 3], F32)
    for h in range(NH):
        P.ts("dve", omk, omk[:, h:h + 1], rvec, rvec[:, h * 10 + 6:h * 10 + 7], -1.0, 1.0, ALU.mult, ALU.add)
    eps12 = P.sb("eps12", [64, 1], F32)
    P.memset("dve", eps12, eps12[:, :], 1e-12)
    epsgn = P.sb("epsgn", [64, 1], F32)
    P.memset("dve", epsgn, epsgn[:, :], 64e-5)

    T = {}

    def tb(name, shape=(64, 64), n=2, dt=F32):
        if name not in T:
            T[name] = ([P.sb("r_" + name, list(shape), dt) for _ in range(n)], [0])
        bufs, k = T[name]
        k[0] += 1
        return bufs[k[0] % len(bufs)]

    lraw = P.sb("lraw", [128, SEG + 1], F32)
    ldif = P.sb("ldif", [128, SEG], F32)
    twd = P.sb("twd", [128, SEG], F32)
    adp = P.sb("adp", [128, SEG], F32)
    sgd = P.sb("sgd", [128, 4, SEG], F32)
    hraw = P.sb("hraw", [64, SEG + 1], F32)
    hdif = P.sb("hdif", [64, SEG], F32)
    t1 = P.sb("t1", [64, SEG], F32)
    HB = []
    for h in range(NH):
        d = {}
        for nm in ("rS", "kS", "vS", "lwS", "aS", "gS", "kkS", "bS", "yS", "bon"):
            d[nm] = P.sb(nm + str(h), [64, SEG], F32)
        d["ob"] = P.sb("rob" + str(h), [64, SEG], BF16)
        d["H"] = P.sb("rH" + str(h), [64, 64], F32)
        P.memset("dve", d["H"], d["H"][:, :], 0.0)
        d["Hb"] = P.sb("rHb" + str(h), [64, 64], BF16)
        P.memset("dve", d["Hb"], d["Hb"][:, :], 0.0)
        HB.append(d)
    I64, Ui64, Us64 = ident[0:64, 0:64], Ui[0:64, 0:64], Us[0:64, 0:64]

    def shift_load(dst_raw, rows, row0, t0, g):
        if g == 0:
            P.memset("pool", dst_raw, dst_raw[0:rows, 0:1], 0.0)
            P.dma("sp", dst_raw, dst_raw[0:rows, 1:SEG + 1], proj, proj[row0:row0 + rows, 0:SEG])
        else:
            P.dma("sp", dst_raw, dst_raw[0:rows, :], proj, proj[row0:row0 + rows, t0 - 1:t0 + SEG])

    res = {}

    def chunkA(h, ci):
        hb = HB[h]
        rS, kS, vS, lwS, aS, bS = hb["rS"], hb["kS"], hb["vS"], hb["lwS"], hb["aS"], hb["bS"]
        sfx = str(h)
        csl = slice(ci * C, ci * C + C)
        pl = yield from get()
        P.tr(pl, pl[0:C, 0:64], lwS, lwS[:, csl], ident, I64)
        yield
        lwt = tb("lwt" + sfx, (C, 64))
        P.copy("act", lwt, lwt[:, :], pl, pl[0:C, 0:64])
        rel(pl)
        yield
        pL = yield from get()
        P.mm(pL, pL[0:64, 0:C], lwt, lwt[:, :], Ui, Ui[0:C, 0:C])
        yield
        Lsb = tb("Lsb" + sfx, (64, C))
        P.copy("act", Lsb, Lsb[:, :], pL, pL[0:64, 0:C])
        rel(pL)
        yield
        eLi = tb("eLi" + sfx, (64, C))
        P.act(eLi, eLi[:, :], Lsb, Lsb[:, :], AF.Exp)
        eLx = tb("eLx" + sfx, (64, C))
        P.tt("dve", eLx, eLx[:, :], Lsb, Lsb[:, :], lwS, lwS[:, csl], ALU.subtract)
        yield
        eLn = tb("eLn" + sfx, (64, C))
        P.act(eLn, eLn[:, :], Lsb, Lsb[:, :], AF.Exp, scale=-1.0)
        yield
        P.act(eLx, eLx[:, :], eLx, eLx[:, :], AF.Exp)
        AR = tb("AR" + sfx, (64, 2 * C))
        P.tt("pool", AR, AR[:, C:2 * C], rS, rS[:, csl], eLi, eLi[:, :], ALU.mult)
        yield
        bt = tb("bt" + sfx, (64, C))
        P.tt("dve", bt, bt[:, :], bS, bS[:, csl], eLn, eLn[:, :], ALU.mult)
        kt_ = tb("kt" + sfx, (64, C))
        P.tt("pool", kt_, kt_[:, :], kS, kS[:, csl], eLn, eLn[:, :], ALU.mult)
        yield
        P.tt("dve", AR, AR[:, 0:C], aS, aS[:, csl], eLx, eLx[:, :], ALU.mult)
        yield
        bt16 = tb("bt16" + sfx, (64, C), dt=BF16)
        P.copy("pool", bt16, bt16[:, :], bt, bt[:, :])
        kt16 = tb("kt16" + sfx, (64, C), dt=BF16)
        P.copy("act", kt16, kt16[:, :], kt_, kt_[:, :])
        AR16 = tb("AR16" + sfx, (64, 2 * C), dt=BF16)
        P.copy("pool", AR16, AR16[:, :], AR, AR[:, :])
        yield
        tok = {}
        for nm, (src, sap) in {"V": (vS, vS[:, csl]), "B": (bt, bt[:, :]), "K": (kt_, kt_[:, :]),
                               "A": (AR, AR[:, 0:C])}.items():
            pt = yield from get()
            P.tr(pt, pt[0:C, 0:64], src, sap, ident, I64)
            yield
            d = tb("tok" + nm + sfx, (C, 64), dt=BF16)
            P.copy("act" if nm in ("B", "V") else "dve", d, d[:, :], pt, pt[0:C, 0:64])
            rel(pt)
            tok[nm] = d
            yield
        pAB = yield from get()
        P.mm(pAB, pAB[0:C, 0:2 * C], bt16, bt16[:, :], AR16, AR16[:, :])
        yield
        Pm32 = tb("Pm32" + sfx, (C, C))
        P.tt("dve", Pm32, Pm32[:, :], pAB, pAB[0:C, 0:C], Us, Us[0:C, 0:C], ALU.mult)
        ArbT = tb("ArbT" + sfx, (C, C), dt=BF16)
        P.tt("dve", ArbT, ArbT[:, :], pAB, pAB[0:C, C:2 * C], Ui, Ui[0:C, 0:C], ALU.mult)
        rel(pAB)
        yield
        pAK = yield from get()
        P.mm(pAK, pAK[0:C, 0:2 * C], kt16, kt16[:, :], AR16, AR16[:, :])
        yield
        AakT = tb("AakT" + sfx, (C, C), dt=BF16)
        P.tt("dve", AakT, AakT[:, :], pAK, pAK[0:C, 0:C], Us, Us[0:C, 0:C], ALU.mult)
        ArkT = tb("ArkT" + sfx, (C, C), dt=BF16)
        P.tt("dve", ArkT, ArkT[:, :], pAK, pAK[0:C, C:2 * C], Ui, Ui[0:C, 0:C], ALU.mult)
        rel(pAK)
        yield
        pq = yield from get()
        P.tr(pq, pq[0:C, 0:C], Pm32, Pm32[:, :], ident, ident[0:C, 0:C])
        Pm = tb("Pm" + sfx, (C, C), n=3, dt=BF16)
        P.copy("pool", Pm, Pm[:, :], Pm32, Pm32[:, :])
        yield
        Qm = tb("Qm" + sfx, (C, C), n=3, dt=BF16)
        P.copy("act", Qm, Qm[:, :], pq, pq[0:C, 0:C])
        rel(pq)
        R = tb("R" + sfx, (C, C), n=3, dt=BF16)
        P.tt("pool", R, R[:, :], Pm32, Pm32[:, :], ident, ident[0:C, 0:C], ALU.add)
        yield
        pX = yield from get()
        P.mm(pX, pX[0:C, 0:64], AakT, AakT[:, :], tok["V"], tok["V"][:, :])
        yield
        X = tb("X" + sfx, (C, 64), dt=BF16)
        P.copy("act", X, X[:, :], pX, pX[0:C, 0:64])
        rel(pX)
        yield
        NK = 6 if C == 128 else 5
        for k in range(1, NK + 1):
            pQ = yield from get()
            P.mm(pQ, pQ[0:C, 0:C], Pm, Pm[:, :], Qm, Qm[:, :])
            yield
            Qn = tb("Qm" + sfx, (C, C), n=3, dt=BF16)
            P.copy("dve", Qn, Qn[:, :], pQ, pQ[0:C, 0:C])
            rel(pQ)
            yield
            if k < NK:
                pP = yield from get()
                P.mm(pP, pP[0:C, 0:C], Qm, Qm[:, :], Pm, Pm[:, :])
                yield
                Pn = tb("Pm" + sfx, (C, C), n=3, dt=BF16)
                P.copy("act", Pn, Pn[:, :], pP, pP[0:C, 0:C])
                rel(pP)
                yield
            pR = yield from get()
            P.mm(pR, pR[0:C, 0:C], Qn, Qn[:, :], R, R[:, :])
            yield
            Rn = tb("R" + sfx, (C, C), n=3, dt=BF16)
            P.tt("dve", Rn, Rn[:, :], R, R[:, :], pR, pR[0:C, 0:C], ALU.add)
            rel(pR)
            R, Qm = Rn, Qn
            if k < NK:
                Pm = Pn
            yield
        pW = yield from get()
        P.mm(pW, pW[0:64, 0:C], tok["A"], tok["A"][:, :], R, R[:, :])
        yield
        WdT = tb("WdT" + sfx, (64, C), dt=BF16)
        P.copy("act", WdT, WdT[:, :], pW, pW[0:64, 0:C])
        rel(pW)
        yield
        pU0 = yield from get()
        P.mm(pU0, pU0[0:C, 0:64], R, R[:, :], X, X[:, :])
        yield
        U0 = tb("U0" + sfx, (C, 64))
        P.copy("dve", U0, U0[:, :], pU0, pU0[0:C, 0:64])
        rel(pU0)
        res[(h, ci)] = dict(WdT=WdT, U0=U0, AR=AR16, ArbT=ArbT, ArkT=ArkT, tok=tok, eLi=eLi)
        yield

    def chunkB(h, ci):
        hb = HB[h]
        H, yS, Hb = hb["H"], hb["yS"], hb["Hb"]
        r = res.pop((h, ci))
        WdT, U0, AR, ArbT, ArkT, tok, eLi = r["WdT"], r["U0"], r["AR"], r["ArbT"], r["ArkT"], r["tok"], r["eLi"]
        sfx = str(h)
        csl = slice(ci * C, ci * C + C)
        pU = yield from get()
        P.mm(pU, pU[0:C, 0:64], WdT, WdT[:, :], Hb, Hb[:, :])
        yield
        U = tb("U" + sfx, (C, 64), dt=BF16)
        P.tt("dve", U, U[:, :], U0, U0[:, :], pU, pU[0:C, 0:64], ALU.add)
        rel(pU)
        yield
        pH = yield from get()
        P.mm(pH, pH[0:64, 0:64], tok["B"], tok["B"][:, :], U, U[:, :], start=True, stop=False)
        P.mm(pH, pH[0:64, 0:64], tok["K"], tok["K"][:, :], tok["V"], tok["V"][:, :], start=False, stop=True)
        pY = yield from get()
        P.mm(pY, pY[0:64, 0:C], Hb, Hb[:, :], AR, AR[:, C:2 * C], start=True, stop=False)
        P.mm(pY, pY[0:64, 0:C], U, U[:, :], ArbT, ArbT[:, :], start=False, stop=False)
        P.mm(pY, pY[0:64, 0:C], tok["V"], tok["V"][:, :], ArkT, ArkT[:, :], start=False, stop=True)
        yield
        P.ts("dve", H, H[:, :], H, H[:, :], eLi[:, C - 1:C], None, ALU.mult, extra_reads=[eLi])
        P.stt("dve", H, H[:, :], pH, pH[0:64, 0:64], eLi[:, C - 1:C], H, H[:, :], ALU.mult, ALU.add, extra_reads=[eLi])
        P.copy("pool", Hb, Hb[:, :], H, H[:, :])
        P.copy("act", yS, yS[:, csl], pY, pY[0:64, 0:C])
        rel(pH)
        rel(pY)
        yield

    def pre(g):
        t0 = g * SEG
        lt = [(cfg.o_l, 128, 0), (cfg.o_l + 128, 128, 1)] + \
             [(cfg.o_l + 256 + 128 * i, min(128, cfg.LG - 128 * i), 2 + i) for i in range(4)]
        for (row0, rows, ti) in lt:
            shift_load(lraw, rows, row0, t0, g)
            P.tt("dve", ldif, ldif[0:rows, :], lraw, lraw[0:rows, 0:SEG], lraw, lraw[0:rows, 1:SEG + 1], ALU.subtract)
            P.stt("dve", ldif, ldif[0:rows, :], ldif, ldif[0:rows, :], rmul[0:rows, ti:ti + 1], lraw, lraw[0:rows, 1:SEG + 1],
                  ALU.mult, ALU.add, extra_reads=[rmul])
            if ti == 0:
                P.act(twd, twd[:, :], ldif, ldif[:, :], AF.Tanh)
            elif ti == 1:
                P.copy("pool", adp, adp[:, :], ldif, ldif[:, :])
            else:
                if rows < 128:
                    P.memset("pool", sgd, sgd[:, ti - 2, :], 0.0)
                P.act(sgd, sgd[0:rows, ti - 2, :], ldif, ldif[0:rows, :], AF.Sigmoid)
        for h in range(NH):
            hb = HB[h]
            rS, kS, vS, lwS, aS, gS, kkS, bS, bon = (hb[n_] for n_ in ("rS", "kS", "vS", "lwS", "aS", "gS", "kkS", "bS", "bon"))
            vb_ = h * 10
            hc = slice(64 * h, 64 * h + 64)
            for bi_, dst in enumerate((rS, kS, vS)):
                shift_load(hraw, 64, cfg.o_r + cfg.RC * bi_ + 64 * h, t0, g)
                P.tt("dve", hdif, hdif[:, :], hraw, hraw[:, 0:SEG], hraw, hraw[:, 1:SEG + 1], ALU.subtract)
                P.stt("dve", dst, dst[:, :], hdif, hdif[:, :], rvec[:, vb_ + bi_:vb_ + bi_ + 1], hraw, hraw[:, 1:SEG + 1],
                      ALU.mult, ALU.add, extra_reads=[rvec])
            for b0 in range(0, SEG, BW):
                bs = slice(b0, b0 + BW)
                pw = pp()
                P.mm(pw, pw[0:64, 0:BW], wup, wup[:, hc], twd, twd[:, bs])
                P.act(lwS, lwS[:, bs], pw, pw[0:64, 0:BW], AF.Sigmoid, bias=rvec[:, vb_ + 3:vb_ + 4], extra_reads=[rvec])
                pa = pp()
                P.mm(pa, pa[0:64, 0:BW], aup, aup[:, hc], adp, adp[:, bs])
                P.act(aS, aS[:, bs], pa, pa[0:64, 0:BW], AF.Sigmoid, bias=rvec[:, vb_ + 4:vb_ + 5], extra_reads=[rvec])
                pg = pp()
                for kt in range(4):
                    P.mm(pg, pg[0:64, 0:BW], gup, gup[:, kt, hc], sgd, sgd[:, kt, bs], start=(kt == 0), stop=(kt == 3))
                P.copy("act", gS, gS[:, bs], pg, pg[0:64, 0:BW])
            P.ts("dve", lwS, lwS[:, :], lwS, lwS[:, :], -math.exp(-0.5), None, ALU.mult)
            P.ts("dve", kkS, kkS[:, :], kS, kS[:, :], rvec[:, vb_ + 5:vb_ + 6], None, ALU.mult, extra_reads=[rvec])
            P.act(t1, t1[:, :], kkS, kkS[:, :], AF.Square)
            for b0 in range(0, SEG, BW):
                bs = slice(b0, b0 + BW)
                pn = pp()
                P.mm(pn, pn[0:64, 0:BW], ones, ones[:, :], t1, t1[:, bs])
                rs = tb("rs", (64, BW))
                P.act(rs, rs[:, :], pn, pn[0:64, 0:BW], AF.Sqrt, bias=eps12[:, 0:1], extra_reads=[eps12])
                P.op("dve", lambda en, rs=rs: en.reciprocal(out=rs[:, :], in_=rs[:, :]), reads=[rs], writes=[rs])
                P.tt("dve", kkS, kkS[:, bs], kkS, kkS[:, bs], rs, rs[:, :], ALU.mult)
            P.ts("dve", t1, t1[:, :], aS, aS[:, :], rvec[:, vb_ + 6:vb_ + 7], omk[:, h:h + 1], ALU.mult, ALU.add,
                 extra_reads=[rvec, omk])
            P.tt("dve", kS, kS[:, :], kS, kS[:, :], t1, t1[:, :], ALU.mult)
            P.tt("pool", bS, bS[:, :], kkS, kkS[:, :], aS, aS[:, :], ALU.mult)
            P.ts("pool", aS, aS[:, :], kkS, kkS[:, :], -1.0, None, ALU.mult)
            P.stt("dve", t1, t1[:, :], rS, rS[:, :], rvec[:, vb_ + 7:vb_ + 8], kS, kS[:, :], ALU.mult, ALU.mult,
                  extra_reads=[rvec])
            for b0 in range(0, SEG, BW):
                bs = slice(b0, b0 + BW)
                pn = pp()
                P.mm(pn, pn[0:64, 0:BW], ones, ones[:, :], t1, t1[:, bs])
                P.tt("dve", bon, bon[:, bs], pn, pn[0:64, 0:BW], vS, vS[:, bs], ALU.mult)
    def pipe(g):
        NCH = SEG // C
        for ci in range(NCH + 1):
            gens = []
            for h in range(NH):
                if ci >= 1:
                    gens.append(chunkB(h, ci - 1))
            for h in range(NH):
                if ci < NCH:
                    gens.append(chunkA(h, ci))
            yield from run_rr_gen(gens)

    def post(g):
        t0 = g * SEG
        for h in range(NH):
            hb = HB[h]
            yS, gS, bon, obuf = hb["yS"], hb["gS"], hb["bon"], hb["ob"]
            vb_ = h * 10
            for b0 in range(0, SEG, BW):
                bs = slice(b0, b0 + BW)
                pm = pp()
                P.mm(pm, pm[0:64, 0:BW], ones, ones[:, :], yS, yS[:, bs])
                P.stt("dve", t1, t1[:, bs], pm, pm[0:64, 0:BW], -1.0 / 64, yS, yS[:, bs], ALU.mult, ALU.add)
                sq = tb("sq", (64, BW))
                P.act(sq, sq[:, :], t1, t1[:, bs], AF.Square)
                pv_ = pp()
                P.mm(pv_, pv_[0:64, 0:BW], ones, ones[:, :], sq, sq[:, :])
                rs = tb("rs", (64, BW))
                P.act(rs, rs[:, :], pv_, pv_[0:64, 0:BW], AF.Sqrt, bias=epsgn[:, 0:1], scale=1.0 / 64, extra_reads=[epsgn])
                P.op("dve", lambda en, rs=rs: en.reciprocal(out=rs[:, :], in_=rs[:, :]), reads=[rs], writes=[rs])
                P.tt("dve", t1, t1[:, bs], t1, t1[:, bs], rs, rs[:, :], ALU.mult)
                P.ts("dve", t1, t1[:, bs], t1, t1[:, bs], rvec[:, vb_ + 8:vb_ + 9], rvec[:, vb_ + 9:vb_ + 10], ALU.mult, ALU.add,
                     extra_reads=[rvec])
                P.tt("pool", t1, t1[:, bs], t1, t1[:, bs], bon, bon[:, bs], ALU.add)
                P.tt("pool", obuf, obuf[:, bs], t1, t1[:, bs], gS, gS[:, bs], ALU.mult)
            P.dma("pool", mix, mix[384 + 64 * h:384 + 64 * h + 64, t0:t0 + SEG], obuf, obuf[:, :])

    if not own:
        return dict(pre=pre, pipe=pipe, post=post, SEG=SEG)
    for g in range(NSEG):
        pre(g)
        run_rr([pipe(g)])
        post(g)
    P.pop()


def emit_swa(P, cfg, proj, mix, bm):
    S = cfg.S
    SEG = 2048
    NSEG = S // SEG
    scale = 128 ** -0.5
    qr, kr, vr = cfg.o_a, cfg.o_a + 128, cfg.o_a + 256
    P.push()
    ident = P.sb("ident", [128, 128], F32)
    make_identity(P, ident)
    ones_c = P.sb("ones_c", [128, 1], F32)
    P.memset("dve", ones_c, ones_c[:, :], 1.0)
    ones_r = P.sb("ones_r", [1, 128], F32)
    P.memset("dve", ones_r, ones_r[:, :], 1.0)
    ones_b = P.sb("ones_b", [128, 128], BF16)
    P.memset("dve", ones_b, ones_b[:, :], 1.0)
    bms = P.sb("bms", [128, 6, 128], F32)
    for p in range(3):
        for t in range(2):
            P.dma("sp", bms, bms[:, p * 2 + t, :], bm, bm[p, t, :, :])
    mx = P.sb("mx", [1, 2], F32)
    P.memset("dve", mx, mx[:, :], 0.0)
    P.push()
    ld = [P.sb("ld", [128, 512], F32) for _ in range(2)]
    sq = [P.sb("sq", [128, 512], F32) for _ in range(2)]
    nps = [P.ps("nps", [1, 512], F32) for _ in range(2)]
    bmx = P.sb("bmx", [1, 1], F32)
    it = 0
    for which, row in enumerate((qr, kr)):
        for t0 in range(0, S, 512):
            a, q, pb = ld[it % 2], sq[it % 2], nps[it % 2]
            it += 1
            P.dma("sp", a, a[:, :], proj, proj[row:row + 128, t0:t0 + 512])
            P.act(q, q[:, :], a, a[:, :], AF.Square)
            P.mm(pb, pb[:, :], ones_c, ones_c[:, :], q, q[:, :])
            P.op("dve", lambda en, pb=pb: en.reduce_max(out=bmx[:, :], in_=pb[:, :], axis=AX.X), reads=[pb], writes=[bmx])
            P.tt("dve", mx, mx[:, which:which + 1], mx, mx[:, which:which + 1], bmx, bmx[:, :], ALU.max)
    P.pop()
    negm1 = P.sb("negm1", [1, 1], F32)
    P.tt("dve", negm1, negm1[:, :], mx, mx[:, 0:1], mx, mx[:, 1:2], ALU.add)
    P.ts("dve", negm1, negm1[:, :], negm1, negm1[:, :], -0.5 * scale, None, ALU.mult)
    negm = P.sb("negm", [128, 1], F32)
    P.push()
    pb = P.ps("nmps", [128, 1], F32)
    P.mm(pb, pb[:, :], ones_r, ones_r[:, :], negm1, negm1[:, :])
    P.copy("dve", negm, negm[:, :], pb, pb[:, :])
    P.pop()
    qT = P.sb("qT", [128, SEG], BF16)
    kT = [P.sb("kT", [128, SEG], BF16) for _ in range(2)]
    vT = [P.sb("vT", [128, SEG], F32) for _ in range(2)]
    ldq = P.sb("ldq", [128, SEG], F32)
    NUM = P.sb("NUM", [128, SEG], F32)
    DEN = P.sb("DEN", [128, SEG], F32)
    ob = P.sb("ob", [128, SEG], BF16)
    Vt = [P.sb("Vt", [128, 128], BF16) for _ in range(3)]
    tmp = [P.sb("tmp", [128, 128], F32) for _ in range(3)]
    pT = [P.sb("pT", [128, 128], BF16) for _ in range(3)]
    trp = [P.ps("trp", [128, 128], F32) for _ in range(2)]
    sp_ = [P.ps("sps", [128, 128], F32) for _ in range(2)]
    nump = [P.ps("nump", [128, 128], F32) for _ in range(2)]
    denp = [P.ps("denp", [128, 128], F32) for _ in range(2)]
    vi = 0
    bi = 0
    for g in range(NSEG):
        t0 = g * SEG
        cur = g % 2
        P.dma("sp", ldq, ldq[:, :], proj, proj[qr:qr + 128, t0:t0 + SEG])
        P.act(qT, qT[:, :], ldq, ldq[:, :], AF.Copy, scale=scale)
        P.dma("sp", ldq, ldq[:, :], proj, proj[kr:kr + 128, t0:t0 + SEG])
        P.copy("pool", kT[cur], kT[cur][:, :], ldq, ldq[:, :])
        P.dma("sp", vT[cur], vT[cur][:, :], proj, proj[vr:vr + 128, t0:t0 + SEG])
        P.memset("pool", NUM, NUM[:, :], 0.0)
        P.memset("pool", DEN, DEN[:, :], 0.0)
        for p, d in enumerate(SWA_DIL):
            span = 128 * d
            for nbl in range(SEG // span):
                for r in range(d):
                    base = nbl * span + r
                    qsl = slice(base, base + 127 * d + 1, d)
                    tiles = []
                    if nbl >= 1:
                        tiles.append((0, cur, slice(base - span, base - span + 127 * d + 1, d)))
                    elif g >= 1:
                        pbase = SEG - span + r
                        tiles.append((0, 1 - cur, slice(pbase, pbase + 127 * d + 1, d)))
                    tiles.append((1, cur, qsl))
                    np_, dp_ = nump[bi % 2], denp[bi % 2]
                    bi += 1
                    for ti, (tt_, ring, ksl) in enumerate(tiles):
                        tp, sps, V, tm, pt = trp[vi % 2], sp_[vi % 2], Vt[vi % 3], tmp[vi % 3], pT[vi % 3]
                        vi += 1
                        P.tr(tp, tp[:, :], vT[ring], vT[ring][:, ksl], ident, ident[:, :])
                        P.copy("pool", V, V[:, :], tp, tp[:, :]) if False else P.copy("act", V, V[:, :], tp, tp[:, :])
                        P.mm(sps, sps[:, :], kT[ring], kT[ring][:, ksl], qT, qT[:, qsl])
                        P.tt("dve", tm, tm[:, :], sps, sps[:, :], bms, bms[:, p * 2 + tt_, :], ALU.add)
                        P.act(pt, pt[:, :], tm, tm[:, :], AF.Exp, bias=negm[:, 0:1], extra_reads=[negm])
                        first, last = ti == 0, ti == len(tiles) - 1
                        P.mm(np_, np_[:, :], V, V[:, :], pt, pt[:, :], start=first, stop=last)
                        P.mm(dp_, dp_[:, :], ones_b, ones_b[:, :], pt, pt[:, :], start=first, stop=last)
                    P.tt("dve", NUM, NUM[:, qsl], NUM, NUM[:, qsl], np_, np_[:, :], ALU.add)
                    P.tt("dve", DEN, DEN[:, qsl], DEN, DEN[:, qsl], dp_, dp_[:, :], ALU.add)
        P.op("dve", lambda en: en.reciprocal(out=DEN[:, :], in_=DEN[:, :]), reads=[DEN], writes=[DEN])
        P.tt("dve", ob, ob[:, :], NUM, NUM[:, :], DEN, DEN[:, :], ALU.mult)
        P.dma("pool", mix, mix[256:384, t0:t0 + SEG], ob, ob[:, :])
    P.pop()


def make_identity(P, ident):
    P.dma("sp", ident, ident[:, :], P.cst, P.cst[0, :, :])


def host_consts():
    c = np.zeros((4, 128, 128), np.float32)
    j = np.arange(128)[:, None]
    i = np.arange(128)[None, :]
    c[0] = (i == j)
    c[1] = (j <= i)
    c[2] = (j < i)
    c[3] = ((i // 64) == (j // 64))
    return c
```

```python
import math
from contextlib import ExitStack

import numpy as np
import ml_dtypes

import concourse.bass as bass
import concourse.mybir as mybir
from concourse.bass_utils import run_bass_kernel_spmd

F32 = mybir.dt.float32
BF16 = mybir.dt.bfloat16
AF = mybir.ActivationFunctionType
ALU = mybir.AluOpType
AX = mybir.AxisListType
NPBF = ml_dtypes.bfloat16


class Cfg:
    def __init__(self, D=4096, S=16384, NR=8, DFF=11008):
        self.D, self.S, self.NR, self.DFF = D, S, NR, DFF
        self.GD, self.GW = 128, 3 * D // 8
        self.GH = self.GW // 128
        self.AD, self.AW = 128, D // 4
        self.AH = self.AW // 128
        self.RD = 64
        self.RW = D - self.GW - self.AW
        self.RH = self.RW // 64
        self.LW, self.LA, self.LG = 128, 128, 480
        self.GDN_IN = 4 * self.GW + 2 * self.GH
        self.SWA_IN = 3 * self.AW
        self.RWKV_IN = 3 * self.RW + self.LW + self.LA + self.LG
        self.IN_W = self.GDN_IN + self.SWA_IN + self.RWKV_IN
        self.GS = -(-self.GH // NR)
        assert self.AH == NR and self.RH == 3 * NR and self.GS == 2
        self.DC = D // NR
        self.FC = DFF // NR
        self.RC = 3 * 64
        self.MIXC = self.GS * 128 + 128 + self.RC
        self.o_g = 0
        self.o_a = 512 * self.GS
        self.o_r = self.o_a + 384
        self.o_l = self.o_r + 3 * self.RC
        self.o_s = self.o_l + self.LW + self.LA + self.LG
        self.PC = self.o_s + 2 * self.GS


class Sem:
    def __init__(self, h, name):
        self.h, self.name, self.cnt = h, name, 0


class Buf:
    def __init__(self, name, t, space):
        self.name, self.t, self.space = name, t, space
        self.last_w = None
        self.readers = {}
        self.dsem = None
        self.last_w_dma = False

    def __getitem__(self, idx):
        return self.t[idx]

    def ap(self):
        return self.t.ap() if hasattr(self.t, "ap") else self.t[:]


class Prog:
    def __init__(self, nc):
        self.nc = nc
        self.root = ExitStack()
        self.stacks = [self.root]
        self.eng = {"pe": nc.tensor, "act": nc.scalar, "dve": nc.vector, "pool": nc.gpsimd, "sp": nc.sync}
        self.esem = {}
        for e in ("pe", "act", "dve", "pool"):
            self.esem[e] = Sem(self.root.enter_context(nc.semaphore("es_" + e)), e)
        self.seen = {e: {} for e in self.eng}
        self.free_dsems = []
        self.all_dsems = []
        self.scope_dsems = [[]]
        self.uid = 0
        self.pe_pending = False

    def _nm(self, name):
        self.uid += 1
        return f"{name}_{self.uid}"

    def sb(self, name, shape, dtype=F32):
        t = self.stacks[-1].enter_context(self.nc.sbuf_tensor(self._nm(name), list(shape), dtype))
        return Buf(name, t, "sb")

    def ps(self, name, shape, dtype=F32):
        t = self.stacks[-1].enter_context(self.nc.psum_tensor(self._nm(name), list(shape), dtype))
        return Buf(name, t, "ps")

    def dram(self, name, shape, dtype=F32, kind="Internal"):
        t = self.nc.dram_tensor(name, list(shape), dtype, kind=kind)
        return Buf(name, t, "dram")

    def _get_dsem(self, buf):
        if buf.dsem is None:
            if self.free_dsems:
                s = self.free_dsems.pop()
            else:
                s = Sem(self.root.enter_context(self.nc.semaphore(self._nm("ds"))), "ds")
                self.all_dsems.append(s)
            buf.dsem = s
            if buf.space != "dram":
                self.scope_dsems[-1].append(s)
        return buf.dsem

    def push(self):
        st = ExitStack()
        self.stacks.append(st)
        self.scope_dsems.append([])

    def pop(self):
        self.barrier()
        self.stacks.pop().close()
        self.free_dsems.extend(self.scope_dsems.pop())

    def barrier(self):
        toks = [(s, s.cnt) for s in self.esem.values() if s.cnt > 0]
        toks += [(s, 16 * s.cnt) for s in self.all_dsems if s.cnt > 0]
        for e in self.eng:
            self._wait(e, toks)

    def _wait(self, e, toks):
        seen = self.seen[e]
        own = self.esem.get(e)
        for s, v in toks:
            if e == "pe" and s is own:
                continue
            if seen.get(s, 0) < v:
                self.eng[e].wait_ge(s.h, v)
                seen[s] = v

    def _deps(self, reads, writes, dma_dst=None):
        toks = []
        for b in reads:
            if b.last_w is not None:
                toks.append(b.last_w)
        for b in writes:
            if b.last_w is not None and not (b is dma_dst and b.last_w_dma):
                toks.append(b.last_w)
            toks.extend(b.readers.items())
        return toks

    def _commit(self, tok, reads, writes, is_dma=False):
        for b in writes:
            b.last_w = tok
            b.last_w_dma = is_dma
            b.readers = {}
        for b in reads:
            s, v = tok
            if b.readers.get(s, 0) < v:
                b.readers[s] = v

    def op(self, e, fn, reads=(), writes=(), inc=True):
        self._wait(e, self._deps(reads, writes))
        ins = fn(self.eng[e])
        s = self.esem[e]
        if inc:
            s.cnt += 1
            ins.then_inc(s.h, 1)
            tok = (s, s.cnt)
        else:
            tok = (s, s.cnt + 1)
        self._commit(tok, reads, writes)
        return ins

    def dma(self, q, dst, dst_ap, src, src_ap):
        self._wait(q, self._deps([src], [dst], dma_dst=dst))
        s = self._get_dsem(dst)
        ins = self.eng[q].dma_start(out=dst_ap, in_=src_ap)
        s.cnt += 1
        ins.then_inc(s.h, 16)
        self._commit((s, 16 * s.cnt), [src], [dst], is_dma=True)
        return ins

    def finish(self, out_bufs):
        toks = [b.last_w for b in out_bufs if b.last_w is not None]
        self._wait("sp", toks)
        self.barrier()
        while len(self.stacks) > 1:
            self.stacks.pop().close()
        self.root.close()

    def mm(self, ps, ps_ap, a, a_ap, b, b_ap, start=True, stop=True, inc=None):
        if inc is None:
            inc = stop
        return self.op("pe", lambda e: e.matmul(ps_ap, a_ap, b_ap, start=start, stop=stop),
                       reads=[a, b], writes=[ps], inc=inc)

    def tr(self, ps, ps_ap, a, a_ap, ident, ident_ap):
        return self.op("pe", lambda e: e.transpose(ps_ap, a_ap, ident_ap), reads=[a, ident], writes=[ps])

    def act(self, out, out_ap, in_, in_ap, func, bias=None, scale=1.0, extra_reads=(), e="act", accum=None):
        kw = {}
        if bias is not None:
            kw["bias"] = bias
        if accum is not None:
            kw["accum_out"] = accum[1]
        w = [out] + ([accum[0]] if accum is not None else [])
        return self.op(e, lambda en: en.activation(out=out_ap, in_=in_ap, func=func, scale=scale, **kw),
                       reads=[in_] + list(extra_reads), writes=w)

    def ts(self, e, out, out_ap, in_, in_ap, s1, s2, op0, op1=None, extra_reads=()):
        if op1 is None:
            f = lambda en: en.tensor_scalar(out=out_ap, in0=in_ap, scalar1=s1, scalar2=None, op0=op0)
        else:
            f = lambda en: en.tensor_scalar(out=out_ap, in0=in_ap, scalar1=s1, scalar2=s2, op0=op0, op1=op1)
        return self.op(e, f, reads=[in_] + list(extra_reads), writes=[out])

    def tt(self, e, out, out_ap, a, a_ap, b, b_ap, op):
        return self.op(e, lambda en: en.tensor_tensor(out=out_ap, in0=a_ap, in1=b_ap, op=op),
                       reads=[a, b], writes=[out])

    def stt(self, e, out, out_ap, a, a_ap, scalar, b, b_ap, op0, op1, extra_reads=()):
        return self.op(e, lambda en: en.scalar_tensor_tensor(out=out_ap, in0=a_ap, scalar=scalar, in1=b_ap,
                                                              op0=op0, op1=op1),
                       reads=[a, b] + list(extra_reads), writes=[out])

    def copy(self, e, out, out_ap, in_, in_ap):
        if e == "act":
            return self.op(e, lambda en: en.copy(out=out_ap, in_=in_ap), reads=[in_], writes=[out])
        return self.op(e, lambda en: en.tensor_copy(out=out_ap, in_=in_ap), reads=[in_], writes=[out])

    def memset(self, e, out, out_ap, val):
        return self.op(e, lambda en: en.memset(out_ap, val), reads=[], writes=[out])


def ktiles(K):
    return [(k0, min(128, K - k0)) for k0 in range(0, K, 128)]


def linear(P, xT, K, S, w, Mc, m_tiles, TB, GC, epilogue, pre_group=None, n_ps=2, ps_bufs=None):
    kts = ktiles(K)
    KT = len(kts)
    groups, cur, cw = [], [], 0
    for mi, (m0, msz) in enumerate(m_tiles):
        if cw + msz > GC and cur:
            groups.append(cur)
            cur, cw = [], 0
        cur.append((mi, m0, msz))
        cw += msz
    if cur:
        groups.append(cur)
    P.push()
    wbf = P.sb("wbf", [128, KT, GC], BF16)
    KC = 4
    wst = [P.sb("wst", [128, KC, GC], F32) for _ in range(2)]
    xb = [P.sb("xblk", [128, KT, TB], BF16) for _ in range(2)]
    if ps_bufs is None:
        ps_bufs = [P.ps("linps", [128, 512], F32) for _ in range(n_ps)]
    xap = xT.ap()
    wap = w.ap()
    nblk = S // TB
    it = 0
    pi = 0
    for grp in groups:
        g0 = grp[0][1]
        gw = sum(g[2] for g in grp)
        ci = 0
        for kc0 in range(0, KT, KC):
            kc1 = min(KT, kc0 + KC)
            st = wst[ci % 2]
            for kt in range(kc0, kc1):
                k0, ksz = kts[kt]
                P.dma("sp", st, st[0:ksz, kt - kc0, 0:gw], w, wap[k0:k0 + ksz, g0:g0 + gw])
            full = [kt for kt in range(kc0, kc1) if kts[kt][1] == 128]
            part = [kt for kt in range(kc0, kc1) if kts[kt][1] != 128]
            ce = "dve" if ci % 2 == 0 else "pool"
            if full:
                a, b = full[0], full[-1] + 1
                P.copy(ce, wbf, wbf[:, a:b, 0:gw], st, st[:, a - kc0:b - kc0, 0:gw])
            for kt in part:
                ksz = kts[kt][1]
                P.copy(ce, wbf, wbf[0:ksz, kt, 0:gw], st, st[0:ksz, kt - kc0, 0:gw])
            ci += 1
        if pre_group is not None:
            pre_group(grp)
        for bi in range(nblk):
            t0 = bi * TB
            x = xb[it % 2]
            it += 1
            for kt, (k0, ksz) in enumerate(kts):
                pass
            nfull = sum(1 for k in kts if k[1] == 128)
            for ka in range(0, nfull, 8):
                kb = min(nfull, ka + 8)
                P.dma("sp", x, x[:, ka:kb, :], xT,
                      xap[ka * 128:kb * 128, t0:t0 + TB].rearrange("(kt p) s -> p kt s", p=128))
            if nfull < KT:
                k0, ksz = kts[-1]
                P.dma("sp", x, x[0:ksz, KT - 1, :], xT, xap[k0:k0 + ksz, t0:t0 + TB])
            for (mi, m0, msz) in grp:
                pb = ps_bufs[pi % len(ps_bufs)]
                pi += 1
                for kt, (k0, ksz) in enumerate(kts):
                    P.mm(pb, pb[0:msz, 0:TB], wbf, wbf[0:ksz, kt, m0 - g0:m0 - g0 + msz], x, x[0:ksz, kt, :],
                         start=(kt == 0), stop=(kt == KT - 1))
                epilogue(mi, (m0, msz), t0, pb)
    P.pop()


def load_rstd(P, part, NR, S, D, eps):
    rstd = P.sb("rstd", [128, S], F32)
    P.push()
    ones = P.sb("ones", [NR, 128], F32)
    P.memset("dve", ones, ones[:, :], 1.0)
    pt = P.sb("part", [NR, S], F32)
    P.dma("sp", pt, pt[:, :], part, part.ap())
    ps = [P.ps("rsps", [128, 512], F32) for _ in range(2)]
    for i, t0 in enumerate(range(0, S, 512)):
        pb = ps[i % 2]
        P.mm(pb, pb[:, :], ones, ones[:, :], pt, pt[:, t0:t0 + 512])
        P.act(rstd, rstd[:, t0:t0 + 512], pb, pb[:, :], AF.Sqrt, bias=eps_ap(P, eps), scale=1.0 / D,
              extra_reads=[P.consts["eps"]])
        P.op("dve", lambda en, t0=t0: en.reciprocal(out=rstd[:, t0:t0 + 512], in_=rstd[:, t0:t0 + 512]),
             reads=[rstd], writes=[rstd])
    P.pop()
    return rstd


def eps_ap(P, eps):
    return P.consts["eps"][:, 0:1]


def setup_consts(P, eps):
    P.consts = {}
    e = P.sb("epsc", [128, 1], F32)
    P.memset("dve", e, e[:, :], eps)
    P.consts["eps"] = e


def norm_prep_epilogue(P, cfg, xnew, xn_ap, msz, m0, t0, TB, normw, x_out, xw_out, sq_ps, first, last, part_sb):
    pass


def build_prep(cfg):
    nc = bass.Bass("TRN2", target_bir_lowering=False)
    P = Prog(nc)
    DC, S = cfg.DC, cfg.S
    x = P.dram("x", [DC, S], F32, "ExternalInput")
    nw = P.dram("nw", [128, DC // 128], F32, "ExternalInput")
    xw = P.dram("xw", [DC, S], BF16, "ExternalOutput")
    part = P.dram("part", [1, S], F32, "ExternalOutput")
    emit_norm_prep(P, cfg, x, nw, xw, part)
    P.finish([xw, part])
    return nc


def emit_norm_prep(P, cfg, x, nw, xw, part):
    DC, S = cfg.DC, cfg.S
    MT = DC // 128
    TB = 512
    P.push()
    nws = P.sb("nws", [128, MT], F32)
    P.dma("sp", nws, nws[:, :], nw, nw.ap())
    ones = P.sb("ones1", [128, 1], F32)
    P.memset("dve", ones, ones[:, :], 1.0)
    xs = [P.sb("xs", [128, MT, TB], F32) for _ in range(2)]
    sq = [P.sb("sq", [128, MT, TB], F32) for _ in range(2)]
    xo = [P.sb("xo", [128, MT, TB], BF16) for _ in range(2)]
    po = [P.sb("po", [1, TB], F32) for _ in range(2)]
    pss = [P.ps("pss", [1, TB], F32) for _ in range(2)]
    xa = x.ap().rearrange("(m p) s -> p m s", p=128)
    xwa = xw.ap().rearrange("(m p) s -> p m s", p=128)
    for i, t0 in enumerate(range(0, S, TB)):
        a, q, o, pb, pp = xs[i % 2], sq[i % 2], xo[i % 2], pss[i % 2], po[i % 2]
        P.dma("sp", a, a[:, :, :], x, xa[:, :, t0:t0 + TB])
        P.act(q, q[:, :, :], a, a[:, :, :], AF.Square)
        for m in range(MT):
            P.mm(pb, pb[:, :], ones, ones[:, :], q, q[:, m, :], start=(m == 0), stop=(m == MT - 1))
            P.ts("dve" if m % 2 == 0 else "pool", o, o[:, m, :], a, a[:, m, :], nws[:, m:m + 1], None, ALU.mult,
                 extra_reads=[nws])
        P.copy("dve", pp, pp[:, :], pb, pb[:, :])
        P.dma("pool", xw, xwa[:, :, t0:t0 + TB], o, o[:, :, :])
        P.dma("pool", part, part[0:1, t0:t0 + TB], pp, pp[:, :])
    P.pop()


def build_res(cfg, K, final=False):
    nc = bass.Bass("TRN2", target_bir_lowering=False)
    P = Prog(nc)
    DC, S = cfg.DC, cfg.S
    MT = DC // 128
    xin = P.dram("xin", [K, S], BF16, "ExternalInput")
    w = P.dram("w", [K, DC], F32, "ExternalInput")
    x = P.dram("x", [DC, S], F32, "ExternalInput")
    nw = P.dram("nw", [128, DC // 128], F32, "ExternalInput")
    xo = P.dram("xo", [DC, S], F32, "ExternalOutput")
    xw = P.dram("xw", [DC, S], BF16, "ExternalOutput")
    part = P.dram("part", [1, S], F32, "ExternalOutput")
    big = K > 6000
    TB = 256 if big else 512
    GC = 256 if big else 512
    P.push()
    nws = P.sb("nws", [128, MT], F32)
    P.dma("sp", nws, nws[:, :], nw, nw.ap())
    xr = [P.sb("xr", [128, TB], F32) for _ in range(3)]
    xn = [P.sb("xn", [128, TB], F32) for _ in range(3)]
    xb = [P.sb("xb", [128, TB], BF16) for _ in range(3)]
    cnt = [0]

    def epi(mi, mt, t0, pb):
        m0, msz = mt
        i = cnt[0] % 3
        cnt[0] += 1
        a, n, b = xr[i], xn[i], xb[i]
        P.dma("act", a, a[:, :], x, x[m0:m0 + msz, t0:t0 + TB])
        P.tt("dve", n, n[:, :], pb, pb[:, 0:TB], a, a[:, :], ALU.add)
        P.dma("pool", xo, xo[m0:m0 + msz, t0:t0 + TB], n, n[:, :])
        P.ts("pool", b, b[:, :], n, n[:, :], nws[:, mi:mi + 1], None, ALU.mult, extra_reads=[nws])
        P.dma("pool", xw, xw[m0:m0 + msz, t0:t0 + TB], b, b[:, :])

    linear(P, xin, K, S, w, DC, [(m * 128, 128) for m in range(MT)], TB, GC, epi)
    P.pop()
    emit_sumsq(P, cfg, xo, part)
    P.finish([xo, xw, part])
    return nc


def emit_sumsq(P, cfg, x, part):
    DC, S = cfg.DC, cfg.S
    MT = DC // 128
    TB = 512
    P.push()
    ones = P.sb("ones1", [128, 1], F32)
    P.memset("dve", ones, ones[:, :], 1.0)
    xs = [P.sb("xs", [128, MT, TB], F32) for _ in range(2)]
    sq = [P.sb("sq", [128, MT, TB], F32) for _ in range(2)]
    po = [P.sb("po", [1, TB], F32) for _ in range(2)]
    pss = [P.ps("pss", [1, TB], F32) for _ in range(2)]
    xa = x.ap().rearrange("(m p) s -> p m s", p=128)
    for i, t0 in enumerate(range(0, S, TB)):
        a, q, pb, pp = xs[i % 2], sq[i % 2], pss[i % 2], po[i % 2]
        P.dma("sp", a, a[:, :, :], x, xa[:, :, t0:t0 + TB])
        P.act(q, q[:, :, :], a, a[:, :, :], AF.Square)
        for m in range(MT):
            P.mm(pb, pb[:, :], ones, ones[:, :], q, q[:, m, :], start=(m == 0), stop=(m == MT - 1))
        P.copy("dve", pp, pp[:, :], pb, pb[:, :])
        P.dma("pool", part, part[0:1, t0:t0 + TB], pp, pp[:, :])
    P.pop()


def build_ffn(cfg):
    nc = bass.Bass("TRN2", target_bir_lowering=False)
    P = Prog(nc)
    D, S, FC, NR = cfg.D, cfg.S, cfg.FC, cfg.NR
    xin = P.dram("xin", [D, S], BF16, "ExternalInput")
    parts = P.dram("parts", [NR, S], F32, "ExternalInput")
    wg = P.dram("wg", [D, FC], F32, "ExternalInput")
    wu = P.dram("wu", [D, FC], F32, "ExternalInput")
    cw = P.dram("cw", [FC, 4], F32, "ExternalInput")
    out = P.dram("out", [FC, S], BF16, "ExternalOutput")
    setup_consts(P, 1e-6)
    rstd = load_rstd(P, parts, NR, S, D, 1e-6)
    mts = ktiles(FC)
    gsc = P.dram("gsc", [FC, S], F32)
    TB = 512
    P.push()
    cws = P.sb("cws", [128, len(mts), 4], F32)
    for mi, (m0, msz) in enumerate(mts):
        P.dma("sp", cws, cws[0:msz, mi, :], cw, cw[m0:m0 + msz, :])
    gbuf = [P.sb("gbuf", [128, TB + 2], F32) for _ in range(2)]
    acc = [P.sb("acc", [128, TB], F32) for _ in range(2)]
    cnt = [0]
    halo = {}

    def epi_gate(mi, mt, t0, pb):
        m0, msz = mt
        i = cnt[0] % 2
        cnt[0] += 1
        g, a = gbuf[i], acc[i]
        prev = halo.get(mi)
        if t0 == 0:
            P.memset("pool", g, g[:, 0:2], 0.0)
        else:
            pg = prev
            P.copy("pool", g, g[0:msz, 0:2], pg, pg[0:msz, TB:TB + 2])
        P.tt("dve", g, g[0:msz, 2:TB + 2], pb, pb[0:msz, 0:TB], rstd, rstd[0:msz, t0:t0 + TB], ALU.mult)
        halo[mi] = g
        P.ts("dve", a, a[0:msz, :], g, g[0:msz, 0:TB], cws[0:msz, mi, 0:1], cws[0:msz, mi, 3:4], ALU.mult, ALU.add,
             extra_reads=[cws])
        P.stt("dve", a, a[0:msz, :], g, g[0:msz, 1:TB + 1], cws[0:msz, mi, 1:2], a, a[0:msz, :], ALU.mult, ALU.add,
              extra_reads=[cws])
        P.stt("dve", a, a[0:msz, :], g, g[0:msz, 2:TB + 2], cws[0:msz, mi, 2:3], a, a[0:msz, :], ALU.mult, ALU.add,
              extra_reads=[cws])
        P.act(a, a[0:msz, :], a, a[0:msz, :], AF.Silu)
        P.dma("pool", gsc, gsc[m0:m0 + msz, t0:t0 + TB], a, a[0:msz, :])

    halo_sb = P.sb("halo", [128, len(mts), 2], F32)

    def epi_gate2(mi, mt, t0, pb):
        m0, msz = mt
        i = cnt[0] % 2
        cnt[0] += 1
        g, a = gbuf[i], acc[i]
        if t0 == 0:
            P.memset("pool", g, g[:, 0:2], 0.0)
        else:
            P.copy("pool", g, g[0:msz, 0:2], halo_sb, halo_sb[0:msz, mi, :])
        P.tt("dve", g, g[0:msz, 2:TB + 2], pb, pb[0:msz, 0:TB], rstd, rstd[0:msz, t0:t0 + TB], ALU.mult)
        P.copy("pool", halo_sb, halo_sb[0:msz, mi, :], g, g[0:msz, TB:TB + 2])
        P.ts("dve", a, a[0:msz, :], g, g[0:msz, 0:TB], cws[0:msz, mi, 0:1], cws[0:msz, mi, 3:4], ALU.mult, ALU.add,
             extra_reads=[cws])
        P.stt("dve", a, a[0:msz, :], g, g[0:msz, 1:TB + 1], cws[0:msz, mi, 1:2], a, a[0:msz, :], ALU.mult, ALU.add,
              extra_reads=[cws])
        P.stt("dve", a, a[0:msz, :], g, g[0:msz, 2:TB + 2], cws[0:msz, mi, 2:3], a, a[0:msz, :], ALU.mult, ALU.add,
              extra_reads=[cws])
        P.act(a, a[0:msz, :], a, a[0:msz, :], AF.Silu)
        P.dma("pool", gsc, gsc[m0:m0 + msz, t0:t0 + TB], a, a[0:msz, :])

    linear(P, xin, D, S, wg, FC, mts, TB, 512, epi_gate2)

    gl = [P.sb("gl", [128, TB], F32) for _ in range(2)]
    ub = [P.sb("ub", [128, TB], F32) for _ in range(2)]
    ob = [P.sb("ob", [128, TB], BF16) for _ in range(2)]

    def epi_up(mi, mt, t0, pb):
        m0, msz = mt
        i = cnt[0] % 2
        cnt[0] += 1
        g, u, o = gl[i], ub[i], ob[i]
        P.dma("act", g, g[0:msz, :], gsc, gsc[m0:m0 + msz, t0:t0 + TB])
        P.tt("dve", u, u[0:msz, :], pb, pb[0:msz, 0:TB], rstd, rstd[0:msz, t0:t0 + TB], ALU.mult)
        P.tt("pool", o, o[0:msz, :], u, u[0:msz, :], g, g[0:msz, :], ALU.mult)
        P.dma("pool", out, out[m0:m0 + msz, t0:t0 + TB], o, o[0:msz, :])

    linear(P, xin, D, S, wu, FC, mts, TB, 512, epi_up)
    P.pop()
    P.finish([out])
    return nc


def build_final(cfg):
    nc = bass.Bass("TRN2", target_bir_lowering=False)
    P = Prog(nc)
    DC, S, NR, D = cfg.DC, cfg.S, cfg.NR, cfg.D
    MT = DC // 128
    x = P.dram("x", [DC, S], F32, "ExternalInput")
    nw = P.dram("nw", [128, DC // 128], F32, "ExternalInput")
    parts = P.dram("parts", [NR, S], F32, "ExternalInput")
    out = P.dram("out", [DC, S], F32, "ExternalOutput")
    setup_consts(P, 1e-6)
    rstd = load_rstd(P, parts, NR, S, D, 1e-6)
    TB = 512
    P.push()
    nws = P.sb("nws", [128, MT], F32)
    P.dma("sp", nws, nws[:, :], nw, nw.ap())
    xs = [P.sb("xs", [128, TB], F32) for _ in range(3)]
    i = 0
    for m in range(MT):
        for t0 in range(0, S, TB):
            a = xs[i % 3]
            i += 1
            P.dma("sp", a, a[:, :], x, x[m * 128:(m + 1) * 128, t0:t0 + TB])
            P.stt("dve", a, a[:, :], a, a[:, :], nws[:, m:m + 1], rstd, rstd[:, t0:t0 + TB], ALU.mult, ALU.mult,
                  extra_reads=[nws])
            P.dma("pool", out, out[m * 128:(m + 1) * 128, t0:t0 + TB], a, a[:, :])
    P.pop()
    P.finish([out])
    return nc


def launch(nc, in_maps, n):
    res = run_bass_kernel_spmd(nc, in_maps, core_ids=list(range(n)))
    return res.results


def colslice(v, c, n):
    return np.ascontiguousarray(v[c * n:(c + 1) * n])


def run_model(cfg, inp, skip_mixers=False, debug=None):
    NR, S, D, DC, FC, DFF = cfg.NR, cfg.S, cfg.D, cfg.DC, cfg.FC, cfg.DFF
    f32 = np.float32
    L = inp["w_in"].shape[0]
    xT = np.ascontiguousarray(np.asarray(inp["x"], f32)[0].T)
    xs = [colslice(xT, c, DC) for c in range(NR)]
    nws = lambda v: [np.ascontiguousarray(colslice(np.asarray(v, f32), c, DC).reshape(DC // 128, 128).T) for c in range(NR)]

    nc_prep = build_prep(cfg)
    nw = nws(inp["attn_norm"][0])
    r = launch(nc_prep, [{"x": xs[c], "nw": nw[c]} for c in range(NR)], NR)
    xw = np.concatenate([r[c]["xw"] for c in range(NR)], 0)
    parts = np.concatenate([r[c]["part"] for c in range(NR)], 0)

    nc_attn = None if skip_mixers else build_attn(cfg)
    nc_res_a = build_res(cfg, NR * cfg.MIXC)
    nc_ffn = build_ffn(cfg)
    nc_res_f = build_res(cfg, DFF)
    for l in range(L):
        if skip_mixers:
            mixT = np.zeros((NR * cfg.MIXC, S), NPBF)
        else:
            maps = [attn_inputs(cfg, inp, l, c, xw, parts) for c in range(NR)]
            r = launch(nc_attn, maps, NR)
            if debug is not None:
                debug.append(r)
            mixT = np.concatenate([r[c]["mix"] for c in range(NR)], 0)
        wo = wout_shards(cfg, np.asarray(inp["w_out"][l], f32))
        nw = nws(inp["ffn_norm"][l])
        r = launch(nc_res_a, [{"xin": mixT, "w": wo[c], "x": xs[c], "nw": nw[c]} for c in range(NR)], NR)
        xs = [r[c]["xo"] for c in range(NR)]
        xw = np.concatenate([r[c]["xw"] for c in range(NR)], 0)
        parts = np.concatenate([r[c]["part"] for c in range(NR)], 0)

        wg = np.asarray(inp["w_ffn_gate"][l], f32)
        wu = np.asarray(inp["w_ffn_up"][l], f32)
        cwl = np.asarray(inp["ffn_conv"][l], f32)
        cb = np.asarray(inp["ffn_conv_b"][l], f32)
        maps = []
        for c in range(NR):
            sl = slice(c * FC, (c + 1) * FC)
            cw = np.ascontiguousarray(np.concatenate([cwl[:, sl].T, cb[sl][:, None]], 1))
            maps.append({"xin": xw, "parts": parts, "wg": np.ascontiguousarray(wg[:, sl]),
                         "wu": np.ascontiguousarray(wu[:, sl]), "cw": cw})
        r = launch(nc_ffn, maps, NR)
        actT = np.concatenate([r[c]["out"] for c in range(NR)], 0)

        wd = np.asarray(inp["w_ffn_down"][l], f32)
        nxt = inp["attn_norm"][l + 1] if l + 1 < L else inp["final_norm"]
        nw = nws(nxt)
        r = launch(nc_res_f, [{"xin": actT, "w": np.ascontiguousarray(wd[:, c * DC:(c + 1) * DC]), "x": xs[c],
                               "nw": nw[c]} for c in range(NR)], NR)
        xs = [r[c]["xo"] for c in range(NR)]
        xw = np.concatenate([r[c]["xw"] for c in range(NR)], 0)
        parts = np.concatenate([r[c]["part"] for c in range(NR)], 0)

    nc_fin = build_final(cfg)
    nw = nws(inp["final_norm"])
    r = launch(nc_fin, [{"x": xs[c], "nw": nw[c], "parts": parts} for c in range(NR)], NR)
    outT = np.concatenate([r[c]["out"] for c in range(NR)], 0)
    return np.ascontiguousarray(outT.T)[None].astype(np.float32)


def gdn_head(cfg, c, j):
    h = c + cfg.NR * j
    return h if h < cfg.GH else None


def wout_shards(cfg, wo):
    NR = cfg.NR
    rows = []
    for r in range(NR):
        for j in range(cfg.GS):
            h = gdn_head(cfg, r, j)
            rows.append(wo[h * 128:(h + 1) * 128] if h is not None else np.zeros((128, cfg.D), np.float32))
        rows.append(wo[cfg.GW + r * 128: cfg.GW + (r + 1) * 128])
        b = cfg.GW + cfg.AW + r * cfg.RC
        rows.append(wo[b:b + cfg.RC])
    full = np.concatenate(rows, 0)
    return [np.ascontiguousarray(full[:, c * cfg.DC:(c + 1) * cfg.DC]) for c in range(NR)]


def kernel(**inputs):
    return run_model(Cfg(), inputs)


def t5_bucket_np(dist):
    exact = 16
    d = np.maximum(dist, 1).astype(np.float32)
    log_b = exact + (np.log(d / np.float32(exact)) / np.float32(math.log(2048 / exact)) * np.float32(32 - exact)).astype(np.int32)
    return np.where(dist < exact, dist, np.minimum(log_b, 31))


SWA_DIL = (1, 4, 16)


def swa_bias_tiles(rel_bias_col):
    out = np.zeros((3, 2, 128, 128), np.float32)
    j = np.arange(128)[:, None]
    i = np.arange(128)[None, :]
    for p, d in enumerate(SWA_DIL):
        for t in range(2):
            steps = (i + 128 - j) if t == 0 else (i - j)
            valid = (steps >= 0) & (steps <= 128)
            idx = t5_bucket_np(np.maximum(steps, 0) * d)
            out[p, t] = np.where(valid, rel_bias_col[idx], np.float32(-30000.0))
    return out


def attn_inputs(cfg, inp, l, c, xw, parts):
    f32 = np.float32
    NR = cfg.NR
    W = np.asarray(inp["w_in"][l], f32)
    D = cfg.D
    cols = []
    GW, GH = cfg.GW, cfg.GH
    zc = np.zeros((D, 128), f32)
    for j in range(cfg.GS):
        h = gdn_head(cfg, c, j)
        for blk in range(4):
            cols.append(W[:, blk * GW + h * 128: blk * GW + (h + 1) * 128] if h is not None else zc)
    b = cfg.GDN_IN
    for blk in range(3):
        cols.append(W[:, b + blk * cfg.AW + c * 128: b + blk * cfg.AW + (c + 1) * 128])
    b = cfg.GDN_IN + cfg.SWA_IN
    for blk in range(3):
        cols.append(W[:, b + blk * cfg.RW + c * cfg.RC: b + blk * cfg.RW + (c + 1) * cfg.RC])
    cols.append(W[:, b + 3 * cfg.RW: b + 3 * cfg.RW + cfg.LW + cfg.LA + cfg.LG])
    z1 = np.zeros((D, 1), f32)
    for j in range(cfg.GS):
        h = gdn_head(cfg, c, j)
        cols.append(W[:, 4 * GW + h: 4 * GW + h + 1] if h is not None else z1)
        cols.append(W[:, 4 * GW + GH + h: 4 * GW + GH + h + 1] if h is not None else z1)
    w = np.ascontiguousarray(np.concatenate(cols, 1))
    assert w.shape[1] == cfg.PC
    m = {"xin": xw, "parts": parts, "w": w}
    m["cst"] = host_consts()
    m["bm"] = swa_bias_tiles(np.asarray(inp["rel_bias"], f32)[:, c])
    m.update(gdn_inputs(cfg, inp, l, c))
    m.update(rwkv_inputs(cfg, inp, l, c))
    return m


def gdn_inputs(cfg, inp, l, c):
    f32 = np.float32
    conv = np.asarray(inp["gdn_conv"][l], f32)
    gconv = np.zeros((128, cfg.GS * 3 * 4), f32)
    gsc = np.zeros((1, cfg.GS * 2), f32)
    for j in range(cfg.GS):
        h = gdn_head(cfg, c, j)
        if h is None:
            continue
        for b in range(3):
            gconv[:, (j * 3 + b) * 4:(j * 3 + b + 1) * 4] = conv[:, b * cfg.GW + h * 128: b * cfg.GW + (h + 1) * 128].T
        gsc[0, 2 * j] = np.asarray(inp["gdn_a_log"], f32)[l, h]
        gsc[0, 2 * j + 1] = np.asarray(inp["gdn_dt_bias"], f32)[l, h]
    gnw = np.ascontiguousarray(np.asarray(inp["gdn_norm"], f32)[l][:, None])
    return {"gconv": gconv, "gsc": gsc, "gnw": gnw}


def rwkv_inputs(cfg, inp, l, c):
    f32 = np.float32
    RW, RC = cfg.RW, cfg.RC
    ch = slice(c * RC, (c + 1) * RC)
    mu = np.asarray(inp["rwkv_mu"], f32)[l]
    vec = np.zeros((64, 3, 10), f32)
    srcs = [mu[0:RW][ch], mu[RW:2 * RW][ch], mu[2 * RW:3 * RW][ch],
            np.asarray(inp["rwkv_w0"], f32)[l][ch], np.asarray(inp["rwkv_a0"], f32)[l][ch],
            np.asarray(inp["rwkv_k_k"], f32)[l][ch], np.asarray(inp["rwkv_k_a"], f32)[l][ch],
            np.asarray(inp["rwkv_r_k"], f32)[l].reshape(-1)[ch],
            np.asarray(inp["rwkv_ln_w"], f32)[l][ch], np.asarray(inp["rwkv_ln_b"], f32)[l][ch]]
    for i, v in enumerate(srcs):
        vec[:, :, i] = v.reshape(3, 64).T
    mul = np.zeros((128, 6), f32)
    ml = mu[3 * RW:]
    for t in range(6):
        seg = ml[t * 128:(t + 1) * 128]
        mul[:len(seg), t] = seg
    gup = np.zeros((512, RC), f32)
    gup[:cfg.LG] = np.asarray(inp["rwkv_g_up"], f32)[l][:, ch]
    return {"rvec": np.ascontiguousarray(vec.reshape(64, 30)), "rmul": mul,
            "rwup": np.ascontiguousarray(np.asarray(inp["rwkv_w_up"], f32)[l][:, ch]),
            "raup": np.ascontiguousarray(np.asarray(inp["rwkv_a_up"], f32)[l][:, ch]),
            "rgup": gup}


def build_attn(cfg, parts_enabled=("swa", "gdn", "rwkv")):
    nc = bass.Bass("TRN2", target_bir_lowering=False)
    P = Prog(nc)
    D, S, NR, PC = cfg.D, cfg.S, cfg.NR, cfg.PC
    xin = P.dram("xin", [D, S], BF16, "ExternalInput")
    parts = P.dram("parts", [NR, S], F32, "ExternalInput")
    w = P.dram("w", [D, PC], F32, "ExternalInput")
    bm = P.dram("bm", [3, 2, 128, 128], F32, "ExternalInput")
    P.cst = P.dram("cst", [4, 128, 128], F32, "ExternalInput")
    P.rin = {"rvec": P.dram("rvec", [64, 30], F32, "ExternalInput"),
             "rmul": P.dram("rmul", [128, 6], F32, "ExternalInput"),
             "rwup": P.dram("rwup", [128, cfg.RC], F32, "ExternalInput"),
             "raup": P.dram("raup", [128, cfg.RC], F32, "ExternalInput"),
             "rgup": P.dram("rgup", [512, cfg.RC], F32, "ExternalInput")}
    P.gin = {"gconv": P.dram("gconv", [128, cfg.GS * 12], F32, "ExternalInput"),
             "gsc": P.dram("gsc", [1, cfg.GS * 2], F32, "ExternalInput"),
             "gnw": P.dram("gnw", [128, 1], F32, "ExternalInput")}
    mix = P.dram("mix", [cfg.MIXC, S], BF16, "ExternalOutput")
    proj = P.dram("proj", [PC, S], F32)
    setup_consts(P, 1e-6)
    P.push()
    rstd = load_rstd(P, parts, NR, S, D, 1e-6)
    TB = 512
    pj = [P.sb("pj", [128, TB], F32) for _ in range(3)]
    cnt = [0]

    def epi(mi, mt, t0, pb):
        m0, msz = mt
        a = pj[cnt[0] % 3]
        cnt[0] += 1
        P.tt("dve", a, a[0:msz, :], pb, pb[0:msz, 0:TB], rstd, rstd[0:msz, t0:t0 + TB], ALU.mult)
        P.dma("pool", proj, proj[m0:m0 + msz, t0:t0 + TB], a, a[0:msz, :])

    linear(P, xin, D, S, w, PC, ktiles(PC), TB, 512, epi)
    P.pop()
    if "gdn" in parts_enabled and "rwkv" in parts_enabled and COSCHED:
        emit_gdn_rwkv(P, cfg, proj, mix)
    else:
        emit_gdn(P, cfg, proj, mix) if "gdn" in parts_enabled else emit_zero_rows(P, cfg, mix, 0, 256)
        emit_rwkv(P, cfg, proj, mix) if "rwkv" in parts_enabled else emit_zero_rows(P, cfg, mix, 384, 192)
    emit_swa(P, cfg, proj, mix, bm) if "swa" in parts_enabled else emit_zero_rows(P, cfg, mix, 256, 128)
    P.finish([mix])
    return nc


COSCHED = False


def emit_gdn_rwkv(P, cfg, proj, mix):
    S = cfg.S
    GSEG = min(512, S)
    RSEG = min(256, S)
    P.push()
    sh = mixer_shared(P, 2, 6)
    gd = emit_gdn(P, cfg, proj, mix, shared=sh, SEG=GSEG)
    rw = emit_rwkv(P, cfg, proj, mix, shared=sh, SEG=RSEG)
    nsub = GSEG // RSEG

    def rw_super(G):
        for k in range(nsub):
            g = G * nsub + k
            rw["pre"](g)
            yield
            yield from rw["pipe"](g)
            rw["post"](g)
            yield

    for G in range(S // GSEG):
        gd["pre"](G)
        run_rr([gd["pipe"](G), rw_super(G)])
        gd["post"](G)
    P.pop()


def emit_zero_rows(P, cfg, mix, r0, n):
    P.push()
    z = P.sb("z", [128, 2048], BF16)
    P.memset("dve", z, z[:, :], 0.0)
    for a in range(r0, r0 + n, 128):
        m = min(128, r0 + n - a)
        for t0 in range(0, cfg.S, 2048):
            P.dma("pool", mix, mix[a:a + m, t0:t0 + 2048], z, z[0:m, :])
    P.pop()


def mixer_shared(P, nburst, nchain):
    sh = {}
    ident = P.sb("ident", [128, 128], F32)
    make_identity(P, ident)
    Ui = P.sb("Ui", [128, 128], F32)
    P.dma("sp", Ui, Ui[:, :], P.cst, P.cst[1, :, :])
    Us = P.sb("Us", [128, 128], F32)
    P.dma("sp", Us, Us[:, :], P.cst, P.cst[2, :, :])
    sh["ident"], sh["Ui"], sh["Us"] = ident, Ui, Us
    burst = [P.ps("bps", [128, 512], F32) for _ in range(nburst)]
    pfree = [P.ps("cps", [128, 512], F32) for _ in range(nchain)]

    def pp():
        b = burst.pop(0)
        burst.append(b)
        return b

    def get():
        while not pfree:
            yield
        return pfree.pop(0)

    def rel(b):
        pfree.append(b)

    sh["pp"], sh["get"], sh["rel"] = pp, get, rel
    return sh


def run_rr_gen(gens):
    gens = list(gens)
    while gens:
        nxt = []
        for g in gens:
            try:
                next(g)
                nxt.append(g)
            except StopIteration:
                pass
        gens = nxt
        yield


def emit_gdn(P, cfg, proj, mix, shared=None, SEG=None):
    S = cfg.S
    if SEG is None:
        SEG = 1024 if S >= 1024 else S
    NSEG = S // SEG
    C = 128
    NSL = cfg.GS
    own = shared is None
    if own:
        P.push()
        shared = mixer_shared(P, 1, 7)
    ident, Ui, Us = shared["ident"], shared["Ui"], shared["Us"]
    pp, get, rel = shared["pp"], shared["get"], shared["rel"]
    ones = P.sb("ones", [128, 128], F32)
    P.memset("dve", ones, ones[:, :], 1.0)
    gconv = P.sb("gconv", [128, cfg.GS * 12], F32)
    P.dma("sp", gconv, gconv[:, :], P.gin["gconv"], P.gin["gconv"].ap())
    gsc = P.sb("gsc", [1, cfg.GS * 2], F32)
    P.dma("sp", gsc, gsc[:, :], P.gin["gsc"], P.gin["gsc"].ap())
    gnw = P.sb("gnw", [128, 1], F32)
    P.dma("sp", gnw, gnw[:, :], P.gin["gnw"], P.gin["gnw"].ap())
    negA = P.sb("negA", [1, cfg.GS], F32)
    for j in range(cfg.GS):
        P.act(negA, negA[:, j:j + 1], gsc, gsc[:, 2 * j:2 * j + 1], AF.Exp)
    P.ts("dve", negA, negA[:, :], negA, negA[:, :], -1.0, None, ALU.mult)
    one1 = P.sb("one1", [1, 1], F32)
    P.memset("dve", one1, one1[:, :], 1.0)

    T = {}

    def tb(name, shape=(128, 128), n=2):
        if name not in T:
            T[name] = ([P.sb("g_" + name, list(shape), F32) for _ in range(n)], [0])
        bufs, k = T[name]
        k[0] += 1
        return bufs[k[0] % len(bufs)]

    raw = P.sb("raw", [128, SEG + 3], F32)
    acc = P.sb("acc", [128, SEG], F32)
    SB = []
    for j in range(NSL):
        d = {}
        for nm in ("qT", "kT", "vT", "zs"):
            d[nm] = P.sb("g" + nm + str(j), [128, SEG], F32)
        d["ob"] = P.sb("gob" + str(j), [128, SEG], BF16)
        d["brow"] = P.sb("brow" + str(j), [1, SEG], F32)
        d["grow"] = P.sb("grow" + str(j), [1, SEG], F32)
        d["St"] = P.sb("gS" + str(j), [128, 128], F32)
        P.memset("dve", d["St"], d["St"][:, :], 0.0)
        SB.append(d)
    res = {}

    def chunkA(j, ci):
        sb = SB[j]
        qT, kT, vT, brow, grow = sb["qT"], sb["kT"], sb["vT"], sb["brow"], sb["grow"]
        sfx = str(j)
        csl = slice(ci * C, ci * C + C)
        pb = yield from get()
        P.mm(pb, pb[:, 0:1], grow, grow[:, csl], one1, one1[:, :])
        P.mm(pb, pb[:, 1:2], brow, brow[:, csl], one1, one1[:, :])
        yield
        cols = tb("cols" + sfx, (128, 8))
        P.copy("dve", cols, cols[:, 0:2], pb, pb[:, 0:2])
        rel(pb)
        yield
        pb = yield from get()
        P.mm(pb, pb[:, 0:1], Ui, Ui[:, :], cols, cols[:, 0:1])
        Ug = tb("Ug" + sfx)
        P.ts("dve", Ug, Ug[:, :], Ui, Ui[:, :], cols[:, 0:1], None, ALU.mult, extra_reads=[cols])
        yield
        P.copy("dve", cols, cols[:, 2:3], pb, pb[:, 0:1])
        rel(pb)
        pG = yield from get()
        P.mm(pG, pG[:, 0:128], ones, ones[:, :], Ug, Ug[:, :])
        yield
        Gsb = tb("Gsb" + sfx)
        P.copy("act", Gsb, Gsb[:, :], pG, pG[:, 0:128])
        rel(pG)
        yield
        dT = tb("dT" + sfx)
        P.ts("dve", dT, dT[:, :], Gsb, Gsb[:, :], cols[:, 2:3], 0.0, ALU.subtract, ALU.min, extra_reads=[cols])
        eG = tb("eG" + sfx)
        P.act(eG, eG[:, :], Gsb, Gsb[:, :], AF.Exp)
        yield
        gam = tb("gam" + sfx)
        P.act(gam, gam[:, :], dT, dT[:, :], AF.Exp)
        P.ts("dve", cols, cols[:, 4:5], Gsb, Gsb[:, 127:128], cols[:, 2:3], None, ALU.subtract, extra_reads=[cols])
        yield
        P.act(cols, cols[:, 3:4], cols, cols[:, 2:3], AF.Exp)
        P.act(cols, cols[:, 4:5], cols, cols[:, 4:5], AF.Exp)
        gami = tb("gami" + sfx)
        P.tt("pool", gami, gami[:, :], gam, gam[:, :], Ui, Ui[:, :], ALU.mult)
        gams = tb("gams" + sfx)
        P.tt("pool", gams, gams[:, :], gam, gam[:, :], Us, Us[:, :], ALU.mult)
        yield
        pk = yield from get()
        P.tr(pk, pk[:, 0:128], kT, kT[:, csl], ident, ident[:, :])
        yield
        kbG = tb("kbG" + sfx)
        P.ts("dve", kbG, kbG[:, :], pk, pk[:, 0:128], cols[:, 1:2], cols[:, 3:4], ALU.mult, ALU.mult, extra_reads=[cols])
        kd = tb("kd" + sfx)
        P.ts("dve", kd, kd[:, :], pk, pk[:, 0:128], cols[:, 4:5], None, ALU.mult, extra_reads=[cols])
        rel(pk)
        yield
        pv = yield from get()
        P.tr(pv, pv[:, 0:128], vT, vT[:, csl], ident, ident[:, :])
        yield
        vb = tb("vb" + sfx)
        P.ts("dve", vb, vb[:, :], pv, pv[:, 0:128], cols[:, 1:2], None, ALU.mult, extra_reads=[cols])
        rel(pv)
        yield
        pbr = yield from get()
        P.mm(pbr, pbr[:, 0:128], ones, ones[0:1, :], brow, brow[:, csl])
        yield
        kbT = tb("kbT" + sfx)
        P.tt("dve", kbT, kbT[:, :], kT, kT[:, csl], pbr, pbr[:, 0:128], ALU.mult)
        rel(pbr)
        yield
        pA = yield from get()
        P.mm(pA, pA[:, 0:128], kT, kT[:, csl], kbT, kbT[:, :])
        P.mm(pA, pA[:, 128:256], kT, kT[:, csl], qT, qT[:, csl])
        yield
        Pm = tb("Pm" + sfx, n=3)
        P.stt("dve", Pm, Pm[:, :], pA, pA[:, 0:128], -1.0, gams, gams[:, :], ALU.mult, ALU.mult)
        attnT = tb("attnT" + sfx)
        P.tt("dve", attnT, attnT[:, :], pA, pA[:, 128:256], gami, gami[:, :], ALU.mult)
        rel(pA)
        yield
        pq = yield from get()
        P.tr(pq, pq[:, 0:128], Pm, Pm[:, :], ident, ident[:, :])
        yield
        Qm = tb("Qm" + sfx, n=3)
        P.copy("act", Qm, Qm[:, :], pq, pq[:, 0:128])
        rel(pq)
        R = tb("R" + sfx, n=3)
        P.tt("pool", R, R[:, :], Pm, Pm[:, :], ident, ident[:, :], ALU.add)
        qgT = tb("qgT" + sfx)
        P.tt("pool", qgT, qgT[:, :], qT, qT[:, csl], eG, eG[:, :], ALU.mult)
        yield
        for k in range(1, 7):
            pQ = yield from get()
            P.mm(pQ, pQ[:, 0:128], Pm, Pm[:, :], Qm, Qm[:, :])
            yield
            Qn = tb("Qm" + sfx, n=3)
            P.copy("dve", Qn, Qn[:, :], pQ, pQ[:, 0:128])
            rel(pQ)
            yield
            if k < 6:
                pP = yield from get()
                P.mm(pP, pP[:, 0:128], Qm, Qm[:, :], Pm, Pm[:, :])
                yield
                Pn = tb("Pm" + sfx, n=3)
                P.copy("act", Pn, Pn[:, :], pP, pP[:, 0:128])
                rel(pP)
                yield
            pR = yield from get()
            P.mm(pR, pR[:, 0:128], Qn, Qn[:, :], R, R[:, :])
            yield
            Rn = tb("R" + sfx, n=3)
            P.tt("dve", Rn, Rn[:, :], R, R[:, :], pR, pR[:, 0:128], ALU.add)
            rel(pR)
            R, Qm = Rn, Qn
            if k < 6:
                Pm = Pn
            yield
        pu = yield from get()
        P.mm(pu, pu[:, 0:128], R, R[:, :], vb, vb[:, :])
        yield
        u = tb("u" + sfx)
        P.copy("act", u, u[:, :], pu, pu[:, 0:128])
        rel(pu)
        yield
        pw = yield from get()
        P.mm(pw, pw[:, 0:128], kbG, kbG[:, :], R, R[:, :])
        yield
        wT = tb("wT" + sfx)
        P.copy("act", wT, wT[:, :], pw, pw[:, 0:128])
        rel(pw)
        res[(j, ci)] = dict(u=u, wT=wT, qgT=qgT, attnT=attnT, kd=kd, eG=eG)
        yield

    def chunkB(j, ci):
        sb = SB[j]
        St, zs, ob = sb["St"], sb["zs"], sb["ob"]
        r = res.pop((j, ci))
        u, wT, qgT, attnT, kd, eG = r["u"], r["wT"], r["qgT"], r["attnT"], r["kd"], r["eG"]
        sfx = str(j)
        csl = slice(ci * C, ci * C + C)
        pa = yield from get()
        P.mm(pa, pa[:, 0:128], wT, wT[:, :], St, St[:, :])
        yield
        vn = tb("vn" + sfx)
        P.tt("dve", vn, vn[:, :], u, u[:, :], pa, pa[:, 0:128], ALU.subtract)
        rel(pa)
        yield
        pS = yield from get()
        P.mm(pS, pS[:, 0:128], kd, kd[:, :], vn, vn[:, :])
        po = yield from get()
        P.mm(po, po[:, 0:128], qgT, qgT[:, :], St, St[:, :], start=True, stop=False)
        P.mm(po, po[:, 0:128], attnT, attnT[:, :], vn, vn[:, :], start=False, stop=True)
        yield
        P.stt("dve", St, St[:, :], St, St[:, :], eG[:, 127:128], pS, pS[:, 0:128], ALU.mult, ALU.add, extra_reads=[eG])
        rel(pS)
        osb = tb("osb" + sfx)
        P.copy("act", osb, osb[:, :], po, po[:, 0:128])
        rel(po)
        yield
        osq = tb("osq" + sfx)
        P.act(osq, osq[:, :], osb, osb[:, :], AF.Square)
        yield
        oc = tb("oc" + sfx, (128, 2))
        P.op("dve", lambda en, oc=oc, osq=osq: en.reduce_sum(out=oc[:, 0:1], in_=osq[:, :], axis=AX.X),
             reads=[osq], writes=[oc])
        yield
        P.act(oc, oc[:, 0:1], oc, oc[:, 0:1], AF.Sqrt, bias=P.consts["eps"][:, 0:1], scale=1.0 / 128,
              extra_reads=[P.consts["eps"]])
        yield
        P.op("dve", lambda en, oc=oc: en.reciprocal(out=oc[:, 0:1], in_=oc[:, 0:1]), reads=[oc], writes=[oc])
        P.ts("dve", osb, osb[:, :], osb, osb[:, :], oc[:, 0:1], None, ALU.mult, extra_reads=[oc])
        yield
        pt = yield from get()
        P.tr(pt, pt[:, 0:128], osb, osb[:, :], ident, ident[:, :])
        yield
        P.stt("dve", ob, ob[:, csl], pt, pt[:, 0:128], gnw[:, 0:1], zs, zs[:, csl], ALU.mult, ALU.mult, extra_reads=[gnw])
        rel(pt)
        yield

    def pre(g):
        t0 = g * SEG
        for j in range(NSL):
            sb = SB[j]
            qT, kT, vT, zs, brow, grow = sb["qT"], sb["kT"], sb["vT"], sb["zs"], sb["brow"], sb["grow"]
            rq = cfg.o_g + 512 * j
            for b, dst in enumerate((qT, kT, vT)):
                row = rq + 128 * b
                if g == 0:
                    P.memset("pool", raw, raw[:, 0:3], 0.0)
                    P.dma("sp", raw, raw[:, 3:SEG + 3], proj, proj[row:row + 128, 0:SEG])
                else:
                    P.dma("sp", raw, raw[:, :], proj, proj[row:row + 128, t0 - 3:t0 + SEG])
                cb = (j * 3 + b) * 4
                P.ts("dve", acc, acc[:, :], raw, raw[:, 0:SEG], gconv[:, cb:cb + 1], None, ALU.mult, extra_reads=[gconv])
                for i in range(1, 4):
                    P.stt("dve", acc, acc[:, :], raw, raw[:, i:i + SEG], gconv[:, cb + i:cb + i + 1],
                          acc, acc[:, :], ALU.mult, ALU.add, extra_reads=[gconv])
                P.act(dst, dst[:, :], acc, acc[:, :], AF.Silu)
            P.dma("sp", acc, acc[:, :], proj, proj[rq + 384:rq + 512, t0:t0 + SEG])
            P.act(zs, zs[:, :], acc, acc[:, :], AF.Silu)
            for dst, sc in ((qT, 128 ** -0.5), (kT, 1.0)):
                P.act(acc, acc[:, :], dst, dst[:, :], AF.Square)
                for c0 in range(0, SEG, 256):
                    pb = pp()
                    P.mm(pb, pb[:, 0:256], ones, ones[:, :], acc, acc[:, c0:c0 + 256])
                    rs = tb("rs", (128, 256))
                    P.act(rs, rs[:, :], pb, pb[:, 0:256], AF.Sqrt, bias=P.consts["eps"][:, 0:1], extra_reads=[P.consts["eps"]])
                    P.op("dve", lambda en, rs=rs: en.reciprocal(out=rs[:, :], in_=rs[:, :]), reads=[rs], writes=[rs])
                    P.stt("dve", dst, dst[:, c0:c0 + 256], dst, dst[:, c0:c0 + 256], sc, rs, rs[:, :], ALU.mult, ALU.mult)
            srow = cfg.o_s + 2 * j
            P.dma("sp", brow, brow[:, :], proj, proj[srow:srow + 1, t0:t0 + SEG])
            P.act(brow, brow[:, :], brow, brow[:, :], AF.Sigmoid)
            P.dma("sp", grow, grow[:, :], proj, proj[srow + 1:srow + 2, t0:t0 + SEG])
            P.act(grow, grow[:, :], grow, grow[:, :], AF.Exp, bias=gsc[:, 2 * j + 1:2 * j + 2], extra_reads=[gsc])
            P.act(grow, grow[:, :], grow, grow[:, :], AF.Ln, bias=one1[:, 0:1], extra_reads=[one1])
            P.ts("dve", grow, grow[:, :], grow, grow[:, :], negA[:, j:j + 1], None, ALU.mult, extra_reads=[negA])

    def pipe(g):
        NCH = SEG // C
        for ci in range(NCH + 1):
            gens = []
            for j in range(NSL):
                if ci >= 1:
                    gens.append(chunkB(j, ci - 1))
            for j in range(NSL):
                if ci < NCH:
                    gens.append(chunkA(j, ci))
            yield from run_rr_gen(gens)

    def post(g):
        t0 = g * SEG
        for j in range(NSL):
            P.dma("pool", mix, mix[128 * j:128 * (j + 1), t0:t0 + SEG], SB[j]["ob"], SB[j]["ob"][:, :])

    if not own:
        return dict(pre=pre, pipe=pipe, post=post, SEG=SEG)
    for g in range(NSEG):
        pre(g)
        run_rr([pipe(g)])
        post(g)
    P.pop()


def run_rr(gens):
    gens = list(gens)
    while gens:
        nxt = []
        for g in gens:
            try:
                next(g)
                nxt.append(g)
            except StopIteration:
                pass
        gens = nxt


def emit_rwkv(P, cfg, proj, mix, shared=None, SEG=None):
    S = cfg.S
    if SEG is None:
        SEG = 512 if S >= 512 else S
    NSEG = S // SEG
    C = 128
    BW = min(256, SEG)
    NH = 3
    own = shared is None
    if own:
        P.push()
        shared = mixer_shared(P, 1, 7)
    ident, Ui, Us = shared["ident"], shared["Ui"], shared["Us"]
    pp, get, rel = shared["pp"], shared["get"], shared["rel"]
    ones = P.sb("ones", [64, 64], F32)
    P.memset("dve", ones, ones[:, :], 1.0)
    rvec = P.sb("rvec", [64, 30], F32)
    P.dma("sp", rvec, rvec[:, :], P.rin["rvec"], P.rin["rvec"].ap())
    rmul = P.sb("rmul", [128, 6], F32)
    P.dma("sp", rmul, rmul[:, :], P.rin["rmul"], P.rin["rmul"].ap())
    wup = P.sb("wup", [128, cfg.RC], F32)
    P.dma("sp", wup, wup[:, :], P.rin["rwup"], P.rin["rwup"].ap())
    aup = P.sb("aup", [128, cfg.RC], F32)
    P.dma("sp", aup, aup[:, :], P.rin["raup"], P.rin["raup"].ap())
    gup = P.sb("gup", [128, 4, cfg.RC], F32)
    P.dma("sp", gup, gup[:, :, :], P.rin["rgup"], P.rin["rgup"].ap().rearrange("(t p) c -> p t c", p=128))
    omk = P.sb("omk", [64, 3], F32)
    for h in range(NH):
        P.ts("dve", omk, omk[:, h:h + 1], rvec, rvec[:, h * 10 + 6:h * 10 + 7], -1.0, 1.0, ALU.mult, ALU.add)
    eps12 = P.sb("eps12", [64, 1], F32)
    P.memset("dve", eps12, eps12[:, :], 1e-12)
    epsgn = P.sb("epsgn", [64, 1], F32)
    P.memset("dve", epsgn, epsgn[:, :], 64e-5)

    T = {}

    def tb(name, shape=(64, 64), n=2, dt=F32):
        if name not in T:
            T[name] = ([P.sb("r_" + name, list(shape), dt) for _ in range(n)], [0])
        bufs, k = T[name]
        k[0] += 1
        return bufs[k[0] % len(bufs)]

    lraw = P.sb("lraw", [128, SEG + 1], F32)
    ldif = P.sb("ldif", [128, SEG], F32)
    twd = P.sb("twd", [128, SEG], F32)
    adp = P.sb("adp", [128, SEG], F32)
    sgd = P.sb("sgd", [128, 4, SEG], F32)
    hraw = P.sb("hraw", [64, SEG + 1], F32)
    hdif = P.sb("hdif", [64, SEG], F32)
    t1 = P.sb("t1", [64, SEG], F32)
    HB = []
    for h in range(NH):
        d = {}
        for nm in ("rS", "kS", "vS", "lwS", "aS", "gS", "kkS", "bS", "yS", "bon"):
            d[nm] = P.sb(nm + str(h), [64, SEG], F32)
        d["ob"] = P.sb("rob" + str(h), [64, SEG], BF16)
        d["H"] = P.sb("rH" + str(h), [64, 64], F32)
        P.memset("dve", d["H"], d["H"][:, :], 0.0)
        d["Hb"] = P.sb("rHb" + str(h), [64, 64], BF16)
        P.memset("dve", d["Hb"], d["Hb"][:, :], 0.0)
        HB.append(d)
    I64, Ui64, Us64 = ident[0:64, 0:64], Ui[0:64, 0:64], Us[0:64, 0:64]

    def shift_load(dst_raw, rows, row0, t0, g):
        if g == 0:
            P.memset("pool", dst_raw, dst_raw[0:rows, 0:1], 0.0)
            P.dma("sp", dst_raw, dst_raw[0:rows, 1:SEG + 1], proj, proj[row0:row0 + rows, 0:SEG])
        else:
            P.dma("sp", dst_raw, dst_raw[0:rows, :], proj, proj[row0:row0 + rows, t0 - 1:t0 + SEG])

    res = {}

    def chunkA(h, ci):
        hb = HB[h]
        rS, kS, vS, lwS, aS, bS = hb["rS"], hb["kS"], hb["vS"], hb["lwS"], hb["aS"], hb["bS"]
        sfx = str(h)
        csl = slice(ci * C, ci * C + C)
        pl = yield from get()
        P.tr(pl, pl[0:C, 0:64], lwS, lwS[:, csl], ident, I64)
        yield
        lwt = tb("lwt" + sfx, (C, 64))
        P.copy("act", lwt, lwt[:, :], pl, pl[0:C, 0:64])
        rel(pl)
        yield
        pL = yield from get()
        P.mm(pL, pL[0:64, 0:C], lwt, lwt[:, :], Ui, Ui[0:C, 0:C])
        yield
        Lsb = tb("Lsb" + sfx, (64, C))
        P.copy("act", Lsb, Lsb[:, :], pL, pL[0:64, 0:C])
        rel(pL)
        yield
        eLi = tb("eLi" + sfx, (64, C))
        P.act(eLi, eLi[:, :], Lsb, Lsb[:, :], AF.Exp)
        eLx = tb("eLx" + sfx, (64, C))
        P.tt("dve", eLx, eLx[:, :], Lsb, Lsb[:, :], lwS, lwS[:, csl], ALU.subtract)
        yield
        eLn = tb("eLn" + sfx, (64, C))
        P.act(eLn, eLn[:, :], Lsb, Lsb[:, :], AF.Exp, scale=-1.0)
        yield
        P.act(eLx, eLx[:, :], eLx, eLx[:, :], AF.Exp)
        AR = tb("AR" + sfx, (64, 2 * C))
        P.tt("pool", AR, AR[:, C:2 * C], rS, rS[:, csl], eLi, eLi[:, :], ALU.mult)
        yield
        bt = tb("bt" + sfx, (64, C))
        P.tt("dve", bt, bt[:, :], bS, bS[:, csl], eLn, eLn[:, :], ALU.mult)
        kt_ = tb("kt" + sfx, (64, C))
        P.tt("pool", kt_, kt_[:, :], kS, kS[:, csl], eLn, eLn[:, :], ALU.mult)
        yield
        P.tt("dve", AR, AR[:, 0:C], aS, aS[:, csl], eLx, eLx[:, :], ALU.mult)
        yield
        bt16 = tb("bt16" + sfx, (64, C), dt=BF16)
        P.copy("pool", bt16, bt16[:, :], bt, bt[:, :])
        kt16 = tb("kt16" + sfx, (64, C), dt=BF16)
        P.copy("act", kt16, kt16[:, :], kt_, kt_[:, :])
        AR16 = tb("AR16" + sfx, (64, 2 * C), dt=BF16)
        P.copy("pool", AR16, AR16[:, :], AR, AR[:, :])
        yield
        tok = {}
        for nm, (src, sap) in {"V": (vS, vS[:, csl]), "B": (bt, bt[:, :]), "K": (kt_, kt_[:, :]),
                               "A": (AR, AR[:, 0:C])}.items():
            pt = yield from get()
            P.tr(pt, pt[0:C, 0:64], src, sap, ident, I64)
            yield
            d = tb("tok" + nm + sfx, (C, 64), dt=BF16)
            P.copy("act" if nm in ("B", "V") else "dve", d, d[:, :], pt, pt[0:C, 0:64])
            rel(pt)
            tok[nm] = d
            yield
        pAB = yield from get()
        P.mm(pAB, pAB[0:C, 0:2 * C], bt16, bt16[:, :], AR16, AR16[:, :])
        yield
        Pm32 = tb("Pm32" + sfx, (C, C))
        P.tt("dve", Pm32, Pm32[:, :], pAB, pAB[0:C, 0:C], Us, Us[0:C, 0:C], ALU.mult)
        ArbT = tb("ArbT" + sfx, (C, C), dt=BF16)
        P.tt("dve", ArbT, ArbT[:, :], pAB, pAB[0:C, C:2 * C], Ui, Ui[0:C, 0:C], ALU.mult)
        rel(pAB)
        yield
        pAK = yield from get()
        P.mm(pAK, pAK[0:C, 0:2 * C], kt16, kt16[:, :], AR16, AR16[:, :])
        yield
        AakT = tb("AakT" + sfx, (C, C), dt=BF16)
        P.tt("dve", AakT, AakT[:, :], pAK, pAK[0:C, 0:C], Us, Us[0:C, 0:C], ALU.mult)
        ArkT = tb("ArkT" + sfx, (C, C), dt=BF16)
        P.tt("dve", ArkT, ArkT[:, :], pAK, pAK[0:C, C:2 * C], Ui, Ui[0:C, 0:C], ALU.mult)
        rel(pAK)
        yield
        pq = yield from get()
        P.tr(pq, pq[0:C, 0:C], Pm32, Pm32[:, :], ident, ident[0:C, 0:C])
        Pm = tb("Pm" + sfx, (C, C), n=3, dt=BF16)
        P.copy("pool", Pm, Pm[:, :], Pm32, Pm32[:, :])
        yield
        Qm = tb("Qm" + sfx, (C, C), n=3, dt=BF16)
        P.copy("act", Qm, Qm[:, :], pq, pq[0:C, 0:C])
        rel(pq)
        R = tb("R" + sfx, (C, C), n=3, dt=BF16)
        P.tt("pool", R, R[:, :], Pm32, Pm32[:, :], ident, ident[0:C, 0:C], ALU.add)
        yield
        pX = yield from get()
        P.mm(pX, pX[0:C, 0:64], AakT, AakT[:, :], tok["V"], tok["V"][:, :])
        yield
        X = tb("X" + sfx, (C, 64), dt=BF16)
        P.copy("act", X, X[:, :], pX, pX[0:C, 0:64])
        rel(pX)
        yield
        NK = 6 if C == 128 else 5
        for k in range(1, NK + 1):
            pQ = yield from get()
            P.mm(pQ, pQ[0:C, 0:C], Pm, Pm[:, :], Qm, Qm[:, :])
            yield
            Qn = tb("Qm" + sfx, (C, C), n=3, dt=BF16)
            P.copy("dve", Qn, Qn[:, :], pQ, pQ[0:C, 0:C])
            rel(pQ)
            yield
            if k < NK:
                pP = yield from get()
                P.mm(pP, pP[0:C, 0:C], Qm, Qm[:, :], Pm, Pm[:, :])
                yield
                Pn = tb("Pm" + sfx, (C, C), n=3, dt=BF16)
                P.copy("act", Pn, Pn[:, :], pP, pP[0:C, 0:C])
                rel(pP)
                yield
            pR = yield from get()
            P.mm(pR, pR[0:C, 0:C], Qn, Qn[:, :], R, R[:, :])
            yield
            Rn = tb("R" + sfx, (C, C), n=3, dt=BF16)
            P.tt("dve", Rn, Rn[:, :], R, R[:, :], pR, pR[0:C, 0:C], ALU.add)
            rel(pR)
            R, Qm = Rn, Qn
            if k < NK:
                Pm = Pn
            yield
        pW = yield from get()
        P.mm(pW, pW[0:64, 0:C], tok["A"], tok["A"][:, :], R, R[:, :])
        yield
        WdT = tb("WdT" + sfx, (64, C), dt=BF16)
        P.copy("act", WdT, WdT[:, :], pW, pW[0:64, 0:C])
        rel(pW)
        yield
        pU0 = yield from get()
        P.mm(pU0, pU0[0:C, 0:64], R, R[:, :], X, X[:, :])
        yield
        U0 = tb("U0" + sfx, (C, 64))
        P.copy("dve", U0, U0[:, :], pU0, pU0[0:C, 0:64])
        rel(pU0)
        res[(h, ci)] = dict(WdT=WdT, U0=U0, AR=AR16, ArbT=ArbT, ArkT=ArkT, tok=tok, eLi=eLi)
        yield

    def chunkB(h, ci):
        hb = HB[h]
        H, yS, Hb = hb["H"], hb["yS"], hb["Hb"]
        r = res.pop((h, ci))
        WdT, U0, AR, ArbT, ArkT, tok, eLi = r["WdT"], r["U0"], r["AR"], r["ArbT"], r["ArkT"], r["tok"], r["eLi"]
        sfx = str(h)
        csl = slice(ci * C, ci * C + C)
        pU = yield from get()
        P.mm(pU, pU[0:C, 0:64], WdT, WdT[:, :], Hb, Hb[:, :])
        yield
        U = tb("U" + sfx, (C, 64), dt=BF16)
        P.tt("dve", U, U[:, :], U0, U0[:, :], pU, pU[0:C, 0:64], ALU.add)
        rel(pU)
        yield
        pH = yield from get()
        P.mm(pH, pH[0:64, 0:64], tok["B"], tok["B"][:, :], U, U[:, :], start=True, stop=False)
        P.mm(pH, pH[0:64, 0:64], tok["K"], tok["K"][:, :], tok["V"], tok["V"][:, :], start=False, stop=True)
        pY = yield from get()
        P.mm(pY, pY[0:64, 0:C], Hb, Hb[:, :], AR, AR[:, C:2 * C], start=True, stop=False)
        P.mm(pY, pY[0:64, 0:C], U, U[:, :], ArbT, ArbT[:, :], start=False, stop=False)
        P.mm(pY, pY[0:64, 0:C], tok["V"], tok["V"][:, :], ArkT, ArkT[:, :], start=False, stop=True)
        yield
        P.ts("dve", H, H[:, :], H, H[:, :], eLi[:, C - 1:C], None, ALU.mult, extra_reads=[eLi])
        P.stt("dve", H, H[:, :], pH, pH[0:64, 0:64], eLi[:, C - 1:C], H, H[:, :], ALU.mult, ALU.add, extra_reads=[eLi])
        P.copy("pool", Hb, Hb[:, :], H, H[:, :])
        P.copy("act", yS, yS[:, csl], pY, pY[0:64, 0:C])
        rel(pH)
        rel(pY)
        yield

    def pre(g):
        t0 = g * SEG
        lt = [(cfg.o_l, 128, 0), (cfg.o_l + 128, 128, 1)] + \
             [(cfg.o_l + 256 + 128 * i, min(128, cfg.LG - 128 * i), 2 + i) for i in range(4)]
        for (row0, rows, ti) in lt:
            shift_load(lraw, rows, row0, t0, g)
            P.tt("dve", ldif, ldif[0:rows, :], lraw, lraw[0:rows, 0:SEG], lraw, lraw[0:rows, 1:SEG + 1], ALU.subtract)
            P.stt("dve", ldif, ldif[0:rows, :], ldif, ldif[0:rows, :], rmul[0:rows, ti:ti + 1], lraw, lraw[0:rows, 1:SEG + 1],
                  ALU.mult, ALU.add, extra_reads=[rmul])
            if ti == 0:
                P.act(twd, twd[:, :], ldif, ldif[:, :], AF.Tanh)
            elif ti == 1:
                P.copy("pool", adp, adp[:, :], ldif, ldif[:, :])
            else:
                if rows < 128:
                    P.memset("pool", sgd, sgd[:, ti - 2, :], 0.0)
                P.act(sgd, sgd[0:rows, ti - 2, :], ldif, ldif[0:rows, :], AF.Sigmoid)
        for h in range(NH):
            hb = HB[h]
            rS, kS, vS, lwS, aS, gS, kkS, bS, bon = (hb[n_] for n_ in ("rS", "kS", "vS", "lwS", "aS", "gS", "kkS", "bS", "bon"))
            vb_ = h * 10
            hc = slice(64 * h, 64 * h + 64)
            for bi_, dst in enumerate((rS, kS, vS)):
                shift_load(hraw, 64, cfg.o_r + cfg.RC * bi_ + 64 * h, t0, g)
                P.tt("dve", hdif, hdif[:, :], hraw, hraw[:, 0:SEG], hraw, hraw[:, 1:SEG + 1], ALU.subtract)
                P.stt("dve", dst, dst[:, :], hdif, hdif[:, :], rvec[:, vb_ + bi_:vb_ + bi_ + 1], hraw, hraw[:, 1:SEG + 1],
                      ALU.mult, ALU.add, extra_reads=[rvec])
            for b0 in range(0, SEG, BW):
                bs = slice(b0, b0 + BW)
                pw = pp()
                P.mm(pw, pw[0:64, 0:BW], wup, wup[:, hc], twd, twd[:, bs])
                P.act(lwS, lwS[:, bs], pw, pw[0:64, 0:BW], AF.Sigmoid, bias=rvec[:, vb_ + 3:vb_ + 4], extra_reads=[rvec])
                pa = pp()
                P.mm(pa, pa[0:64, 0:BW], aup, aup[:, hc], adp, adp[:, bs])
                P.act(aS, aS[:, bs], pa, pa[0:64, 0:BW], AF.Sigmoid, bias=rvec[:, vb_ + 4:vb_ + 5], extra_reads=[rvec])
                pg = pp()
                for kt in range(4):
                    P.mm(pg, pg[0:64, 0:BW], gup, gup[:, kt, hc], sgd, sgd[:, kt, bs], start=(kt == 0), stop=(kt == 3))
                P.copy("act", gS, gS[:, bs], pg, pg[0:64, 0:BW])
            P.ts("dve", lwS, lwS[:, :], lwS, lwS[:, :], -math.exp(-0.5), None, ALU.mult)
            P.ts("dve", kkS, kkS[:, :], kS, kS[:, :], rvec[:, vb_ + 5:vb_ + 6], None, ALU.mult, extra_reads=[rvec])
            P.act(t1, t1[:, :], kkS, kkS[:, :], AF.Square)
            for b0 in range(0, SEG, BW):
                bs = slice(b0, b0 + BW)
                pn = pp()
                P.mm(pn, pn[0:64, 0:BW], ones, ones[:, :], t1, t1[:, bs])
                rs = tb("rs", (64, BW))
                P.act(rs, rs[:, :], pn, pn[0:64, 0:BW], AF.Sqrt, bias=eps12[:, 0:1], extra_reads=[eps12])
                P.op("dve", lambda en, rs=rs: en.reciprocal(out=rs[:, :], in_=rs[:, :]), reads=[rs], writes=[rs])
                P.tt("dve", kkS, kkS[:, bs], kkS, kkS[:, bs], rs, rs[:, :], ALU.mult)
            P.ts("dve", t1, t1[:, :], aS, aS[:, :], rvec[:, vb_ + 6:vb_ + 7], omk[:, h:h + 1], ALU.mult, ALU.add,
                 extra_reads=[rvec, omk])
            P.tt("dve", kS, kS[:, :], kS, kS[:, :], t1, t1[:, :], ALU.mult)
            P.tt("pool", bS, bS[:, :], kkS, kkS[:, :], aS, aS[:, :], ALU.mult)
            P.ts("pool", aS, aS[:, :], kkS, kkS[:, :], -1.0, None, ALU.mult)
            P.stt("dve", t1, t1[:, :], rS, rS[:, :], rvec[:, vb_ + 7:vb_ + 8], kS, kS[:, :], ALU.mult, ALU.mult,
                  extra_reads=[rvec])
            for b0 in range(0, SEG, BW):
                bs = slice(b0, b0 + BW)
                pn = pp()
                P.mm(pn, pn[0:64, 0:BW], ones, ones[:, :], t1, t1[:, bs])
                P.tt("dve", bon, bon[:, bs], pn, pn[0:64, 0:BW], vS, vS[:, bs], ALU.mult)
    def pipe(g):
        NCH = SEG // C
        for ci in range(NCH + 1):
            gens = []
            for h in range(NH):
                if ci >= 1:
                    gens.append(chunkB(h, ci - 1))
            for h in range(NH):
                if ci < NCH:
                    gens.append(chunkA(h, ci))
            yield from run_rr_gen(gens)

    def post(g):
        t0 = g * SEG
        for h in range(NH):
            hb = HB[h]
            yS, gS, bon, obuf = hb["yS"], hb["gS"], hb["bon"], hb["ob"]
            vb_ = h * 10
            for b0 in range(0, SEG, BW):
                bs = slice(b0, b0 + BW)
                pm = pp()
                P.mm(pm, pm[0:64, 0:BW], ones, ones[:, :], yS, yS[:, bs])
                P.stt("dve", t1, t1[:, bs], pm, pm[0:64, 0:BW], -1.0 / 64, yS, yS[:, bs], ALU.mult, ALU.add)
                sq = tb("sq", (64, BW))
                P.act(sq, sq[:, :], t1, t1[:, bs], AF.Square)
                pv_ = pp()
                P.mm(pv_, pv_[0:64, 0:BW], ones, ones[:, :], sq, sq[:, :])
                rs = tb("rs", (64, BW))
                P.act(rs, rs[:, :], pv_, pv_[0:64, 0:BW], AF.Sqrt, bias=epsgn[:, 0:1], scale=1.0 / 64, extra_reads=[epsgn])
                P.op("dve", lambda en, rs=rs: en.reciprocal(out=rs[:, :], in_=rs[:, :]), reads=[rs], writes=[rs])
                P.tt("dve", t1, t1[:, bs], t1, t1[:, bs], rs, rs[:, :], ALU.mult)
                P.ts("dve", t1, t1[:, bs], t1, t1[:, bs], rvec[:, vb_ + 8:vb_ + 9], rvec[:, vb_ + 9:vb_ + 10], ALU.mult, ALU.add,
                     extra_reads=[rvec])
                P.tt("pool", t1, t1[:, bs], t1, t1[:, bs], bon, bon[:, bs], ALU.add)
                P.tt("pool", obuf, obuf[:, bs], t1, t1[:, bs], gS, gS[:, bs], ALU.mult)
            P.dma("pool", mix, mix[384 + 64 * h:384 + 64 * h + 64, t0:t0 + SEG], obuf, obuf[:, :])

    if not own:
        return dict(pre=pre, pipe=pipe, post=post, SEG=SEG)
    for g in range(NSEG):
        pre(g)
        run_rr([pipe(g)])
        post(g)
    P.pop()


def emit_swa(P, cfg, proj, mix, bm):
    S = cfg.S
    SEG = 2048
    NSEG = S // SEG
    scale = 128 ** -0.5
    qr, kr, vr = cfg.o_a, cfg.o_a + 128, cfg.o_a + 256
    P.push()
    ident = P.sb("ident", [128, 128], F32)
    make_identity(P, ident)
    ones_c = P.sb("ones_c", [128, 1], F32)
    P.memset("dve", ones_c, ones_c[:, :], 1.0)
    ones_r = P.sb("ones_r", [1, 128], F32)
    P.memset("dve", ones_r, ones_r[:, :], 1.0)
    ones_b = P.sb("ones_b", [128, 128], BF16)
    P.memset("dve", ones_b, ones_b[:, :], 1.0)
    bms = P.sb("bms", [128, 6, 128], F32)
    for p in range(3):
        for t in range(2):
            P.dma("sp", bms, bms[:, p * 2 + t, :], bm, bm[p, t, :, :])
    mx = P.sb("mx", [1, 2], F32)
    P.memset("dve", mx, mx[:, :], 0.0)
    P.push()
    ld = [P.sb("ld", [128, 512], F32) for _ in range(2)]
    sq = [P.sb("sq", [128, 512], F32) for _ in range(2)]
    nps = [P.ps("nps", [1, 512], F32) for _ in range(2)]
    bmx = P.sb("bmx", [1, 1], F32)
    it = 0
    for which, row in enumerate((qr, kr)):
        for t0 in range(0, S, 512):
            a, q, pb = ld[it % 2], sq[it % 2], nps[it % 2]
            it += 1
            P.dma("sp", a, a[:, :], proj, proj[row:row + 128, t0:t0 + 512])
            P.act(q, q[:, :], a, a[:, :], AF.Square)
            P.mm(pb, pb[:, :], ones_c, ones_c[:, :], q, q[:, :])
            P.op("dve", lambda en, pb=pb: en.reduce_max(out=bmx[:, :], in_=pb[:, :], axis=AX.X), reads=[pb], writes=[bmx])
            P.tt("dve", mx, mx[:, which:which + 1], mx, mx[:, which:which + 1], bmx, bmx[:, :], ALU.max)
    P.pop()
    negm1 = P.sb("negm1", [1, 1], F32)
    P.tt("dve", negm1, negm1[:, :], mx, mx[:, 0:1], mx, mx[:, 1:2], ALU.add)
    P.ts("dve", negm1, negm1[:, :], negm1, negm1[:, :], -0.5 * scale, None, ALU.mult)
    negm = P.sb("negm", [128, 1], F32)
    P.push()
    pb = P.ps("nmps", [128, 1], F32)
    P.mm(pb, pb[:, :], ones_r, ones_r[:, :], negm1, negm1[:, :])
    P.copy("dve", negm, negm[:, :], pb, pb[:, :])
    P.pop()
    qT = P.sb("qT", [128, SEG], BF16)
    kT = [P.sb("kT", [128, SEG], BF16) for _ in range(2)]
    vT = [P.sb("vT", [128, SEG], F32) for _ in range(2)]
    ldq = P.sb("ldq", [128, SEG], F32)
    NUM = P.sb("NUM", [128, SEG], F32)
    DEN = P.sb("DEN", [128, SEG], F32)
    ob = P.sb("ob", [128, SEG], BF16)
    Vt = [P.sb("Vt", [128, 128], BF16) for _ in range(3)]
    tmp = [P.sb("tmp", [128, 128], F32) for _ in range(3)]
    pT = [P.sb("pT", [128, 128], BF16) for _ in range(3)]
    trp = [P.ps("trp", [128, 128], F32) for _ in range(2)]
    sp_ = [P.ps("sps", [128, 128], F32) for _ in range(2)]
    nump = [P.ps("nump", [128, 128], F32) for _ in range(2)]
    denp = [P.ps("denp", [128, 128], F32) for _ in range(2)]
    vi = 0
    bi = 0
    pending = []

    def flush_pending():
        while pending:
            np_, dp_, V, pt, first, last, qsl = pending.pop(0)
            P.mm(np_, np_[:, :], V, V[:, :], pt, pt[:, :], start=first, stop=last)
            P.mm(dp_, dp_[:, :], ones_b, ones_b[:, :], pt, pt[:, :], start=first, stop=last)
            if last:
                P.tt("dve", NUM, NUM[:, qsl], NUM, NUM[:, qsl], np_, np_[:, :], ALU.add)
                P.tt("dve", DEN, DEN[:, qsl], DEN, DEN[:, qsl], dp_, dp_[:, :], ALU.add)

    for g in range(NSEG):
        t0 = g * SEG
        cur = g % 2
        P.dma("sp", ldq, ldq[:, :], proj, proj[qr:qr + 128, t0:t0 + SEG])
        P.act(qT, qT[:, :], ldq, ldq[:, :], AF.Copy, scale=scale)
        P.dma("sp", ldq, ldq[:, :], proj, proj[kr:kr + 128, t0:t0 + SEG])
        P.copy("pool", kT[cur], kT[cur][:, :], ldq, ldq[:, :])
        P.dma("sp", vT[cur], vT[cur][:, :], proj, proj[vr:vr + 128, t0:t0 + SEG])
        P.memset("pool", NUM, NUM[:, :], 0.0)
        P.memset("pool", DEN, DEN[:, :], 0.0)
        for p, d in enumerate(SWA_DIL):
            span = 128 * d
            for nbl in range(SEG // span):
                for r in range(d):
                    base = nbl * span + r
                    qsl = slice(base, base + 127 * d + 1, d)
                    tiles = []
                    if nbl >= 1:
                        tiles.append((0, cur, slice(base - span, base - span + 127 * d + 1, d)))
                    elif g >= 1:
                        pbase = SEG - span + r
                        tiles.append((0, 1 - cur, slice(pbase, pbase + 127 * d + 1, d)))
                    tiles.append((1, cur, qsl))
                    np_, dp_ = nump[bi % 2], denp[bi % 2]
                    bi += 1
                    for ti, (tt_, ring, ksl) in enumerate(tiles):
                        tp, sps, V, tm, pt = trp[vi % 2], sp_[vi % 2], Vt[vi % 3], tmp[vi % 3], pT[vi % 3]
                        vi += 1
                        P.tr(tp, tp[:, :], vT[ring], vT[ring][:, ksl], ident, ident[:, :])
                        P.copy("act", V, V[:, :], tp, tp[:, :])
                        P.mm(sps, sps[:, :], kT[ring], kT[ring][:, ksl], qT, qT[:, qsl])
                        P.tt("dve", tm, tm[:, :], sps, sps[:, :], bms, bms[:, p * 2 + tt_, :], ALU.add)
                        P.act(pt, pt[:, :], tm, tm[:, :], AF.Exp, bias=negm[:, 0:1], extra_reads=[negm])
                        first, last = ti == 0, ti == len(tiles) - 1
                        flush_pending()
                        pending.append((np_, dp_, V, pt, first, last, qsl))
        flush_pending()
        P.op("dve", lambda en: en.reciprocal(out=DEN[:, :], in_=DEN[:, :]), reads=[DEN], writes=[DEN])
        P.tt("dve", ob, ob[:, :], NUM, NUM[:, :], DEN, DEN[:, :], ALU.mult)
        P.dma("pool", mix, mix[256:384, t0:t0 + SEG], ob, ob[:, :])
    P.pop()


def make_identity(P, ident):
    P.dma("sp", ident, ident[:, :], P.cst, P.cst[0, :, :])


def host_consts():
    c = np.zeros((4, 128, 128), np.float32)
    j = np.arange(128)[:, None]
    i = np.arange(128)[None, :]
    c[0] = (i == j)
    c[1] = (j <= i)
    c[2] = (j < i)
    c[3] = ((i // 64) == (j // 64))
    return c
```

```python
import math
from contextlib import ExitStack

import numpy as np
import ml_dtypes

import concourse.bass as bass
import concourse.mybir as mybir
from concourse.bass_utils import run_bass_kernel_spmd

F32 = mybir.dt.float32
BF16 = mybir.dt.bfloat16
AF = mybir.ActivationFunctionType
ALU = mybir.AluOpType
AX = mybir.AxisListType
NPBF = ml_dtypes.bfloat16


class Cfg:
    def __init__(self, D=4096, S=16384, NR=8, DFF=11008):
        self.D, self.S, self.NR, self.DFF = D, S, NR, DFF
        self.GD, self.GW = 128, 3 * D // 8
        self.GH = self.GW // 128
        self.AD, self.AW = 128, D // 4
        self.AH = self.AW // 128
        self.RD = 64
        self.RW = D - self.GW - self.AW
        self.RH = self.RW // 64
        self.LW, self.LA, self.LG = 128, 128, 480
        self.GDN_IN = 4 * self.GW + 2 * self.GH
        self.SWA_IN = 3 * self.AW
        self.RWKV_IN = 3 * self.RW + self.LW + self.LA + self.LG
        self.IN_W = self.GDN_IN + self.SWA_IN + self.RWKV_IN
        self.GS = -(-self.GH // NR)
        assert self.AH == NR and self.RH == 3 * NR and self.GS == 2
        self.DC = D // NR
        self.FC = DFF // NR
        self.RC = 3 * 64
        self.MIXC = self.GS * 128 + 128 + self.RC
        self.o_g = 0
        self.o_a = 512 * self.GS
        self.o_r = self.o_a + 384
        self.o_l = self.o_r + 3 * self.RC
        self.o_s = self.o_l + self.LW + self.LA + self.LG
        self.PC = self.o_s + 2 * self.GS


class Sem:
    def __init__(self, h, name):
        self.h, self.name, self.cnt = h, name, 0


class Buf:
    def __init__(self, name, t, space):
        self.name, self.t, self.space = name, t, space
        self.last_w = None
        self.readers = {}
        self.dsem = None
        self.last_w_dma = False

    def __getitem__(self, idx):
        return self.t[idx]

    def ap(self):
        return self.t.ap() if hasattr(self.t, "ap") else self.t[:]


class Prog:
    def __init__(self, nc):
        self.nc = nc
        self.root = ExitStack()
        self.stacks = [self.root]
        self.eng = {"pe": nc.tensor, "act": nc.scalar, "dve": nc.vector, "pool": nc.gpsimd, "sp": nc.sync}
        self.esem = {}
        for e in ("pe", "act", "dve", "pool"):
            self.esem[e] = Sem(self.root.enter_context(nc.semaphore("es_" + e)), e)
        self.seen = {e: {} for e in self.eng}
        self.free_dsems = []
        self.all_dsems = []
        self.scope_dsems = [[]]
        self.uid = 0
        self.pe_pending = False

    def _nm(self, name):
        self.uid += 1
        return f"{name}_{self.uid}"

    def sb(self, name, shape, dtype=F32):
        t = self.stacks[-1].enter_context(self.nc.sbuf_tensor(self._nm(name), list(shape), dtype))
        return Buf(name, t, "sb")

    def ps(self, name, shape, dtype=F32):
        t = self.stacks[-1].enter_context(self.nc.psum_tensor(self._nm(name), list(shape), dtype))
        return Buf(name, t, "ps")

    def dram(self, name, shape, dtype=F32, kind="Internal"):
        t = self.nc.dram_tensor(name, list(shape), dtype, kind=kind)
        return Buf(name, t, "dram")

    def _get_dsem(self, buf):
        if buf.dsem is None:
            if self.free_dsems:
                s = self.free_dsems.pop()
            else:
                s = Sem(self.root.enter_context(self.nc.semaphore(self._nm("ds"))), "ds")
                self.all_dsems.append(s)
            buf.dsem = s
            if buf.space != "dram":
                self.scope_dsems[-1].append(s)
        return buf.dsem

    def push(self):
        st = ExitStack()
        self.stacks.append(st)
        self.scope_dsems.append([])

    def pop(self):
        self.barrier()
        self.stacks.pop().close()
        self.free_dsems.extend(self.scope_dsems.pop())

    def barrier(self):
        toks = [(s, s.cnt) for s in self.esem.values() if s.cnt > 0]
        toks += [(s, 16 * s.cnt) for s in self.all_dsems if s.cnt > 0]
        for e in self.eng:
            self._wait(e, toks)

    def _wait(self, e, toks):
        seen = self.seen[e]
        own = self.esem.get(e)
        for s, v in toks:
            if e == "pe" and s is own:
                continue
            if seen.get(s, 0) < v:
                self.eng[e].wait_ge(s.h, v)
                seen[s] = v

    def _deps(self, reads, writes, dma_dst=None):
        toks = []
        for b in reads:
            if b.last_w is not None:
                toks.append(b.last_w)
        for b in writes:
            if b.last_w is not None and not (b is dma_dst and b.last_w_dma):
                toks.append(b.last_w)
            toks.extend(b.readers.items())
        return toks

    def _commit(self, tok, reads, writes, is_dma=False):
        for b in writes:
            b.last_w = tok
            b.last_w_dma = is_dma
            b.readers = {}
        for b in reads:
            s, v = tok
            if b.readers.get(s, 0) < v:
                b.readers[s] = v

    def op(self, e, fn, reads=(), writes=(), inc=True):
        self._wait(e, self._deps(reads, writes))
        ins = fn(self.eng[e])
        s = self.esem[e]
        if inc:
            s.cnt += 1
            ins.then_inc(s.h, 1)
            tok = (s, s.cnt)
        else:
            tok = (s, s.cnt + 1)
        self._commit(tok, reads, writes)
        return ins

    def dma(self, q, dst, dst_ap, src, src_ap):
        self._wait(q, self._deps([src], [dst], dma_dst=dst))
        s = self._get_dsem(dst)
        ins = self.eng[q].dma_start(out=dst_ap, in_=src_ap)
        s.cnt += 1
        ins.then_inc(s.h, 16)
        self._commit((s, 16 * s.cnt), [src], [dst], is_dma=True)
        return ins

    def finish(self, out_bufs):
        toks = [b.last_w for b in out_bufs if b.last_w is not None]
        self._wait("sp", toks)
        self.barrier()
        while len(self.stacks) > 1:
            self.stacks.pop().close()
        self.root.close()

    def mm(self, ps, ps_ap, a, a_ap, b, b_ap, start=True, stop=True, inc=None):
        if inc is None:
            inc = stop
        return self.op("pe", lambda e: e.matmul(ps_ap, a_ap, b_ap, start=start, stop=stop),
                       reads=[a, b], writes=[ps], inc=inc)

    def tr(self, ps, ps_ap, a, a_ap, ident, ident_ap):
        return self.op("pe", lambda e: e.transpose(ps_ap, a_ap, ident_ap), reads=[a, ident], writes=[ps])

    def act(self, out, out_ap, in_, in_ap, func, bias=None, scale=1.0, extra_reads=(), e="act", accum=None):
        kw = {}
        if bias is not None:
            kw["bias"] = bias
        if accum is not None:
            kw["accum_out"] = accum[1]
        w = [out] + ([accum[0]] if accum is not None else [])
        return self.op(e, lambda en: en.activation(out=out_ap, in_=in_ap, func=func, scale=scale, **kw),
                       reads=[in_] + list(extra_reads), writes=w)

    def ts(self, e, out, out_ap, in_, in_ap, s1, s2, op0, op1=None, extra_reads=()):
        if op1 is None:
            f = lambda en: en.tensor_scalar(out=out_ap, in0=in_ap, scalar1=s1, scalar2=None, op0=op0)
        else:
            f = lambda en: en.tensor_scalar(out=out_ap, in0=in_ap, scalar1=s1, scalar2=s2, op0=op0, op1=op1)
        return self.op(e, f, reads=[in_] + list(extra_reads), writes=[out])

    def tt(self, e, out, out_ap, a, a_ap, b, b_ap, op):
        return self.op(e, lambda en: en.tensor_tensor(out=out_ap, in0=a_ap, in1=b_ap, op=op),
                       reads=[a, b], writes=[out])

    def stt(self, e, out, out_ap, a, a_ap, scalar, b, b_ap, op0, op1, extra_reads=()):
        return self.op(e, lambda en: en.scalar_tensor_tensor(out=out_ap, in0=a_ap, scalar=scalar, in1=b_ap,
                                                              op0=op0, op1=op1),
                       reads=[a, b] + list(extra_reads), writes=[out])

    def copy(self, e, out, out_ap, in_, in_ap):
        if e == "act":
            return self.op(e, lambda en: en.copy(out=out_ap, in_=in_ap), reads=[in_], writes=[out])
        return self.op(e, lambda en: en.tensor_copy(out=out_ap, in_=in_ap), reads=[in_], writes=[out])

    def memset(self, e, out, out_ap, val):
        return self.op(e, lambda en: en.memset(out_ap, val), reads=[], writes=[out])


def xin_shape(K, S, TB):
    return [S // TB, 128, (K // 128) * TB] if K % 128 == 0 else [K, S]


def tile_x(arr, TB):
    K, S = arr.shape
    if K % 128 != 0:
        return arr
    KT = K // 128
    return np.ascontiguousarray(arr.reshape(KT, 128, S // TB, TB).transpose(2, 1, 0, 3)).reshape(S // TB, 128, KT * TB)


def ktiles(K):
    return [(k0, min(128, K - k0)) for k0 in range(0, K, 128)]


def linear(P, xT, K, S, w, Mc, m_tiles, TB, GC, epilogue, pre_group=None, n_ps=2, ps_bufs=None):
    kts = ktiles(K)
    KT = len(kts)
    groups, cur, cw = [], [], 0
    for mi, (m0, msz) in enumerate(m_tiles):
        if cw + msz > GC and cur:
            groups.append(cur)
            cur, cw = [], 0
        cur.append((mi, m0, msz))
        cw += msz
    if cur:
        groups.append(cur)
    P.push()
    wbf = P.sb("wbf", [128, KT, GC], BF16)
    KC = 4
    wst = [P.sb("wst", [128, KC, GC], F32) for _ in range(2)]
    xb = [P.sb("xblk", [128, KT, TB], BF16) for _ in range(2)]
    if ps_bufs is None:
        ps_bufs = [P.ps("linps", [128, 512], F32) for _ in range(n_ps)]
    xap = xT.ap()
    wap = w.ap()
    tiled = (K % 128 == 0)
    nblk = S // TB
    it = 0
    pi = 0
    for grp in groups:
        g0 = grp[0][1]
        gw = sum(g[2] for g in grp)
        ci = 0
        for kc0 in range(0, KT, KC):
            kc1 = min(KT, kc0 + KC)
            st = wst[ci % 2]
            for kt in range(kc0, kc1):
                k0, ksz = kts[kt]
                P.dma("sp", st, st[0:ksz, kt - kc0, 0:gw], w, wap[k0:k0 + ksz, g0:g0 + gw])
            full = [kt for kt in range(kc0, kc1) if kts[kt][1] == 128]
            part = [kt for kt in range(kc0, kc1) if kts[kt][1] != 128]
            ce = "dve" if ci % 2 == 0 else "pool"
            if full:
                a, b = full[0], full[-1] + 1
                P.copy(ce, wbf, wbf[:, a:b, 0:gw], st, st[:, a - kc0:b - kc0, 0:gw])
            for kt in part:
                ksz = kts[kt][1]
                P.copy(ce, wbf, wbf[0:ksz, kt, 0:gw], st, st[0:ksz, kt - kc0, 0:gw])
            ci += 1
        if pre_group is not None:
            pre_group(grp)
        for bi in range(nblk):
            t0 = bi * TB
            x = xb[it % 2]
            it += 1
            for kt, (k0, ksz) in enumerate(kts):
                pass
            nfull = sum(1 for k in kts if k[1] == 128)
            for ka in range(0, nfull, 8):
                kb = min(nfull, ka + 8)
                if tiled:
                    P.dma("sp", x, x[:, ka:kb, :], xT,
                          xap[bi, :, ka * TB:kb * TB].rearrange("p (kt s) -> p kt s", s=TB))
                else:
                    P.dma("sp", x, x[:, ka:kb, :], xT,
                          xap[ka * 128:kb * 128, t0:t0 + TB].rearrange("(kt p) s -> p kt s", p=128))
            if nfull < KT:
                k0, ksz = kts[-1]
                P.dma("sp", x, x[0:ksz, KT - 1, :], xT, xap[k0:k0 + ksz, t0:t0 + TB])
            for (mi, m0, msz) in grp:
                pb = ps_bufs[pi % len(ps_bufs)]
                pi += 1
                for kt, (k0, ksz) in enumerate(kts):
                    P.mm(pb, pb[0:msz, 0:TB], wbf, wbf[0:ksz, kt, m0 - g0:m0 - g0 + msz], x, x[0:ksz, kt, :],
                         start=(kt == 0), stop=(kt == KT - 1))
                epilogue(mi, (m0, msz), t0, pb)
    P.pop()


def load_rstd(P, part, NR, S, D, eps):
    rstd = P.sb("rstd", [128, S], F32)
    P.push()
    ones = P.sb("ones", [NR, 128], F32)
    P.memset("dve", ones, ones[:, :], 1.0)
    pt = P.sb("part", [NR, S], F32)
    P.dma("sp", pt, pt[:, :], part, part.ap())
    ps = [P.ps("rsps", [128, 512], F32) for _ in range(2)]
    for i, t0 in enumerate(range(0, S, 512)):
        pb = ps[i % 2]
        P.mm(pb, pb[:, :], ones, ones[:, :], pt, pt[:, t0:t0 + 512])
        P.act(rstd, rstd[:, t0:t0 + 512], pb, pb[:, :], AF.Sqrt, bias=eps_ap(P, eps), scale=1.0 / D,
              extra_reads=[P.consts["eps"]])
        P.op("dve", lambda en, t0=t0: en.reciprocal(out=rstd[:, t0:t0 + 512], in_=rstd[:, t0:t0 + 512]),
             reads=[rstd], writes=[rstd])
    P.pop()
    return rstd


def eps_ap(P, eps):
    return P.consts["eps"][:, 0:1]


def setup_consts(P, eps):
    P.consts = {}
    e = P.sb("epsc", [128, 1], F32)
    P.memset("dve", e, e[:, :], eps)
    P.consts["eps"] = e


def norm_prep_epilogue(P, cfg, xnew, xn_ap, msz, m0, t0, TB, normw, x_out, xw_out, sq_ps, first, last, part_sb):
    pass


def build_prep(cfg):
    nc = bass.Bass("TRN2", target_bir_lowering=False)
    P = Prog(nc)
    DC, S = cfg.DC, cfg.S
    x = P.dram("x", [DC, S], F32, "ExternalInput")
    nw = P.dram("nw", [128, DC // 128], F32, "ExternalInput")
    xw = P.dram("xw", [DC, S], BF16, "ExternalOutput")
    part = P.dram("part", [1, S], F32, "ExternalOutput")
    emit_norm_prep(P, cfg, x, nw, xw, part)
    P.finish([xw, part])
    return nc


def emit_norm_prep(P, cfg, x, nw, xw, part):
    DC, S = cfg.DC, cfg.S
    MT = DC // 128
    TB = 512
    P.push()
    nws = P.sb("nws", [128, MT], F32)
    P.dma("sp", nws, nws[:, :], nw, nw.ap())
    ones = P.sb("ones1", [128, 1], F32)
    P.memset("dve", ones, ones[:, :], 1.0)
    xs = [P.sb("xs", [128, MT, TB], F32) for _ in range(2)]
    sq = [P.sb("sq", [128, MT, TB], F32) for _ in range(2)]
    xo = [P.sb("xo", [128, MT, TB], BF16) for _ in range(2)]
    po = [P.sb("po", [1, TB], F32) for _ in range(2)]
    pss = [P.ps("pss", [1, TB], F32) for _ in range(2)]
    xa = x.ap().rearrange("(m p) s -> p m s", p=128)
    xwa = xw.ap().rearrange("(m p) s -> p m s", p=128)
    for i, t0 in enumerate(range(0, S, TB)):
        a, q, o, pb, pp = xs[i % 2], sq[i % 2], xo[i % 2], pss[i % 2], po[i % 2]
        P.dma("sp", a, a[:, :, :], x, xa[:, :, t0:t0 + TB])
        P.act(q, q[:, :, :], a, a[:, :, :], AF.Square)
        for m in range(MT):
            P.mm(pb, pb[:, :], ones, ones[:, :], q, q[:, m, :], start=(m == 0), stop=(m == MT - 1))
            P.ts("dve" if m % 2 == 0 else "pool", o, o[:, m, :], a, a[:, m, :], nws[:, m:m + 1], None, ALU.mult,
                 extra_reads=[nws])
        P.copy("dve", pp, pp[:, :], pb, pb[:, :])
        P.dma("pool", xw, xwa[:, :, t0:t0 + TB], o, o[:, :, :])
        P.dma("pool", part, part[0:1, t0:t0 + TB], pp, pp[:, :])
    P.pop()


def build_res(cfg, K, final=False):
    nc = bass.Bass("TRN2", target_bir_lowering=False)
    P = Prog(nc)
    DC, S = cfg.DC, cfg.S
    MT = DC // 128
    xin = P.dram("xin", xin_shape(K, S, 256 if K > 6000 else 512), BF16, "ExternalInput")
    w = P.dram("w", [K, DC], F32, "ExternalInput")
    x = P.dram("x", [DC, S], F32, "ExternalInput")
    nw = P.dram("nw", [128, DC // 128], F32, "ExternalInput")
    xo = P.dram("xo", [DC, S], F32, "ExternalOutput")
    xw = P.dram("xw", [DC, S], BF16, "ExternalOutput")
    part = P.dram("part", [1, S], F32, "ExternalOutput")
    big = K > 6000
    TB = 256 if big else 512
    GC = 256 if big else 512
    P.push()
    nws = P.sb("nws", [128, MT], F32)
    P.dma("sp", nws, nws[:, :], nw, nw.ap())
    xr = [P.sb("xr", [128, TB], F32) for _ in range(3)]
    xn = [P.sb("xn", [128, TB], F32) for _ in range(3)]
    xb = [P.sb("xb", [128, TB], BF16) for _ in range(3)]
    cnt = [0]

    def epi(mi, mt, t0, pb):
        m0, msz = mt
        i = cnt[0] % 3
        cnt[0] += 1
        a, n, b = xr[i], xn[i], xb[i]
        P.dma("act", a, a[:, :], x, x[m0:m0 + msz, t0:t0 + TB])
        P.tt("dve", n, n[:, :], pb, pb[:, 0:TB], a, a[:, :], ALU.add)
        P.dma("pool", xo, xo[m0:m0 + msz, t0:t0 + TB], n, n[:, :])
        P.ts("pool", b, b[:, :], n, n[:, :], nws[:, mi:mi + 1], None, ALU.mult, extra_reads=[nws])
        P.dma("pool", xw, xw[m0:m0 + msz, t0:t0 + TB], b, b[:, :])

    linear(P, xin, K, S, w, DC, [(m * 128, 128) for m in range(MT)], TB, GC, epi)
    P.pop()
    emit_sumsq(P, cfg, xo, part)
    P.finish([xo, xw, part])
    return nc


def emit_sumsq(P, cfg, x, part):
    DC, S = cfg.DC, cfg.S
    MT = DC // 128
    TB = 512
    P.push()
    ones = P.sb("ones1", [128, 1], F32)
    P.memset("dve", ones, ones[:, :], 1.0)
    xs = [P.sb("xs", [128, MT, TB], F32) for _ in range(2)]
    sq = [P.sb("sq", [128, MT, TB], F32) for _ in range(2)]
    po = [P.sb("po", [1, TB], F32) for _ in range(2)]
    pss = [P.ps("pss", [1, TB], F32) for _ in range(2)]
    xa = x.ap().rearrange("(m p) s -> p m s", p=128)
    for i, t0 in enumerate(range(0, S, TB)):
        a, q, pb, pp = xs[i % 2], sq[i % 2], pss[i % 2], po[i % 2]
        P.dma("sp", a, a[:, :, :], x, xa[:, :, t0:t0 + TB])
        P.act(q, q[:, :, :], a, a[:, :, :], AF.Square)
        for m in range(MT):
            P.mm(pb, pb[:, :], ones, ones[:, :], q, q[:, m, :], start=(m == 0), stop=(m == MT - 1))
        P.copy("dve", pp, pp[:, :], pb, pb[:, :])
        P.dma("pool", part, part[0:1, t0:t0 + TB], pp, pp[:, :])
    P.pop()


def build_ffn(cfg):
    nc = bass.Bass("TRN2", target_bir_lowering=False)
    P = Prog(nc)
    D, S, FC, NR = cfg.D, cfg.S, cfg.FC, cfg.NR
    xin = P.dram("xin", xin_shape(D, S, 512), BF16, "ExternalInput")
    parts = P.dram("parts", [NR, S], F32, "ExternalInput")
    wg = P.dram("wg", [D, FC], F32, "ExternalInput")
    wu = P.dram("wu", [D, FC], F32, "ExternalInput")
    cw = P.dram("cw", [FC, 4], F32, "ExternalInput")
    out = P.dram("out", [FC, S], BF16, "ExternalOutput")
    setup_consts(P, 1e-6)
    rstd = load_rstd(P, parts, NR, S, D, 1e-6)
    mts = ktiles(FC)
    gsc = P.dram("gsc", [FC, S], F32)
    TB = 512
    P.push()
    cws = P.sb("cws", [128, len(mts), 4], F32)
    for mi, (m0, msz) in enumerate(mts):
        P.dma("sp", cws, cws[0:msz, mi, :], cw, cw[m0:m0 + msz, :])
    gbuf = [P.sb("gbuf", [128, TB + 2], F32) for _ in range(2)]
    acc = [P.sb("acc", [128, TB], F32) for _ in range(2)]
    cnt = [0]
    halo = {}

    def epi_gate(mi, mt, t0, pb):
        m0, msz = mt
        i = cnt[0] % 2
        cnt[0] += 1
        g, a = gbuf[i], acc[i]
        prev = halo.get(mi)
        if t0 == 0:
            P.memset("pool", g, g[:, 0:2], 0.0)
        else:
            pg = prev
            P.copy("pool", g, g[0:msz, 0:2], pg, pg[0:msz, TB:TB + 2])
        P.tt("dve", g, g[0:msz, 2:TB + 2], pb, pb[0:msz, 0:TB], rstd, rstd[0:msz, t0:t0 + TB], ALU.mult)
        halo[mi] = g
        P.ts("dve", a, a[0:msz, :], g, g[0:msz, 0:TB], cws[0:msz, mi, 0:1], cws[0:msz, mi, 3:4], ALU.mult, ALU.add,
             extra_reads=[cws])
        P.stt("dve", a, a[0:msz, :], g, g[0:msz, 1:TB + 1], cws[0:msz, mi, 1:2], a, a[0:msz, :], ALU.mult, ALU.add,
              extra_reads=[cws])
        P.stt("dve", a, a[0:msz, :], g, g[0:msz, 2:TB + 2], cws[0:msz, mi, 2:3], a, a[0:msz, :], ALU.mult, ALU.add,
              extra_reads=[cws])
        P.act(a, a[0:msz, :], a, a[0:msz, :], AF.Silu)
        P.dma("pool", gsc, gsc[m0:m0 + msz, t0:t0 + TB], a, a[0:msz, :])

    halo_sb = P.sb("halo", [128, len(mts), 2], F32)

    def epi_gate2(mi, mt, t0, pb):
        m0, msz = mt
        i = cnt[0] % 2
        cnt[0] += 1
        g, a = gbuf[i], acc[i]
        if t0 == 0:
            P.memset("pool", g, g[:, 0:2], 0.0)
        else:
            P.copy("pool", g, g[0:msz, 0:2], halo_sb, halo_sb[0:msz, mi, :])
        P.tt("dve", g, g[0:msz, 2:TB + 2], pb, pb[0:msz, 0:TB], rstd, rstd[0:msz, t0:t0 + TB], ALU.mult)
        P.copy("pool", halo_sb, halo_sb[0:msz, mi, :], g, g[0:msz, TB:TB + 2])
        P.ts("dve", a, a[0:msz, :], g, g[0:msz, 0:TB], cws[0:msz, mi, 0:1], cws[0:msz, mi, 3:4], ALU.mult, ALU.add,
             extra_reads=[cws])
        P.stt("dve", a, a[0:msz, :], g, g[0:msz, 1:TB + 1], cws[0:msz, mi, 1:2], a, a[0:msz, :], ALU.mult, ALU.add,
              extra_reads=[cws])
        P.stt("dve", a, a[0:msz, :], g, g[0:msz, 2:TB + 2], cws[0:msz, mi, 2:3], a, a[0:msz, :], ALU.mult, ALU.add,
              extra_reads=[cws])
        P.act(a, a[0:msz, :], a, a[0:msz, :], AF.Silu)
        P.dma("pool", gsc, gsc[m0:m0 + msz, t0:t0 + TB], a, a[0:msz, :])

    linear(P, xin, D, S, wg, FC, mts, TB, 512, epi_gate2)

    gl = [P.sb("gl", [128, TB], F32) for _ in range(2)]
    ub = [P.sb("ub", [128, TB], F32) for _ in range(2)]
    ob = [P.sb("ob", [128, TB], BF16) for _ in range(2)]

    def epi_up(mi, mt, t0, pb):
        m0, msz = mt
        i = cnt[0] % 2
        cnt[0] += 1
        g, u, o = gl[i], ub[i], ob[i]
        P.dma("act", g, g[0:msz, :], gsc, gsc[m0:m0 + msz, t0:t0 + TB])
        P.tt("dve", u, u[0:msz, :], pb, pb[0:msz, 0:TB], rstd, rstd[0:msz, t0:t0 + TB], ALU.mult)
        P.tt("pool", o, o[0:msz, :], u, u[0:msz, :], g, g[0:msz, :], ALU.mult)
        P.dma("pool", out, out[m0:m0 + msz, t0:t0 + TB], o, o[0:msz, :])

    linear(P, xin, D, S, wu, FC, mts, TB, 512, epi_up)
    P.pop()
    P.finish([out])
    return nc


def build_final(cfg):
    nc = bass.Bass("TRN2", target_bir_lowering=False)
    P = Prog(nc)
    DC, S, NR, D = cfg.DC, cfg.S, cfg.NR, cfg.D
    MT = DC // 128
    x = P.dram("x", [DC, S], F32, "ExternalInput")
    nw = P.dram("nw", [128, DC // 128], F32, "ExternalInput")
    parts = P.dram("parts", [NR, S], F32, "ExternalInput")
    out = P.dram("out", [DC, S], F32, "ExternalOutput")
    setup_consts(P, 1e-6)
    rstd = load_rstd(P, parts, NR, S, D, 1e-6)
    TB = 512
    P.push()
    nws = P.sb("nws", [128, MT], F32)
    P.dma("sp", nws, nws[:, :], nw, nw.ap())
    xs = [P.sb("xs", [128, TB], F32) for _ in range(3)]
    i = 0
    for m in range(MT):
        for t0 in range(0, S, TB):
            a = xs[i % 3]
            i += 1
            P.dma("sp", a, a[:, :], x, x[m * 128:(m + 1) * 128, t0:t0 + TB])
            P.stt("dve", a, a[:, :], a, a[:, :], nws[:, m:m + 1], rstd, rstd[:, t0:t0 + TB], ALU.mult, ALU.mult,
                  extra_reads=[nws])
            P.dma("pool", out, out[m * 128:(m + 1) * 128, t0:t0 + TB], a, a[:, :])
    P.pop()
    P.finish([out])
    return nc


def launch(nc, in_maps, n):
    res = run_bass_kernel_spmd(nc, in_maps, core_ids=list(range(n)))
    return res.results


def colslice(v, c, n):
    return np.ascontiguousarray(v[c * n:(c + 1) * n])


def run_model(cfg, inp, skip_mixers=False, debug=None):
    NR, S, D, DC, FC, DFF = cfg.NR, cfg.S, cfg.D, cfg.DC, cfg.FC, cfg.DFF
    f32 = np.float32
    L = inp["w_in"].shape[0]
    xT = np.ascontiguousarray(np.asarray(inp["x"], f32)[0].T)
    xs = [colslice(xT, c, DC) for c in range(NR)]
    nws = lambda v: [np.ascontiguousarray(colslice(np.asarray(v, f32), c, DC).reshape(DC // 128, 128).T) for c in range(NR)]

    nc_prep = build_prep(cfg)
    nw = nws(inp["attn_norm"][0])
    r = launch(nc_prep, [{"x": xs[c], "nw": nw[c]} for c in range(NR)], NR)
    xw = np.concatenate([r[c]["xw"] for c in range(NR)], 0)
    parts = np.concatenate([r[c]["part"] for c in range(NR)], 0)

    nc_attn = None if skip_mixers else build_attn(cfg)
    nc_res_a = build_res(cfg, NR * cfg.MIXC)
    nc_ffn = build_ffn(cfg)
    nc_res_f = build_res(cfg, DFF)
    for l in range(L):
        if skip_mixers:
            mixT = np.zeros((NR * cfg.MIXC, S), NPBF)
        else:
            xwt = tile_x(xw, 512)
            maps = [attn_inputs(cfg, inp, l, c, xwt, parts) for c in range(NR)]
            r = launch(nc_attn, maps, NR)
            if debug is not None:
                debug.append(r)
            mixT = np.concatenate([r[c]["mix"] for c in range(NR)], 0)
        wo = wout_shards(cfg, np.asarray(inp["w_out"][l], f32))
        nw = nws(inp["ffn_norm"][l])
        mixT = tile_x(mixT, 256 if mixT.shape[0] > 6000 else 512)
        r = launch(nc_res_a, [{"xin": mixT, "w": wo[c], "x": xs[c], "nw": nw[c]} for c in range(NR)], NR)
        xs = [r[c]["xo"] for c in range(NR)]
        xw = np.concatenate([r[c]["xw"] for c in range(NR)], 0)
        parts = np.concatenate([r[c]["part"] for c in range(NR)], 0)

        wg = np.asarray(inp["w_ffn_gate"][l], f32)
        wu = np.asarray(inp["w_ffn_up"][l], f32)
        cwl = np.asarray(inp["ffn_conv"][l], f32)
        cb = np.asarray(inp["ffn_conv_b"][l], f32)
        maps = []
        xwt2 = tile_x(xw, 512)
        for c in range(NR):
            sl = slice(c * FC, (c + 1) * FC)
            cw = np.ascontiguousarray(np.concatenate([cwl[:, sl].T, cb[sl][:, None]], 1))
            maps.append({"xin": xwt2, "parts": parts, "wg": np.ascontiguousarray(wg[:, sl]),
                         "wu": np.ascontiguousarray(wu[:, sl]), "cw": cw})
        r = launch(nc_ffn, maps, NR)
        actT = np.concatenate([r[c]["out"] for c in range(NR)], 0)

        wd = np.asarray(inp["w_ffn_down"][l], f32)
        nxt = inp["attn_norm"][l + 1] if l + 1 < L else inp["final_norm"]
        nw = nws(nxt)
        actT = tile_x(actT, 256 if actT.shape[0] > 6000 else 512)
        r = launch(nc_res_f, [{"xin": actT, "w": np.ascontiguousarray(wd[:, c * DC:(c + 1) * DC]), "x": xs[c],
                               "nw": nw[c]} for c in range(NR)], NR)
        xs = [r[c]["xo"] for c in range(NR)]
        xw = np.concatenate([r[c]["xw"] for c in range(NR)], 0)
        parts = np.concatenate([r[c]["part"] for c in range(NR)], 0)

    nc_fin = build_final(cfg)
    nw = nws(inp["final_norm"])
    r = launch(nc_fin, [{"x": xs[c], "nw": nw[c], "parts": parts} for c in range(NR)], NR)
    outT = np.concatenate([r[c]["out"] for c in range(NR)], 0)
    return np.ascontiguousarray(outT.T)[None].astype(np.float32)


def gdn_head(cfg, c, j):
    h = c + cfg.NR * j
    return h if h < cfg.GH else None


def wout_shards(cfg, wo):
    NR = cfg.NR
    rows = []
    for r in range(NR):
        for j in range(cfg.GS):
            h = gdn_head(cfg, r, j)
            rows.append(wo[h * 128:(h + 1) * 128] if h is not None else np.zeros((128, cfg.D), np.float32))
        rows.append(wo[cfg.GW + r * 128: cfg.GW + (r + 1) * 128])
        b = cfg.GW + cfg.AW + r * cfg.RC
        rows.append(wo[b:b + cfg.RC])
    full = np.concatenate(rows, 0)
    return [np.ascontiguousarray(full[:, c * cfg.DC:(c + 1) * cfg.DC]) for c in range(NR)]


def kernel(**inputs):
    return run_model(Cfg(), inputs)


def t5_bucket_np(dist):
    exact = 16
    d = np.maximum(dist, 1).astype(np.float32)
    log_b = exact + (np.log(d / np.float32(exact)) / np.float32(math.log(2048 / exact)) * np.float32(32 - exact)).astype(np.int32)
    return np.where(dist < exact, dist, np.minimum(log_b, 31))


SWA_DIL = (1, 4, 16)


def swa_bias_tiles(rel_bias_col):
    out = np.zeros((3, 2, 128, 128), np.float32)
    j = np.arange(128)[:, None]
    i = np.arange(128)[None, :]
    for p, d in enumerate(SWA_DIL):
        for t in range(2):
            steps = (i + 128 - j) if t == 0 else (i - j)
            valid = (steps >= 0) & (steps <= 128)
            idx = t5_bucket_np(np.maximum(steps, 0) * d)
            out[p, t] = np.where(valid, rel_bias_col[idx], np.float32(-30000.0))
    return out


def attn_inputs(cfg, inp, l, c, xw, parts):
    f32 = np.float32
    NR = cfg.NR
    W = np.asarray(inp["w_in"][l], f32)
    D = cfg.D
    cols = []
    GW, GH = cfg.GW, cfg.GH
    zc = np.zeros((D, 128), f32)
    for j in range(cfg.GS):
        h = gdn_head(cfg, c, j)
        for blk in range(4):
            cols.append(W[:, blk * GW + h * 128: blk * GW + (h + 1) * 128] if h is not None else zc)
    b = cfg.GDN_IN
    for blk in range(3):
        cols.append(W[:, b + blk * cfg.AW + c * 128: b + blk * cfg.AW + (c + 1) * 128])
    b = cfg.GDN_IN + cfg.SWA_IN
    for blk in range(3):
        cols.append(W[:, b + blk * cfg.RW + c * cfg.RC: b + blk * cfg.RW + (c + 1) * cfg.RC])
    cols.append(W[:, b + 3 * cfg.RW: b + 3 * cfg.RW + cfg.LW + cfg.LA + cfg.LG])
    z1 = np.zeros((D, 1), f32)
    for j in range(cfg.GS):
        h = gdn_head(cfg, c, j)
        cols.append(W[:, 4 * GW + h: 4 * GW + h + 1] if h is not None else z1)
        cols.append(W[:, 4 * GW + GH + h: 4 * GW + GH + h + 1] if h is not None else z1)
    w = np.ascontiguousarray(np.concatenate(cols, 1))
    assert w.shape[1] == cfg.PC
    m = {"xin": xw, "parts": parts, "w": w}
    m["cst"] = host_consts()
    m["bm"] = swa_bias_tiles(np.asarray(inp["rel_bias"], f32)[:, c])
    m.update(gdn_inputs(cfg, inp, l, c))
    m.update(rwkv_inputs(cfg, inp, l, c))
    return m


def gdn_inputs(cfg, inp, l, c):
    f32 = np.float32
    conv = np.asarray(inp["gdn_conv"][l], f32)
    gconv = np.zeros((128, cfg.GS * 3 * 4), f32)
    gsc = np.zeros((1, cfg.GS * 2), f32)
    for j in range(cfg.GS):
        h = gdn_head(cfg, c, j)
        if h is None:
            continue
        for b in range(3):
            gconv[:, (j * 3 + b) * 4:(j * 3 + b + 1) * 4] = conv[:, b * cfg.GW + h * 128: b * cfg.GW + (h + 1) * 128].T
        gsc[0, 2 * j] = np.asarray(inp["gdn_a_log"], f32)[l, h]
        gsc[0, 2 * j + 1] = np.asarray(inp["gdn_dt_bias"], f32)[l, h]
    gnw = np.ascontiguousarray(np.asarray(inp["gdn_norm"], f32)[l][:, None])
    return {"gconv": gconv, "gsc": gsc, "gnw": gnw}


def rwkv_inputs(cfg, inp, l, c):
    f32 = np.float32
    RW, RC = cfg.RW, cfg.RC
    ch = slice(c * RC, (c + 1) * RC)
    mu = np.asarray(inp["rwkv_mu"], f32)[l]
    vec = np.zeros((64, 3, 10), f32)
    srcs = [mu[0:RW][ch], mu[RW:2 * RW][ch], mu[2 * RW:3 * RW][ch],
            np.asarray(inp["rwkv_w0"], f32)[l][ch], np.asarray(inp["rwkv_a0"], f32)[l][ch],
            np.asarray(inp["rwkv_k_k"], f32)[l][ch], np.asarray(inp["rwkv_k_a"], f32)[l][ch],
            np.asarray(inp["rwkv_r_k"], f32)[l].reshape(-1)[ch],
            np.asarray(inp["rwkv_ln_w"], f32)[l][ch], np.asarray(inp["rwkv_ln_b"], f32)[l][ch]]
    for i, v in enumerate(srcs):
        vec[:, :, i] = v.reshape(3, 64).T
    mul = np.zeros((128, 6), f32)
    ml = mu[3 * RW:]
    for t in range(6):
        seg = ml[t * 128:(t + 1) * 128]
        mul[:len(seg), t] = seg
    gup = np.zeros((512, RC), f32)
    gup[:cfg.LG] = np.asarray(inp["rwkv_g_up"], f32)[l][:, ch]
    return {"rvec": np.ascontiguousarray(vec.reshape(64, 30)), "rmul": mul,
            "rwup": np.ascontiguousarray(np.asarray(inp["rwkv_w_up"], f32)[l][:, ch]),
            "raup": np.ascontiguousarray(np.asarray(inp["rwkv_a_up"], f32)[l][:, ch]),
            "rgup": gup}


def build_attn(cfg, parts_enabled=("swa", "gdn", "rwkv")):
    nc = bass.Bass("TRN2", target_bir_lowering=False)
    P = Prog(nc)
    D, S, NR, PC = cfg.D, cfg.S, cfg.NR, cfg.PC
    xin = P.dram("xin", xin_shape(D, S, 512), BF16, "ExternalInput")
    parts = P.dram("parts", [NR, S], F32, "ExternalInput")
    w = P.dram("w", [D, PC], F32, "ExternalInput")
    bm = P.dram("bm", [3, 2, 128, 128], F32, "ExternalInput")
    P.cst = P.dram("cst", [4, 128, 128], F32, "ExternalInput")
    P.rin = {"rvec": P.dram("rvec", [64, 30], F32, "ExternalInput"),
             "rmul": P.dram("rmul", [128, 6], F32, "ExternalInput"),
             "rwup": P.dram("rwup", [128, cfg.RC], F32, "ExternalInput"),
             "raup": P.dram("raup", [128, cfg.RC], F32, "ExternalInput"),
             "rgup": P.dram("rgup", [512, cfg.RC], F32, "ExternalInput")}
    P.gin = {"gconv": P.dram("gconv", [128, cfg.GS * 12], F32, "ExternalInput"),
             "gsc": P.dram("gsc", [1, cfg.GS * 2], F32, "ExternalInput"),
             "gnw": P.dram("gnw", [128, 1], F32, "ExternalInput")}
    mix = P.dram("mix", [cfg.MIXC, S], BF16, "ExternalOutput")
    proj = P.dram("proj", [PC, S], F32)
    setup_consts(P, 1e-6)
    P.push()
    rstd = load_rstd(P, parts, NR, S, D, 1e-6)
    TB = 512
    pj = [P.sb("pj", [128, TB], F32) for _ in range(3)]
    cnt = [0]

    def epi(mi, mt, t0, pb):
        m0, msz = mt
        a = pj[cnt[0] % 3]
        cnt[0] += 1
        P.tt("dve", a, a[0:msz, :], pb, pb[0:msz, 0:TB], rstd, rstd[0:msz, t0:t0 + TB], ALU.mult)
        P.dma("pool", proj, proj[m0:m0 + msz, t0:t0 + TB], a, a[0:msz, :])

    linear(P, xin, D, S, w, PC, ktiles(PC), TB, 512, epi)
    P.pop()
    if "gdn" in parts_enabled and "rwkv" in parts_enabled and COSCHED:
        emit_gdn_rwkv(P, cfg, proj, mix)
    else:
        emit_gdn(P, cfg, proj, mix) if "gdn" in parts_enabled else emit_zero_rows(P, cfg, mix, 0, 256)
        emit_rwkv(P, cfg, proj, mix) if "rwkv" in parts_enabled else emit_zero_rows(P, cfg, mix, 384, 192)
    emit_swa(P, cfg, proj, mix, bm) if "swa" in parts_enabled else emit_zero_rows(P, cfg, mix, 256, 128)
    P.finish([mix])
    return nc


COSCHED = False


def emit_gdn_rwkv(P, cfg, proj, mix):
    S = cfg.S
    GSEG = min(512, S)
    RSEG = min(256, S)
    P.push()
    sh = mixer_shared(P, 2, 6)
    gd = emit_gdn(P, cfg, proj, mix, shared=sh, SEG=GSEG)
    rw = emit_rwkv(P, cfg, proj, mix, shared=sh, SEG=RSEG)
    nsub = GSEG // RSEG

    def rw_super(G):
        for k in range(nsub):
            g = G * nsub + k
            rw["pre"](g)
            yield
            yield from rw["pipe"](g)
            rw["post"](g)
            yield

    for G in range(S // GSEG):
        gd["pre"](G)
        run_rr([gd["pipe"](G), rw_super(G)])
        gd["post"](G)
    P.pop()


def emit_zero_rows(P, cfg, mix, r0, n):
    P.push()
    z = P.sb("z", [128, 2048], BF16)
    P.memset("dve", z, z[:, :], 0.0)
    for a in range(r0, r0 + n, 128):
        m = min(128, r0 + n - a)
        for t0 in range(0, cfg.S, 2048):
            P.dma("pool", mix, mix[a:a + m, t0:t0 + 2048], z, z[0:m, :])
    P.pop()


def mixer_shared(P, nburst, nchain):
    sh = {}
    ident = P.sb("ident", [128, 128], F32)
    make_identity(P, ident)
    Ui = P.sb("Ui", [128, 128], F32)
    P.dma("sp", Ui, Ui[:, :], P.cst, P.cst[1, :, :])
    Us = P.sb("Us", [128, 128], F32)
    P.dma("sp", Us, Us[:, :], P.cst, P.cst[2, :, :])
    sh["ident"], sh["Ui"], sh["Us"] = ident, Ui, Us
    burst = [P.ps("bps", [128, 512], F32) for _ in range(nburst)]
    pfree = [P.ps("cps", [128, 512], F32) for _ in range(nchain)]

    def pp():
        b = burst.pop(0)
        burst.append(b)
        return b

    def get():
        while not pfree:
            yield
        return pfree.pop(0)

    def rel(b):
        pfree.append(b)

    sh["pp"], sh["get"], sh["rel"] = pp, get, rel
    return sh


def run_rr_gen(gens):
    gens = list(gens)
    while gens:
        nxt = []
        for g in gens:
            try:
                next(g)
                nxt.append(g)
            except StopIteration:
                pass
        gens = nxt
        yield


def emit_gdn(P, cfg, proj, mix, shared=None, SEG=None):
    S = cfg.S
    if SEG is None:
        SEG = 1024 if S >= 1024 else S
    NSEG = S // SEG
    C = 128
    NSL = cfg.GS
    own = shared is None
    if own:
        P.push()
        shared = mixer_shared(P, 1, 7)
    ident, Ui, Us = shared["ident"], shared["Ui"], shared["Us"]
    pp, get, rel = shared["pp"], shared["get"], shared["rel"]
    ones = P.sb("ones", [128, 128], F32)
    P.memset("dve", ones, ones[:, :], 1.0)
    gconv = P.sb("gconv", [128, cfg.GS * 12], F32)
    P.dma("sp", gconv, gconv[:, :], P.gin["gconv"], P.gin["gconv"].ap())
    gsc = P.sb("gsc", [1, cfg.GS * 2], F32)
    P.dma("sp", gsc, gsc[:, :], P.gin["gsc"], P.gin["gsc"].ap())
    gnw = P.sb("gnw", [128, 1], F32)
    P.dma("sp", gnw, gnw[:, :], P.gin["gnw"], P.gin["gnw"].ap())
    negA = P.sb("negA", [1, cfg.GS], F32)
    for j in range(cfg.GS):
        P.act(negA, negA[:, j:j + 1], gsc, gsc[:, 2 * j:2 * j + 1], AF.Exp)
    P.ts("dve", negA, negA[:, :], negA, negA[:, :], -1.0, None, ALU.mult)
    one1 = P.sb("one1", [1, 1], F32)
    P.memset("dve", one1, one1[:, :], 1.0)

    T = {}

    def tb(name, shape=(128, 128), n=2):
        if name not in T:
            T[name] = ([P.sb("g_" + name, list(shape), F32) for _ in range(n)], [0])
        bufs, k = T[name]
        k[0] += 1
        return bufs[k[0] % len(bufs)]

    raw = P.sb("raw", [128, SEG + 3], F32)
    acc = P.sb("acc", [128, SEG], F32)
    SB = []
    for j in range(NSL):
        d = {}
        for nm in ("qT", "kT", "vT", "zs"):
            d[nm] = P.sb("g" + nm + str(j), [128, SEG], F32)
        d["ob"] = P.sb("gob" + str(j), [128, SEG], BF16)
        d["brow"] = P.sb("brow" + str(j), [1, SEG], F32)
        d["grow"] = P.sb("grow" + str(j), [1, SEG], F32)
        d["St"] = P.sb("gS" + str(j), [128, 128], F32)
        P.memset("dve", d["St"], d["St"][:, :], 0.0)
        SB.append(d)
    res = {}

    def chunkA(j, ci):
        sb = SB[j]
        qT, kT, vT, brow, grow = sb["qT"], sb["kT"], sb["vT"], sb["brow"], sb["grow"]
        sfx = str(j)
        csl = slice(ci * C, ci * C + C)
        pb = yield from get()
        P.mm(pb, pb[:, 0:1], grow, grow[:, csl], one1, one1[:, :])
        P.mm(pb, pb[:, 1:2], brow, brow[:, csl], one1, one1[:, :])
        yield
        cols = tb("cols" + sfx, (128, 8))
        P.copy("dve", cols, cols[:, 0:2], pb, pb[:, 0:2])
        rel(pb)
        yield
        pb = yield from get()
        P.mm(pb, pb[:, 0:1], Ui, Ui[:, :], cols, cols[:, 0:1])
        Ug = tb("Ug" + sfx)
        P.ts("dve", Ug, Ug[:, :], Ui, Ui[:, :], cols[:, 0:1], None, ALU.mult, extra_reads=[cols])
        yield
        P.copy("dve", cols, cols[:, 2:3], pb, pb[:, 0:1])
        rel(pb)
        pG = yield from get()
        P.mm(pG, pG[:, 0:128], ones, ones[:, :], Ug, Ug[:, :])
        yield
        Gsb = tb("Gsb" + sfx)
        P.copy("act", Gsb, Gsb[:, :], pG, pG[:, 0:128])
        rel(pG)
        yield
        dT = tb("dT" + sfx)
        P.ts("dve", dT, dT[:, :], Gsb, Gsb[:, :], cols[:, 2:3], 0.0, ALU.subtract, ALU.min, extra_reads=[cols])
        eG = tb("eG" + sfx)
        P.act(eG, eG[:, :], Gsb, Gsb[:, :], AF.Exp)
        yield
        gam = tb("gam" + sfx)
        P.act(gam, gam[:, :], dT, dT[:, :], AF.Exp)
        P.ts("dve", cols, cols[:, 4:5], Gsb, Gsb[:, 127:128], cols[:, 2:3], None, ALU.subtract, extra_reads=[cols])
        yield
        P.act(cols, cols[:, 3:4], cols, cols[:, 2:3], AF.Exp)
        P.act(cols, cols[:, 4:5], cols, cols[:, 4:5], AF.Exp)
        gami = tb("gami" + sfx)
        P.tt("pool", gami, gami[:, :], gam, gam[:, :], Ui, Ui[:, :], ALU.mult)
        gams = tb("gams" + sfx)
        P.tt("pool", gams, gams[:, :], gam, gam[:, :], Us, Us[:, :], ALU.mult)
        yield
        pk = yield from get()
        P.tr(pk, pk[:, 0:128], kT, kT[:, csl], ident, ident[:, :])
        yield
        kbG = tb("kbG" + sfx)
        P.ts("dve", kbG, kbG[:, :], pk, pk[:, 0:128], cols[:, 1:2], cols[:, 3:4], ALU.mult, ALU.mult, extra_reads=[cols])
        kd = tb("kd" + sfx)
        P.ts("dve", kd, kd[:, :], pk, pk[:, 0:128], cols[:, 4:5], None, ALU.mult, extra_reads=[cols])
        rel(pk)
        yield
        pv = yield from get()
        P.tr(pv, pv[:, 0:128], vT, vT[:, csl], ident, ident[:, :])
        yield
        vb = tb("vb" + sfx)
        P.ts("dve", vb, vb[:, :], pv, pv[:, 0:128], cols[:, 1:2], None, ALU.mult, extra_reads=[cols])
        rel(pv)
        yield
        pbr = yield from get()
        P.mm(pbr, pbr[:, 0:128], ones, ones[0:1, :], brow, brow[:, csl])
        yield
        kbT = tb("kbT" + sfx)
        P.tt("dve", kbT, kbT[:, :], kT, kT[:, csl], pbr, pbr[:, 0:128], ALU.mult)
        rel(pbr)
        yield
        pA = yield from get()
        P.mm(pA, pA[:, 0:128], kT, kT[:, csl], kbT, kbT[:, :])
        P.mm(pA, pA[:, 128:256], kT, kT[:, csl], qT, qT[:, csl])
        yield
        Pm = tb("Pm" + sfx, n=3)
        P.stt("dve", Pm, Pm[:, :], pA, pA[:, 0:128], -1.0, gams, gams[:, :], ALU.mult, ALU.mult)
        attnT = tb("attnT" + sfx)
        P.tt("dve", attnT, attnT[:, :], pA, pA[:, 128:256], gami, gami[:, :], ALU.mult)
        rel(pA)
        yield
        pq = yield from get()
        P.tr(pq, pq[:, 0:128], Pm, Pm[:, :], ident, ident[:, :])
        yield
        Qm = tb("Qm" + sfx, n=3)
        P.copy("act", Qm, Qm[:, :], pq, pq[:, 0:128])
        rel(pq)
        R = tb("R" + sfx, n=3)
        P.tt("pool", R, R[:, :], Pm, Pm[:, :], ident, ident[:, :], ALU.add)
        qgT = tb("qgT" + sfx)
        P.tt("pool", qgT, qgT[:, :], qT, qT[:, csl], eG, eG[:, :], ALU.mult)
        yield
        for k in range(1, 7):
            pQ = yield from get()
            P.mm(pQ, pQ[:, 0:128], Pm, Pm[:, :], Qm, Qm[:, :])
            yield
            Qn = tb("Qm" + sfx, n=3)
            P.copy("dve", Qn, Qn[:, :], pQ, pQ[:, 0:128])
            rel(pQ)
            yield
            if k < 6:
                pP = yield from get()
                P.mm(pP, pP[:, 0:128], Qm, Qm[:, :], Pm, Pm[:, :])
                yield
                Pn = tb("Pm" + sfx, n=3)
                P.copy("act", Pn, Pn[:, :], pP, pP[:, 0:128])
                rel(pP)
                yield
            pR = yield from get()
            P.mm(pR, pR[:, 0:128], Qn, Qn[:, :], R, R[:, :])
            yield
            Rn = tb("R" + sfx, n=3)
            P.tt("dve", Rn, Rn[:, :], R, R[:, :], pR, pR[:, 0:128], ALU.add)
            rel(pR)
            R, Qm = Rn, Qn
            if k < 6:
                Pm = Pn
            yield
        pu = yield from get()
        P.mm(pu, pu[:, 0:128], R, R[:, :], vb, vb[:, :])
        yield
        u = tb("u" + sfx)
        P.copy("act", u, u[:, :], pu, pu[:, 0:128])
        rel(pu)
        yield
        pw = yield from get()
        P.mm(pw, pw[:, 0:128], kbG, kbG[:, :], R, R[:, :])
        yield
        wT = tb("wT" + sfx)
        P.copy("act", wT, wT[:, :], pw, pw[:, 0:128])
        rel(pw)
        res[(j, ci)] = dict(u=u, wT=wT, qgT=qgT, attnT=attnT, kd=kd, eG=eG)
        yield

    def chunkB(j, ci):
        sb = SB[j]
        St, zs, ob = sb["St"], sb["zs"], sb["ob"]
        r = res.pop((j, ci))
        u, wT, qgT, attnT, kd, eG = r["u"], r["wT"], r["qgT"], r["attnT"], r["kd"], r["eG"]
        sfx = str(j)
        csl = slice(ci * C, ci * C + C)
        pa = yield from get()
        P.mm(pa, pa[:, 0:128], wT, wT[:, :], St, St[:, :])
        yield
        vn = tb("vn" + sfx)
        P.tt("dve", vn, vn[:, :], u, u[:, :], pa, pa[:, 0:128], ALU.subtract)
        rel(pa)
        yield
        pS = yield from get()
        P.mm(pS, pS[:, 0:128], kd, kd[:, :], vn, vn[:, :])
        po = yield from get()
        P.mm(po, po[:, 0:128], qgT, qgT[:, :], St, St[:, :], start=True, stop=False)
        P.mm(po, po[:, 0:128], attnT, attnT[:, :], vn, vn[:, :], start=False, stop=True)
        yield
        P.stt("dve", St, St[:, :], St, St[:, :], eG[:, 127:128], pS, pS[:, 0:128], ALU.mult, ALU.add, extra_reads=[eG])
        rel(pS)
        osb = tb("osb" + sfx)
        P.copy("act", osb, osb[:, :], po, po[:, 0:128])
        rel(po)
        yield
        osq = tb("osq" + sfx)
        P.act(osq, osq[:, :], osb, osb[:, :], AF.Square)
        yield
        oc = tb("oc" + sfx, (128, 2))
        P.op("dve", lambda en, oc=oc, osq=osq: en.reduce_sum(out=oc[:, 0:1], in_=osq[:, :], axis=AX.X),
             reads=[osq], writes=[oc])
        yield
        P.act(oc, oc[:, 0:1], oc, oc[:, 0:1], AF.Sqrt, bias=P.consts["eps"][:, 0:1], scale=1.0 / 128,
              extra_reads=[P.consts["eps"]])
        yield
        P.op("dve", lambda en, oc=oc: en.reciprocal(out=oc[:, 0:1], in_=oc[:, 0:1]), reads=[oc], writes=[oc])
        P.ts("dve", osb, osb[:, :], osb, osb[:, :], oc[:, 0:1], None, ALU.mult, extra_reads=[oc])
        yield
        pt = yield from get()
        P.tr(pt, pt[:, 0:128], osb, osb[:, :], ident, ident[:, :])
        yield
        P.stt("dve", ob, ob[:, csl], pt, pt[:, 0:128], gnw[:, 0:1], zs, zs[:, csl], ALU.mult, ALU.mult, extra_reads=[gnw])
        rel(pt)
        yield

    def pre(g):
        t0 = g * SEG
        for j in range(NSL):
            sb = SB[j]
            qT, kT, vT, zs, brow, grow = sb["qT"], sb["kT"], sb["vT"], sb["zs"], sb["brow"], sb["grow"]
            rq = cfg.o_g + 512 * j
            for b, dst in enumerate((qT, kT, vT)):
                row = rq + 128 * b
                if g == 0:
                    P.memset("pool", raw, raw[:, 0:3], 0.0)
                    P.dma("sp", raw, raw[:, 3:SEG + 3], proj, proj[row:row + 128, 0:SEG])
                else:
                    P.dma("sp", raw, raw[:, :], proj, proj[row:row + 128, t0 - 3:t0 + SEG])
                cb = (j * 3 + b) * 4
                P.ts("dve", acc, acc[:, :], raw, raw[:, 0:SEG], gconv[:, cb:cb + 1], None, ALU.mult, extra_reads=[gconv])
                for i in range(1, 4):
                    P.stt("dve", acc, acc[:, :], raw, raw[:, i:i + SEG], gconv[:, cb + i:cb + i + 1],
                          acc, acc[:, :], ALU.mult, ALU.add, extra_reads=[gconv])
                P.act(dst, dst[:, :], acc, acc[:, :], AF.Silu)
            P.dma("sp", acc, acc[:, :], proj, proj[rq + 384:rq + 512, t0:t0 + SEG])
            P.act(zs, zs[:, :], acc, acc[:, :], AF.Silu)
            for dst, sc in ((qT, 128 ** -0.5), (kT, 1.0)):
                P.act(acc, acc[:, :], dst, dst[:, :], AF.Square)
                for c0 in range(0, SEG, 256):
                    pb = pp()
                    P.mm(pb, pb[:, 0:256], ones, ones[:, :], acc, acc[:, c0:c0 + 256])
                    rs = tb("rs", (128, 256))
                    P.act(rs, rs[:, :], pb, pb[:, 0:256], AF.Sqrt, bias=P.consts["eps"][:, 0:1], extra_reads=[P.consts["eps"]])
                    P.op("dve", lambda en, rs=rs: en.reciprocal(out=rs[:, :], in_=rs[:, :]), reads=[rs], writes=[rs])
                    P.stt("dve", dst, dst[:, c0:c0 + 256], dst, dst[:, c0:c0 + 256], sc, rs, rs[:, :], ALU.mult, ALU.mult)
            srow = cfg.o_s + 2 * j
            P.dma("sp", brow, brow[:, :], proj, proj[srow:srow + 1, t0:t0 + SEG])
            P.act(brow, brow[:, :], brow, brow[:, :], AF.Sigmoid)
            P.dma("sp", grow, grow[:, :], proj, proj[srow + 1:srow + 2, t0:t0 + SEG])
            P.act(grow, grow[:, :], grow, grow[:, :], AF.Exp, bias=gsc[:, 2 * j + 1:2 * j + 2], extra_reads=[gsc])
            P.act(grow, grow[:, :], grow, grow[:, :], AF.Ln, bias=one1[:, 0:1], extra_reads=[one1])
            P.ts("dve", grow, grow[:, :], grow, grow[:, :], negA[:, j:j + 1], None, ALU.mult, extra_reads=[negA])

    def pipe(g):
        NCH = SEG // C
        for ci in range(NCH + 1):
            gens = []
            for j in range(NSL):
                if ci >= 1:
                    gens.append(chunkB(j, ci - 1))
            for j in range(NSL):
                if ci < NCH:
                    gens.append(chunkA(j, ci))
            yield from run_rr_gen(gens)

    def post(g):
        t0 = g * SEG
        for j in range(NSL):
            P.dma("pool", mix, mix[128 * j:128 * (j + 1), t0:t0 + SEG], SB[j]["ob"], SB[j]["ob"][:, :])

    if not own:
        return dict(pre=pre, pipe=pipe, post=post, SEG=SEG)
    for g in range(NSEG):
        pre(g)
        run_rr([pipe(g)])
        post(g)
    P.pop()


def run_rr(gens):
    gens = list(gens)
    while gens:
        nxt = []
        for g in gens:
            try:
                next(g)
                nxt.append(g)
            except StopIteration:
                pass
        gens = nxt


def emit_rwkv(P, cfg, proj, mix, shared=None, SEG=None):
    S = cfg.S
    if SEG is None:
        SEG = 512 if S >= 512 else S
    NSEG = S // SEG
    C = 128
    BW = min(256, SEG)
    NH = 3
    own = shared is None
    if own:
        P.push()
        shared = mixer_shared(P, 1, 7)
    ident, Ui, Us = shared["ident"], shared["Ui"], shared["Us"]
    pp, get, rel = shared["pp"], shared["get"], shared["rel"]
    ones = P.sb("ones", [64, 64], F32)
    P.memset("dve", ones, ones[:, :], 1.0)
    rvec = P.sb("rvec", [64, 30], F32)
    P.dma("sp", rvec, rvec[:, :], P.rin["rvec"], P.rin["rvec"].ap())
    rmul = P.sb("rmul", [128, 6], F32)
    P.dma("sp", rmul, rmul[:, :], P.rin["rmul"], P.rin["rmul"].ap())
    wup = P.sb("wup", [128, cfg.RC], F32)
    P.dma("sp", wup, wup[:, :], P.rin["rwup"], P.rin["rwup"].ap())
    aup = P.sb("aup", [128, cfg.RC], F32)
    P.dma("sp", aup, aup[:, :], P.rin["raup"], P.rin["raup"].ap())
    gup = P.sb("gup", [128, 4, cfg.RC], F32)
    P.dma("sp", gup, gup[:, :, :], P.rin["rgup"], P.rin["rgup"].ap().rearrange("(t p) c -> p t c", p=128))
    omk = P.sb("omk", [64, 3], F32)
    for h in range(NH):
        P.ts("dve", omk, omk[:, h:h + 1], rvec, rvec[:, h * 10 + 6:h * 10 + 7], -1.0, 1.0, ALU.mult, ALU.add)
    eps12 = P.sb("eps12", [64, 1], F32)
    P.memset("dve", eps12, eps12[:, :], 1e-12)
    epsgn = P.sb("epsgn", [64, 1], F32)
    P.memset("dve", epsgn, epsgn[:, :], 64e-5)

    T = {}

    def tb(name, shape=(64, 64), n=2, dt=F32):
        if name not in T:
            T[name] = ([P.sb("r_" + name, list(shape), dt) for _ in range(n)], [0])
        bufs, k = T[name]
        k[0] += 1
        return bufs[k[0] % len(bufs)]

    lraw = P.sb("lraw", [128, SEG + 1], F32)
    ldif = P.sb("ldif", [128, SEG], F32)
    twd = P.sb("twd", [128, SEG], F32)
    adp = P.sb("adp", [128, SEG], F32)
    sgd = P.sb("sgd", [128, 4, SEG], F32)
    hraw = P.sb("hraw", [64, SEG + 1], F32)
    hdif = P.sb("hdif", [64, SEG], F32)
    t1 = P.sb("t1", [64, SEG], F32)
    HB = []
    for h in range(NH):
        d = {}
        for nm in ("rS", "kS", "vS", "lwS", "aS", "gS", "kkS", "bS", "yS", "bon"):
            d[nm] = P.sb(nm + str(h), [64, SEG], F32)
        d["ob"] = P.sb("rob" + str(h), [64, SEG], BF16)
        d["H"] = P.sb("rH" + str(h), [64, 64], F32)
        P.memset("dve", d["H"], d["H"][:, :], 0.0)
        d["Hb"] = P.sb("rHb" + str(h), [64, 64], BF16)
        P.memset("dve", d["Hb"], d["Hb"][:, :], 0.0)
        HB.append(d)
    I64, Ui64, Us64 = ident[0:64, 0:64], Ui[0:64, 0:64], Us[0:64, 0:64]

    def shift_load(dst_raw, rows, row0, t0, g):
        if g == 0:
            P.memset("pool", dst_raw, dst_raw[0:rows, 0:1], 0.0)
            P.dma("sp", dst_raw, dst_raw[0:rows, 1:SEG + 1], proj, proj[row0:row0 + rows, 0:SEG])
        else:
            P.dma("sp", dst_raw, dst_raw[0:rows, :], proj, proj[row0:row0 + rows, t0 - 1:t0 + SEG])

    res = {}

    def chunkA(h, ci):
        hb = HB[h]
        rS, kS, vS, lwS, aS, bS = hb["rS"], hb["kS"], hb["vS"], hb["lwS"], hb["aS"], hb["bS"]
        sfx = str(h)
        csl = slice(ci * C, ci * C + C)
        pl = yield from get()
        P.tr(pl, pl[0:C, 0:64], lwS, lwS[:, csl], ident, I64)
        yield
        lwt = tb("lwt" + sfx, (C, 64))
        P.copy("act", lwt, lwt[:, :], pl, pl[0:C, 0:64])
        rel(pl)
        yield
        pL = yield from get()
        P.mm(pL, pL[0:64, 0:C], lwt, lwt[:, :], Ui, Ui[0:C, 0:C])
        yield
        Lsb = tb("Lsb" + sfx, (64, C))
        P.copy("act", Lsb, Lsb[:, :], pL, pL[0:64, 0:C])
        rel(pL)
        yield
        eLi = tb("eLi" + sfx, (64, C))
        P.act(eLi, eLi[:, :], Lsb, Lsb[:, :], AF.Exp)
        eLx = tb("eLx" + sfx, (64, C))
        P.tt("dve", eLx, eLx[:, :], Lsb, Lsb[:, :], lwS, lwS[:, csl], ALU.subtract)
        yield
        eLn = tb("eLn" + sfx, (64, C))
        P.act(eLn, eLn[:, :], Lsb, Lsb[:, :], AF.Exp, scale=-1.0)
        yield
        P.act(eLx, eLx[:, :], eLx, eLx[:, :], AF.Exp)
        AR = tb("AR" + sfx, (64, 2 * C))
        P.tt("pool", AR, AR[:, C:2 * C], rS, rS[:, csl], eLi, eLi[:, :], ALU.mult)
        yield
        bt = tb("bt" + sfx, (64, C))
        P.tt("dve", bt, bt[:, :], bS, bS[:, csl], eLn, eLn[:, :], ALU.mult)
        kt_ = tb("kt" + sfx, (64, C))
        P.tt("pool", kt_, kt_[:, :], kS, kS[:, csl], eLn, eLn[:, :], ALU.mult)
        yield
        P.tt("dve", AR, AR[:, 0:C], aS, aS[:, csl], eLx, eLx[:, :], ALU.mult)
        yield
        bt16 = tb("bt16" + sfx, (64, C), dt=BF16)
        P.copy("pool", bt16, bt16[:, :], bt, bt[:, :])
        kt16 = tb("kt16" + sfx, (64, C), dt=BF16)
        P.copy("act", kt16, kt16[:, :], kt_, kt_[:, :])
        AR16 = tb("AR16" + sfx, (64, 2 * C), dt=BF16)
        P.copy("pool", AR16, AR16[:, :], AR, AR[:, :])
        yield
        tok = {}
        for nm, (src, sap) in {"V": (vS, vS[:, csl]), "B": (bt, bt[:, :]), "K": (kt_, kt_[:, :]),
                               "A": (AR, AR[:, 0:C])}.items():
            pt = yield from get()
            P.tr(pt, pt[0:C, 0:64], src, sap, ident, I64)
            yield
            d = tb("tok" + nm + sfx, (C, 64), dt=BF16)
            P.copy("act" if nm in ("B", "V") else "dve", d, d[:, :], pt, pt[0:C, 0:64])
            rel(pt)
            tok[nm] = d
            yield
        pAB = yield from get()
        P.mm(pAB, pAB[0:C, 0:2 * C], bt16, bt16[:, :], AR16, AR16[:, :])
        yield
        Pm32 = tb("Pm32" + sfx, (C, C))
        P.tt("dve", Pm32, Pm32[:, :], pAB, pAB[0:C, 0:C], Us, Us[0:C, 0:C], ALU.mult)
        ArbT = tb("ArbT" + sfx, (C, C), dt=BF16)
        P.tt("dve", ArbT, ArbT[:, :], pAB, pAB[0:C, C:2 * C], Ui, Ui[0:C, 0:C], ALU.mult)
        rel(pAB)
        yield
        pAK = yield from get()
        P.mm(pAK, pAK[0:C, 0:2 * C], kt16, kt16[:, :], AR16, AR16[:, :])
        yield
        AakT = tb("AakT" + sfx, (C, C), dt=BF16)
        P.tt("dve", AakT, AakT[:, :], pAK, pAK[0:C, 0:C], Us, Us[0:C, 0:C], ALU.mult)
        ArkT = tb("ArkT" + sfx, (C, C), dt=BF16)
        P.tt("dve", ArkT, ArkT[:, :], pAK, pAK[0:C, C:2 * C], Ui, Ui[0:C, 0:C], ALU.mult)
        rel(pAK)
        yield
        pq = yield from get()
        P.tr(pq, pq[0:C, 0:C], Pm32, Pm32[:, :], ident, ident[0:C, 0:C])
        Pm = tb("Pm" + sfx, (C, C), n=3, dt=BF16)
        P.copy("pool", Pm, Pm[:, :], Pm32, Pm32[:, :])
        yield
        Qm = tb("Qm" + sfx, (C, C), n=3, dt=BF16)
        P.copy("act", Qm, Qm[:, :], pq, pq[0:C, 0:C])
        rel(pq)
        R = tb("R" + sfx, (C, C), n=3, dt=BF16)
        P.tt("pool", R, R[:, :], Pm32, Pm32[:, :], ident, ident[0:C, 0:C], ALU.add)
        yield
        pX = yield from get()
        P.mm(pX, pX[0:C, 0:64], AakT, AakT[:, :], tok["V"], tok["V"][:, :])
        yield
        X = tb("X" + sfx, (C, 64), dt=BF16)
        P.copy("act", X, X[:, :], pX, pX[0:C, 0:64])
        rel(pX)
        yield
        NK = 6 if C == 128 else 5
        for k in range(1, NK + 1):
            pQ = yield from get()
            P.mm(pQ, pQ[0:C, 0:C], Pm, Pm[:, :], Qm, Qm[:, :])
            yield
            Qn = tb("Qm" + sfx, (C, C), n=3, dt=BF16)
            P.copy("dve", Qn, Qn[:, :], pQ, pQ[0:C, 0:C])
            rel(pQ)
            yield
            if k < NK:
                pP = yield from get()
                P.mm(pP, pP[0:C, 0:C], Qm, Qm[:, :], Pm, Pm[:, :])
                yield
                Pn = tb("Pm" + sfx, (C, C), n=3, dt=BF16)
                P.copy("act", Pn, Pn[:, :], pP, pP[0:C, 0:C])
                rel(pP)
                yield
            pR = yield from get()
            P.mm(pR, pR[0:C, 0:C], Qn, Qn[:, :], R, R[:, :])
            yield
            Rn = tb("R" + sfx, (C, C), n=3, dt=BF16)
            P.tt("dve", Rn, Rn[:, :], R, R[:, :], pR, pR[0:C, 0:C], ALU.add)
            rel(pR)
            R, Qm = Rn, Qn
            if k < NK:
                Pm = Pn
            yield
        pW = yield from get()
        P.mm(pW, pW[0:64, 0:C], tok["A"], tok["A"][:, :], R, R[:, :])
        yield
        WdT = tb("WdT" + sfx, (64, C), dt=BF16)
        P.copy("act", WdT, WdT[:, :], pW, pW[0:64, 0:C])
        rel(pW)
        yield
        pU0 = yield from get()
        P.mm(pU0, pU0[0:C, 0:64], R, R[:, :], X, X[:, :])
        yield
        U0 = tb("U0" + sfx, (C, 64))
        P.copy("dve", U0, U0[:, :], pU0, pU0[0:C, 0:64])
        rel(pU0)
        res[(h, ci)] = dict(WdT=WdT, U0=U0, AR=AR16, ArbT=ArbT, ArkT=ArkT, tok=tok, eLi=eLi)
        yield

    def chunkB(h, ci):
        hb = HB[h]
        H, yS, Hb = hb["H"], hb["yS"], hb["Hb"]
        r = res.pop((h, ci))
        WdT, U0, AR, ArbT, ArkT, tok, eLi = r["WdT"], r["U0"], r["AR"], r["ArbT"], r["ArkT"], r["tok"], r["eLi"]
        sfx = str(h)
        csl = slice(ci * C, ci * C + C)
        pU = yield from get()
        P.mm(pU, pU[0:C, 0:64], WdT, WdT[:, :], Hb, Hb[:, :])
        yield
        U = tb("U" + sfx, (C, 64), dt=BF16)
        P.tt("dve", U, U[:, :], U0, U0[:, :], pU, pU[0:C, 0:64], ALU.add)
        rel(pU)
        yield
        pH = yield from get()
        P.mm(pH, pH[0:64, 0:64], tok["B"], tok["B"][:, :], U, U[:, :], start=True, stop=False)
        P.mm(pH, pH[0:64, 0:64], tok["K"], tok["K"][:, :], tok["V"], tok["V"][:, :], start=False, stop=True)
        pY = yield from get()
        P.mm(pY, pY[0:64, 0:C], Hb, Hb[:, :], AR, AR[:, C:2 * C], start=True, stop=False)
        P.mm(pY, pY[0:64, 0:C], U, U[:, :], ArbT, ArbT[:, :], start=False, stop=False)
        P.mm(pY, pY[0:64, 0:C], tok["V"], tok["V"][:, :], ArkT, ArkT[:, :], start=False, stop=True)
        yield
        P.ts("dve", H, H[:, :], H, H[:, :], eLi[:, C - 1:C], None, ALU.mult, extra_reads=[eLi])
        P.stt("dve", H, H[:, :], pH, pH[0:64, 0:64], eLi[:, C - 1:C], H, H[:, :], ALU.mult, ALU.add, extra_reads=[eLi])
        P.copy("pool", Hb, Hb[:, :], H, H[:, :])
        P.copy("act", yS, yS[:, csl], pY, pY[0:64, 0:C])
        rel(pH)
        rel(pY)
        yield

    def pre(g):
        t0 = g * SEG
        lt = [(cfg.o_l, 128, 0), (cfg.o_l + 128, 128, 1)] + \
             [(cfg.o_l + 256 + 128 * i, min(128, cfg.LG - 128 * i), 2 + i) for i in range(4)]
        for (row0, rows, ti) in lt:
            shift_load(lraw, rows, row0, t0, g)
            P.tt("dve", ldif, ldif[0:rows, :], lraw, lraw[0:rows, 0:SEG], lraw, lraw[0:rows, 1:SEG + 1], ALU.subtract)
            P.stt("dve", ldif, ldif[0:rows, :], ldif, ldif[0:rows, :], rmul[0:rows, ti:ti + 1], lraw, lraw[0:rows, 1:SEG + 1],
                  ALU.mult, ALU.add, extra_reads=[rmul])
            if ti == 0:
                P.act(twd, twd[:, :], ldif, ldif[:, :], AF.Tanh)
            elif ti == 1:
                P.copy("pool", adp, adp[:, :], ldif, ldif[:, :])
            else:
                if rows < 128:
                    P.memset("pool", sgd, sgd[:, ti - 2, :], 0.0)
                P.act(sgd, sgd[0:rows, ti - 2, :], ldif, ldif[0:rows, :], AF.Sigmoid)
        for h in range(NH):
            hb = HB[h]
            rS, kS, vS, lwS, aS, gS, kkS, bS, bon = (hb[n_] for n_ in ("rS", "kS", "vS", "lwS", "aS", "gS", "kkS", "bS", "bon"))
            vb_ = h * 10
            hc = slice(64 * h, 64 * h + 64)
            for bi_, dst in enumerate((rS, kS, vS)):
                shift_load(hraw, 64, cfg.o_r + cfg.RC * bi_ + 64 * h, t0, g)
                P.tt("dve", hdif, hdif[:, :], hraw, hraw[:, 0:SEG], hraw, hraw[:, 1:SEG + 1], ALU.subtract)
                P.stt("dve", dst, dst[:, :], hdif, hdif[:, :], rvec[:, vb_ + bi_:vb_ + bi_ + 1], hraw, hraw[:, 1:SEG + 1],
                      ALU.mult, ALU.add, extra_reads=[rvec])
            for b0 in range(0, SEG, BW):
                bs = slice(b0, b0 + BW)
                pw = pp()
                P.mm(pw, pw[0:64, 0:BW], wup, wup[:, hc], twd, twd[:, bs])
                P.act(lwS, lwS[:, bs], pw, pw[0:64, 0:BW], AF.Sigmoid, bias=rvec[:, vb_ + 3:vb_ + 4], extra_reads=[rvec])
                pa = pp()
                P.mm(pa, pa[0:64, 0:BW], aup, aup[:, hc], adp, adp[:, bs])
                P.act(aS, aS[:, bs], pa, pa[0:64, 0:BW], AF.Sigmoid, bias=rvec[:, vb_ + 4:vb_ + 5], extra_reads=[rvec])
                pg = pp()
                for kt in range(4):
                    P.mm(pg, pg[0:64, 0:BW], gup, gup[:, kt, hc], sgd, sgd[:, kt, bs], start=(kt == 0), stop=(kt == 3))
                P.copy("act", gS, gS[:, bs], pg, pg[0:64, 0:BW])
            P.ts("dve", lwS, lwS[:, :], lwS, lwS[:, :], -math.exp(-0.5), None, ALU.mult)
            P.ts("dve", kkS, kkS[:, :], kS, kS[:, :], rvec[:, vb_ + 5:vb_ + 6], None, ALU.mult, extra_reads=[rvec])
            P.act(t1, t1[:, :], kkS, kkS[:, :], AF.Square)
            for b0 in range(0, SEG, BW):
                bs = slice(b0, b0 + BW)
                pn = pp()
                P.mm(pn, pn[0:64, 0:BW], ones, ones[:, :], t1, t1[:, bs])
                rs = tb("rs", (64, BW))
                P.act(rs, rs[:, :], pn, pn[0:64, 0:BW], AF.Sqrt, bias=eps12[:, 0:1], extra_reads=[eps12])
                P.op("dve", lambda en, rs=rs: en.reciprocal(out=rs[:, :], in_=rs[:, :]), reads=[rs], writes=[rs])
                P.tt("dve", kkS, kkS[:, bs], kkS, kkS[:, bs], rs, rs[:, :], ALU.mult)
            P.ts("dve", t1, t1[:, :], aS, aS[:, :], rvec[:, vb_ + 6:vb_ + 7], omk[:, h:h + 1], ALU.mult, ALU.add,
                 extra_reads=[rvec, omk])
            P.tt("dve", kS, kS[:, :], kS, kS[:, :], t1, t1[:, :], ALU.mult)
            P.tt("pool", bS, bS[:, :], kkS, kkS[:, :], aS, aS[:, :], ALU.mult)
            P.ts("pool", aS, aS[:, :], kkS, kkS[:, :], -1.0, None, ALU.mult)
            P.stt("dve", t1, t1[:, :], rS, rS[:, :], rvec[:, vb_ + 7:vb_ + 8], kS, kS[:, :], ALU.mult, ALU.mult,
                  extra_reads=[rvec])
            for b0 in range(0, SEG, BW):
                bs = slice(b0, b0 + BW)
                pn = pp()
                P.mm(pn, pn[0:64, 0:BW], ones, ones[:, :], t1, t1[:, bs])
                P.tt("dve", bon, bon[:, bs], pn, pn[0:64, 0:BW], vS, vS[:, bs], ALU.mult)
    def pipe(g):
        NCH = SEG // C
        for ci in range(NCH + 1):
            gens = []
            for h in range(NH):
                if ci >= 1:
                    gens.append(chunkB(h, ci - 1))
            for h in range(NH):
                if ci < NCH:
                    gens.append(chunkA(h, ci))
            yield from run_rr_gen(gens)

    def post(g):
        t0 = g * SEG
        for h in range(NH):
            hb = HB[h]
            yS, gS, bon, obuf = hb["yS"], hb["gS"], hb["bon"], hb["ob"]
            vb_ = h * 10
            for b0 in range(0, SEG, BW):
                bs = slice(b0, b0 + BW)
                pm = pp()
                P.mm(pm, pm[0:64, 0:BW], ones, ones[:, :], yS, yS[:, bs])
                P.stt("dve", t1, t1[:, bs], pm, pm[0:64, 0:BW], -1.0 / 64, yS, yS[:, bs], ALU.mult, ALU.add)
                sq = tb("sq", (64, BW))
                P.act(sq, sq[:, :], t1, t1[:, bs], AF.Square)
                pv_ = pp()
                P.mm(pv_, pv_[0:64, 0:BW], ones, ones[:, :], sq, sq[:, :])
                rs = tb("rs", (64, BW))
                P.act(rs, rs[:, :], pv_, pv_[0:64, 0:BW], AF.Sqrt, bias=epsgn[:, 0:1], scale=1.0 / 64, extra_reads=[epsgn])
                P.op("dve", lambda en, rs=rs: en.reciprocal(out=rs[:, :], in_=rs[:, :]), reads=[rs], writes=[rs])
                P.tt("dve", t1, t1[:, bs], t1, t1[:, bs], rs, rs[:, :], ALU.mult)
                P.ts("dve", t1, t1[:, bs], t1, t1[:, bs], rvec[:, vb_ + 8:vb_ + 9], rvec[:, vb_ + 9:vb_ + 10], ALU.mult, ALU.add,
                     extra_reads=[rvec])
                P.tt("pool", t1, t1[:, bs], t1, t1[:, bs], bon, bon[:, bs], ALU.add)
                P.tt("pool", obuf, obuf[:, bs], t1, t1[:, bs], gS, gS[:, bs], ALU.mult)
            P.dma("pool", mix, mix[384 + 64 * h:384 + 64 * h + 64, t0:t0 + SEG], obuf, obuf[:, :])

    if not own:
        return dict(pre=pre, pipe=pipe, post=post, SEG=SEG)
    for g in range(NSEG):
        pre(g)
        run_rr([pipe(g)])
        post(g)
    P.pop()


def emit_swa(P, cfg, proj, mix, bm):
    S = cfg.S
    SEG = 2048
    NSEG = S // SEG
    scale = 128 ** -0.5
    qr, kr, vr = cfg.o_a, cfg.o_a + 128, cfg.o_a + 256
    P.push()
    ident = P.sb("ident", [128, 128], F32)
    make_identity(P, ident)
    ones_c = P.sb("ones_c", [128, 1], F32)
    P.memset("dve", ones_c, ones_c[:, :], 1.0)
    ones_r = P.sb("ones_r", [1, 128], F32)
    P.memset("dve", ones_r, ones_r[:, :], 1.0)
    ones_b = P.sb("ones_b", [128, 128], BF16)
    P.memset("dve", ones_b, ones_b[:, :], 1.0)
    bms = P.sb("bms", [128, 6, 128], F32)
    for p in range(3):
        for t in range(2):
            P.dma("sp", bms, bms[:, p * 2 + t, :], bm, bm[p, t, :, :])
    mx = P.sb("mx", [1, 2], F32)
    P.memset("dve", mx, mx[:, :], 0.0)
    P.push()
    ld = [P.sb("ld", [128, 512], F32) for _ in range(2)]
    sq = [P.sb("sq", [128, 512], F32) for _ in range(2)]
    nps = [P.ps("nps", [1, 512], F32) for _ in range(2)]
    bmx = P.sb("bmx", [1, 1], F32)
    it = 0
    for which, row in enumerate((qr, kr)):
        for t0 in range(0, S, 512):
            a, q, pb = ld[it % 2], sq[it % 2], nps[it % 2]
            it += 1
            P.dma("sp", a, a[:, :], proj, proj[row:row + 128, t0:t0 + 512])
            P.act(q, q[:, :], a, a[:, :], AF.Square)
            P.mm(pb, pb[:, :], ones_c, ones_c[:, :], q, q[:, :])
            P.op("dve", lambda en, pb=pb: en.reduce_max(out=bmx[:, :], in_=pb[:, :], axis=AX.X), reads=[pb], writes=[bmx])
            P.tt("dve", mx, mx[:, which:which + 1], mx, mx[:, which:which + 1], bmx, bmx[:, :], ALU.max)
    P.pop()
    negm1 = P.sb("negm1", [1, 1], F32)
    P.tt("dve", negm1, negm1[:, :], mx, mx[:, 0:1], mx, mx[:, 1:2], ALU.add)
    P.ts("dve", negm1, negm1[:, :], negm1, negm1[:, :], -0.5 * scale, None, ALU.mult)
    negm = P.sb("negm", [128, 1], F32)
    P.push()
    pb = P.ps("nmps", [128, 1], F32)
    P.mm(pb, pb[:, :], ones_r, ones_r[:, :], negm1, negm1[:, :])
    P.copy("dve", negm, negm[:, :], pb, pb[:, :])
    P.pop()
    qT = P.sb("qT", [128, SEG], BF16)
    kT = [P.sb("kT", [128, SEG], BF16) for _ in range(2)]
    vT = [P.sb("vT", [128, SEG], F32) for _ in range(2)]
    ldq = P.sb("ldq", [128, SEG], F32)
    NUM = P.sb("NUM", [128, SEG], F32)
    DEN = P.sb("DEN", [128, SEG], F32)
    ob = P.sb("ob", [128, SEG], BF16)
    Vt = [P.sb("Vt", [128, 128], BF16) for _ in range(3)]
    tmp = [P.sb("tmp", [128, 128], F32) for _ in range(3)]
    pT = [P.sb("pT", [128, 128], BF16) for _ in range(3)]
    trp = [P.ps("trp", [128, 128], F32) for _ in range(2)]
    sp_ = [P.ps("sps", [128, 128], F32) for _ in range(2)]
    nump = [P.ps("nump", [128, 128], F32) for _ in range(2)]
    denp = [P.ps("denp", [128, 128], F32) for _ in range(2)]
    vi = 0
    bi = 0
    pending = []

    def flush_pending():
        while pending:
            np_, dp_, V, pt, first, last, qsl = pending.pop(0)
            P.mm(np_, np_[:, :], V, V[:, :], pt, pt[:, :], start=first, stop=last)
            P.mm(dp_, dp_[:, :], ones_b, ones_b[:, :], pt, pt[:, :], start=first, stop=last)
            if last:
                P.tt("dve", NUM, NUM[:, qsl], NUM, NUM[:, qsl], np_, np_[:, :], ALU.add)
                P.tt("dve", DEN, DEN[:, qsl], DEN, DEN[:, qsl], dp_, dp_[:, :], ALU.add)

    for g in range(NSEG):
        t0 = g * SEG
        cur = g % 2
        P.dma("sp", ldq, ldq[:, :], proj, proj[qr:qr + 128, t0:t0 + SEG])
        P.act(qT, qT[:, :], ldq, ldq[:, :], AF.Copy, scale=scale)
        P.dma("sp", ldq, ldq[:, :], proj, proj[kr:kr + 128, t0:t0 + SEG])
        P.copy("pool", kT[cur], kT[cur][:, :], ldq, ldq[:, :])
        P.dma("sp", vT[cur], vT[cur][:, :], proj, proj[vr:vr + 128, t0:t0 + SEG])
        P.memset("pool", NUM, NUM[:, :], 0.0)
        P.memset("pool", DEN, DEN[:, :], 0.0)
        for p, d in enumerate(SWA_DIL):
            span = 128 * d
            for nbl in range(SEG // span):
                for r in range(d):
                    base = nbl * span + r
                    qsl = slice(base, base + 127 * d + 1, d)
                    tiles = []
                    if nbl >= 1:
                        tiles.append((0, cur, slice(base - span, base - span + 127 * d + 1, d)))
                    elif g >= 1:
                        pbase = SEG - span + r
                        tiles.append((0, 1 - cur, slice(pbase, pbase + 127 * d + 1, d)))
                    tiles.append((1, cur, qsl))
                    np_, dp_ = nump[bi % 2], denp[bi % 2]
                    bi += 1
                    for ti, (tt_, ring, ksl) in enumerate(tiles):
                        tp, sps, V, tm, pt = trp[vi % 2], sp_[vi % 2], Vt[vi % 3], tmp[vi % 3], pT[vi % 3]
                        vi += 1
                        P.tr(tp, tp[:, :], vT[ring], vT[ring][:, ksl], ident, ident[:, :])
                        P.copy("act", V, V[:, :], tp, tp[:, :])
                        P.mm(sps, sps[:, :], kT[ring], kT[ring][:, ksl], qT, qT[:, qsl])
                        P.tt("dve", tm, tm[:, :], sps, sps[:, :], bms, bms[:, p * 2 + tt_, :], ALU.add)
                        P.act(pt, pt[:, :], tm, tm[:, :], AF.Exp, bias=negm[:, 0:1], extra_reads=[negm])
                        first, last = ti == 0, ti == len(tiles) - 1
                        flush_pending()
                        pending.append((np_, dp_, V, pt, first, last, qsl))
        flush_pending()
        P.op("dve", lambda en: en.reciprocal(out=DEN[:, :], in_=DEN[:, :]), reads=[DEN], writes=[DEN])
        P.tt("dve", ob, ob[:, :], NUM, NUM[:, :], DEN, DEN[:, :], ALU.mult)
        P.dma("pool", mix, mix[256:384, t0:t0 + SEG], ob, ob[:, :])
    P.pop()


def make_identity(P, ident):
    P.dma("sp", ident, ident[:, :], P.cst, P.cst[0, :, :])


def host_consts():
    c = np.zeros((4, 128, 128), np.float32)
    j = np.arange(128)[:, None]
    i = np.arange(128)[None, :]
    c[0] = (i == j)
    c[1] = (j <= i)
    c[2] = (j < i)
    c[3] = ((i // 64) == (j // 64))
    return c
```
